# Optimizing a Trainium2 kernel written in Bass

```python
import jax, jax.numpy as jnp
from jax import lax
import numpy as np

D_MODEL = 1024
BATCH = 8
SEQ = 2048
DEPTH = 1
DEC_BATCH = 128
DEC_SEQ = 8
PAST_LEN = 16384
PAGE_SIZE = 128

D_MIX = D_MODEL
LRU_WIDTH = D_MIX // 2
LRU_HEADS = 8
LRU_HEAD_DIM = LRU_WIDTH // LRU_HEADS
LRU_C = 8.0
CONV_W = 4
POOL_WIDTH = D_MIX - LRU_WIDTH
POOL_WINDOWS = (2, 4, 8, 16)
POOL_GROUPS = len(POOL_WINDOWS)
POOL_GROUP_DIM = POOL_WIDTH // POOL_GROUPS
POOL_MAX = max(POOL_WINDOWS)
D_FF = 2816
MEM_LEN = 256
XA_HEADS = 4
XA_HEAD_DIM = D_MODEL // XA_HEADS
EPS = 1e-6

kernel_name = "hymba_rglru_pool_macaron_memxattn_step"


def rmsnorm(x, g):
    xf = x.astype(jnp.float32)
    var = jnp.mean(xf * xf, axis=-1, keepdims=True)
    return (xf * lax.rsqrt(var + EPS) * g.astype(jnp.float32)).astype(x.dtype)


def swiglu(x, w_gate, w_up, w_down):
    return (jax.nn.silu(x @ w_gate) * (x @ w_up)) @ w_down


def block_diag(u, w):
    b, t, _ = u.shape
    g, dg, _ = w.shape
    uh = u.reshape(b, t, g, dg)
    return jnp.einsum('btgi,gij->btgj', uh, w).reshape(b, t, g * dg)


def causal_conv(u, buf, w, bias):
    t = u.shape[1]
    ext = jnp.concatenate([buf, u], axis=1)
    out = bias + sum(ext[:, k:k + t] * w[k] for k in range(CONV_W))
    return out, ext[:, -(CONV_W - 1):]


def rglru(u, h0, wa, ba, wx, bx, lam):
    r = jax.nn.sigmoid(block_diag(u, wa) + ba).astype(jnp.float32)
    i = jax.nn.sigmoid(block_diag(u, wx) + bx).astype(jnp.float32)
    log_a = -LRU_C * r * jax.nn.softplus(-lam.astype(jnp.float32))
    a = jnp.exp(log_a)
    mult = jnp.sqrt(jnp.maximum(-jnp.expm1(2.0 * log_a), 0.0))
    bterm = mult * i * u.astype(jnp.float32)

    def step(h, ab):
        a_t, b_t = ab
        h = a_t * h + b_t
        return h, h

    h_last, hs = lax.scan(step, h0.astype(jnp.float32),
                          (jnp.swapaxes(a, 0, 1), jnp.swapaxes(bterm, 0, 1)))
    return jnp.swapaxes(hs, 0, 1).astype(u.dtype), h_last.astype(h0.dtype)


def pool_mixer(u, buf, pos0, w, scale):
    b, t, c = u.shape
    ext = jnp.concatenate([buf, u], axis=1)
    cs = jnp.cumsum(ext.astype(jnp.float32), axis=1)
    cs = jnp.pad(cs, ((0, 0), (1, 0), (0, 0)))
    csg = cs.reshape(b, t + POOL_MAX, POOL_GROUPS, POOL_GROUP_DIM)
    pos = pos0 + jnp.arange(t)
    pooled = []
    for g, win in enumerate(POOL_WINDOWS):
        s = csg[:, POOL_MAX:POOL_MAX + t, g] - csg[:, POOL_MAX - win:POOL_MAX - win + t, g]
        cnt = jnp.minimum(pos + 1, win).astype(jnp.float32)
        pooled.append(s / cnt[None, :, None])
    pooled = jnp.stack(pooled, axis=2).reshape(b, t, c)
    delta = (pooled - u.astype(jnp.float32)).astype(u.dtype)
    out = scale * block_diag(delta, w)
    return out, ext[:, -(POOL_MAX - 1):]


def mem_kv(mem, g, wk, wv):
    m = rmsnorm(mem, g)
    b = mem.shape[0]
    k = (m @ wk).reshape(b, MEM_LEN, XA_HEADS, XA_HEAD_DIM)
    v = (m @ wv).reshape(b, MEM_LEN, XA_HEADS, XA_HEAD_DIM)
    return k, v


def cross_attend(h, k, v, wq, wo):
    b, t, _ = h.shape
    q = (h @ wq).reshape(b, t, XA_HEADS, XA_HEAD_DIM)
    s = jnp.einsum('bthd,bmhd->bhtm', q, k).astype(jnp.float32) * (XA_HEAD_DIM ** -0.5)
    p = jax.nn.softmax(s, axis=-1).astype(v.dtype)
    o = jnp.einsum('bhtm,bmhd->bthd', p, v).reshape(b, t, D_MODEL)
    return o @ wo


def layer_forward(x, conv_buf, lru_h, pool_buf, mem_k, mem_v, pos0, p):
    x = x + 0.5 * swiglu(rmsnorm(x, p['ffn1_norm']), p['ffn1_w_gate'], p['ffn1_w_up'], p['ffn1_w_down'])
    h = rmsnorm(x, p['mix_norm'])
    proj = h @ p['w_in']
    u_lru, gate, u_pool = jnp.split(proj, [LRU_WIDTH, 2 * LRU_WIDTH], axis=-1)
    u_conv, conv_buf = causal_conv(u_lru, conv_buf, p['conv_w'], p['conv_b'])
    hs, lru_h = rglru(u_conv, lru_h, p['lru_wa'], p['lru_ba'], p['lru_wx'], p['lru_bx'], p['lru_lambda'])
    y_lru = jax.nn.gelu(gate) * hs
    y_pool, pool_buf = pool_mixer(u_pool, pool_buf, pos0, p['pool_w'], p['pool_scale'])
    x = x + jnp.concatenate([y_lru, y_pool], axis=-1) @ p['w_out']
    x = x + cross_attend(rmsnorm(x, p['xattn_norm']), mem_k, mem_v, p['xattn_wq'], p['xattn_wo'])
    x = x + 0.5 * swiglu(rmsnorm(x, p['ffn2_norm']), p['ffn2_w_gate'], p['ffn2_w_up'], p['ffn2_w_down'])
    return x, conv_buf, lru_h, pool_buf


def setup_inputs(seed: int = 0) -> dict:
    key = jax.random.key(seed)
    ks = iter(jax.random.split(key, 48))

    def nrm(shape, scale=1.0):
        return jax.random.normal(next(ks), shape, jnp.float32) * scale

    def gain(n=D_MODEL):
        return 1.0 + nrm((DEPTH, n), 0.02)

    d = D_MODEL
    a8 = jax.random.uniform(next(ks), (DEPTH, LRU_WIDTH), jnp.float32, 0.9, 0.999)
    pa = a8 ** (1.0 / LRU_C)
    lru_lambda = jnp.log(pa) - jnp.log1p(-pa)
    return {
        'x_prompt': nrm((BATCH, SEQ, d)),
        'x_sample': nrm((DEC_BATCH, DEC_SEQ, d)),
        'mem_prompt': nrm((BATCH, MEM_LEN, d)),
        'state_conv': nrm((DEPTH, DEC_BATCH, CONV_W - 1, LRU_WIDTH)),
        'state_lru': nrm((DEPTH, DEC_BATCH, LRU_WIDTH), 0.5),
        'state_pool': nrm((DEPTH, DEC_BATCH, POOL_MAX - 1, POOL_WIDTH)),
        'cache_mem_k': nrm((DEPTH, DEC_BATCH, MEM_LEN, XA_HEADS, XA_HEAD_DIM)),
        'cache_mem_v': nrm((DEPTH, DEC_BATCH, MEM_LEN, XA_HEADS, XA_HEAD_DIM)),
        'ffn1_norm': gain(),
        'ffn1_w_gate': nrm((DEPTH, d, D_FF), d ** -0.5),
        'ffn1_w_up': nrm((DEPTH, d, D_FF), d ** -0.5),
        'ffn1_w_down': nrm((DEPTH, D_FF, d), D_FF ** -0.5),
        'mix_norm': gain(),
        'w_in': nrm((DEPTH, d, 2 * LRU_WIDTH + POOL_WIDTH), d ** -0.5),
        'conv_w': nrm((DEPTH, CONV_W, LRU_WIDTH), CONV_W ** -0.5),
        'conv_b': nrm((DEPTH, LRU_WIDTH), 0.01),
        'lru_wa': nrm((DEPTH, LRU_HEADS, LRU_HEAD_DIM, LRU_HEAD_DIM), LRU_HEAD_DIM ** -0.5),
        'lru_ba': nrm((DEPTH, LRU_WIDTH), 0.01),
        'lru_wx': nrm((DEPTH, LRU_HEADS, LRU_HEAD_DIM, LRU_HEAD_DIM), LRU_HEAD_DIM ** -0.5),
        'lru_bx': nrm((DEPTH, LRU_WIDTH), 0.01),
        'lru_lambda': lru_lambda,
        'pool_w': nrm((DEPTH, POOL_GROUPS, POOL_GROUP_DIM, POOL_GROUP_DIM), POOL_GROUP_DIM ** -0.5),
        'pool_scale': gain(POOL_WIDTH),
        'w_out': nrm((DEPTH, D_MIX, d), D_MIX ** -0.5),
        'xattn_norm': gain(),
        'mem_norm': gain(),
        'xattn_wq': nrm((DEPTH, d, d), d ** -0.5),
        'xattn_wk': nrm((DEPTH, d, d), d ** -0.5),
        'xattn_wv': nrm((DEPTH, d, d), d ** -0.5),
        'xattn_wo': nrm((DEPTH, d, d), d ** -0.5),
        'ffn2_norm': gain(),
        'ffn2_w_gate': nrm((DEPTH, d, D_FF), d ** -0.5),
        'ffn2_w_up': nrm((DEPTH, d, D_FF), d ** -0.5),
        'ffn2_w_down': nrm((DEPTH, D_FF, d), D_FF ** -0.5),
        'final_norm': 1.0 + nrm((D_MODEL,), 0.02),
    }


def reference(x_prompt, x_sample, mem_prompt, state_conv, state_lru, state_pool, cache_mem_k, cache_mem_v,
              ffn1_norm, ffn1_w_gate, ffn1_w_up, ffn1_w_down,
              mix_norm, w_in, conv_w, conv_b, lru_wa, lru_ba, lru_wx, lru_bx, lru_lambda, pool_w, pool_scale, w_out,
              xattn_norm, mem_norm, xattn_wq, xattn_wk, xattn_wv, xattn_wo,
              ffn2_norm, ffn2_w_gate, ffn2_w_up, ffn2_w_down,
              final_norm):
    yp, ys = x_prompt, x_sample
    b = x_prompt.shape[0]
    zero_conv = jnp.zeros((b, CONV_W - 1, LRU_WIDTH), x_prompt.dtype)
    zero_h = jnp.zeros((b, LRU_WIDTH), state_lru.dtype)
    zero_pool = jnp.zeros((b, POOL_MAX - 1, POOL_WIDTH), x_prompt.dtype)
    p_conv, p_lru, p_pool, p_mk, p_mv = [], [], [], [], []
    s_conv, s_lru, s_pool = [], [], []
    for l in range(DEPTH):
        prm = {
            'ffn1_norm': ffn1_norm[l], 'ffn1_w_gate': ffn1_w_gate[l], 'ffn1_w_up': ffn1_w_up[l],
            'ffn1_w_down': ffn1_w_down[l], 'mix_norm': mix_norm[l], 'w_in': w_in[l],
            'conv_w': conv_w[l], 'conv_b': conv_b[l], 'lru_wa': lru_wa[l], 'lru_ba': lru_ba[l],
            'lru_wx': lru_wx[l], 'lru_bx': lru_bx[l], 'lru_lambda': lru_lambda[l],
            'pool_w': pool_w[l], 'pool_scale': pool_scale[l], 'w_out': w_out[l],
            'xattn_norm': xattn_norm[l], 'xattn_wq': xattn_wq[l], 'xattn_wo': xattn_wo[l],
            'ffn2_norm': ffn2_norm[l], 'ffn2_w_gate': ffn2_w_gate[l], 'ffn2_w_up': ffn2_w_up[l],
            'ffn2_w_down': ffn2_w_down[l],
        }
        mk, mv = mem_kv(mem_prompt, mem_norm[l], xattn_wk[l], xattn_wv[l])
        yp, pc, ph, pb = layer_forward(yp, zero_conv, zero_h, zero_pool, mk, mv, 0, prm)
        p_conv.append(pc); p_lru.append(ph); p_pool.append(pb); p_mk.append(mk); p_mv.append(mv)
        ys, sc, sh, sb = layer_forward(ys, state_conv[l], state_lru[l], state_pool[l],
                                       cache_mem_k[l], cache_mem_v[l], PAST_LEN, prm)
        s_conv.append(sc); s_lru.append(sh); s_pool.append(sb)
    y_prompt = rmsnorm(yp, final_norm)
    y_sample = rmsnorm(ys, final_norm)
    return (y_prompt, y_sample,
            jnp.stack(p_conv), jnp.stack(p_lru), jnp.stack(p_pool), jnp.stack(p_mk), jnp.stack(p_mv),
            jnp.stack(s_conv), jnp.stack(s_lru), jnp.stack(s_pool))
```

```python
import contextlib
import os
import numpy as np
import concourse.bass as bass
import concourse.mybir as mybir
from concourse.bass_utils import run_bass_kernel_spmd

F32 = mybir.dt.float32
BF16 = mybir.dt.bfloat16
AF = mybir.ActivationFunctionType
ALU = mybir.AluOpType

ENGS = ("pe", "act", "dve", "pool", "sp")
NCORES = 8
TM = 1152
NS = 6
RING_W = 2816
RA_WORDS = 21504
EPS = 1e-6


class _Op:
    __slots__ = ("eng", "emit", "reads", "writes", "dma", "deps", "idx", "sig", "dma_cnt", "n_dma", "cost", "nbytes", "alldeps")


class Prog:
    def __init__(self, nc):
        self.nc = nc
        self.ops = []
        self.last_w = {}
        self.readers = {}
        self.dma_tot = {}
        self.muted = False
        self.rb_with_ra = True
        self.defw = 512
        st = os.environ.get("KSTAGES")
        self.stages = None if not st else set(int(x) for x in st.split(","))

    def stage(self, n):
        self.muted = self.stages is not None and n != 0 and n not in self.stages

    def add(self, eng, emit, reads=(), writes=(), dma=None, n_dma=1, cost=None, nbytes=0):
        if self.muted:
            return -1
        if cost is None:
            w = self.defw
            cost = {"pe": 0.5, "act": 0.22 + w / 1400.0, "dve": 0.16 + w / 960.0, "pool": 0.3, "sp": 0.06}[eng]
            if dma is not None:
                cost = 0.06 if eng == "sp" else 0.9
        if self.rb_with_ra and "RA" in reads and "RB" not in reads:
            reads = list(reads) + ["RB"]
        pr = [k for k in reads if isinstance(k, tuple) and k[0] == "ps"]
        if pr:
            reads = [k for k in reads if k not in pr]
            writes = list(writes) + [k for k in pr if k not in writes]
        op = _Op()
        op.eng, op.emit, op.dma, op.n_dma = eng, emit, dma, n_dma
        op.reads, op.writes = tuple(reads), tuple(writes)
        op.cost, op.nbytes = cost, nbytes
        deps = set()
        for k in op.reads:
            w = self.last_w.get(k)
            if w is not None:
                deps.add(w)
        for k in op.writes:
            w = self.last_w.get(k)
            if w is not None:
                deps.add(w)
            deps.update(self.readers.get(k, ()))
        i = len(self.ops)
        op.deps = deps
        op.sig = False
        op.idx = 0
        if dma is not None:
            self.dma_tot[dma] = self.dma_tot.get(dma, 0) + 16 * n_dma
            op.dma_cnt = self.dma_tot[dma]
        else:
            op.dma_cnt = 0
        self.ops.append(op)
        for k in op.writes:
            self.last_w[k] = i
            self.readers[k] = []
        for k in op.reads:
            if k not in op.writes:
                self.readers.setdefault(k, []).append(i)
        return i

    def schedule(self):
        ops = self.ops
        n = len(ops)
        succ = [[] for _ in range(n)]
        indeg = [0] * n
        for i, op in enumerate(ops):
            op.alldeps = set(op.deps)
            indeg[i] = len(op.deps)
            for d in op.deps:
                succ[d].append(i)
        done = [0.0] * n
        ready = [0.0] * n
        avail = {e: [] for e in ENGS}
        for i, op in enumerate(ops):
            if indeg[i] == 0:
                avail[op.eng].append(i)
        free = {e: 0.0 for e in ENGS}
        order = {e: [] for e in ENGS}
        dma_pipe = 0.0
        left = n
        DELTA = 0.4
        if os.environ.get("KNOSCHED"):
            for i, op in enumerate(ops):
                order[op.eng].append(i)
            return order
        while left:
            best = None
            for e in ENGS:
                av = avail[e]
                if not av:
                    continue
                f = free[e]
                st_min = min(max(ready[i], f) for i in av)
                cand = min((i for i in av if max(ready[i], f) <= st_min + DELTA), key=lambda i: (ops[i].cost > 2.5, i))
                st = max(ready[cand], f)
                if best is None or (st, cand) < (best[0], best[1]):
                    best = (st, cand, e)
            st, i, e = best
            op = ops[i]
            avail[e].remove(i)
            order[e].append(i)
            if op.dma is not None:
                free[e] = st + op.cost
                x0 = max(st + op.cost + 1.2, dma_pipe)
                dur = op.nbytes / 330e3
                dma_pipe = x0 + dur
                done[i] = x0 + dur + 0.8
            else:
                free[e] = st + op.cost
                done[i] = st + op.cost + 0.05
            left -= 1
            for j in succ[i]:
                indeg[j] -= 1
                if done[i] > ready[j]:
                    ready[j] = done[i]
                if indeg[j] == 0:
                    avail[ops[j].eng].append(j)
        self.est_total = max(done) if done else 0.0
        return order

    def build(self):
        nc = self.nc
        ops = self.ops
        order = self.schedule()
        for op in ops:
            pruned = set()
            for d in op.deps:
                p = ops[d]
                if p.dma is None and p.eng == "pe" and op.eng == "pe" and op.dma is None:
                    continue
                pruned.add(d)
                if p.dma is None:
                    p.sig = True
            op.deps = pruned
        cnt = {e: 0 for e in ENGS}
        dtot = {}
        for e in ENGS:
            for i in order[e]:
                op = ops[i]
                if op.dma is None:
                    if op.sig:
                        cnt[e] += 1
                        op.idx = cnt[e]
                else:
                    dtot[op.dma] = dtot.get(op.dma, 0) + 16 * op.n_dma
                    op.dma_cnt = dtot[op.dma]
        with contextlib.ExitStack() as es:
            esem = {e: es.enter_context(nc.semaphore("s_" + e)) for e in ENGS}
            dsem = {k: es.enter_context(nc.semaphore("d_" + str(k))) for k in self.dma_tot}
            block = es.enter_context(nc.Block())

            def run_engine(ename):
                def body(eng):
                    waited = {}
                    for oi in order[ename]:
                        op = ops[oi]
                        need = {}
                        for d in op.deps:
                            p = ops[d]
                            if p.dma is not None:
                                key, val = ("d", p.dma), p.dma_cnt
                            else:
                                key, val = ("e", p.eng), p.idx
                            if val > need.get(key, 0):
                                need[key] = val
                        for key, val in need.items():
                            if waited.get(key, 0) >= val:
                                continue
                            waited[key] = val
                            sem = dsem[key[1]] if key[0] == "d" else esem[key[1]]
                            eng.wait_ge(sem, val)
                        ins = op.emit(eng)
                        if op.dma is not None:
                            if isinstance(ins, (list, tuple)):
                                assert len(ins) == op.n_dma
                                for x in ins:
                                    x.then_inc(dsem[op.dma], 16)
                            else:
                                assert op.n_dma == 1
                                ins.then_inc(dsem[op.dma], 16)
                        elif op.sig:
                            ins.then_inc(esem[ename], 1)
                return body

            block.tensor(run_engine("pe"))
            block.scalar(run_engine("act"))
            block.vector(run_engine("dve"))
            block.gpsimd(run_engine("pool"))
            block.sync(run_engine("sp"))


def build_nc():
    nc = bass.Bass("TRN2", target_bir_lowering=False)

    def din(name, shape):
        return nc.dram_tensor(name, shape, F32, kind="ExternalInput").ap()

    def dout(name, shape):
        return nc.dram_tensor(name, shape, F32, kind="ExternalOutput").ap()

    xp = din("xp", [2048, 1024]); xsm = din("xsm", [128, 1024]); mem = din("mem", [256, 1024])
    sconv = din("sconv", [48, 512]); slru = din("slru", [16, 512]); spool = din("spool", [240, 512])
    ck = din("ck", [16, 256, 1024]); cvv = din("cv", [16, 256, 1024])
    W = {}
    for nm, shp in [("f1g", [1024, 2816]), ("f1u", [1024, 2816]), ("f1d", [2816, 1024]),
                    ("win", [1024, 1536]), ("wout", [1024, 1024]), ("wq", [1024, 1024]),
                    ("wk", [1024, 1024]), ("wv", [1024, 1024]), ("wo", [1024, 1024]),
                    ("f2g", [1024, 2816]), ("f2u", [1024, 2816]), ("f2d", [2816, 1024])]:
        W[nm] = din(nm, shp)
    gains_d = din("gains", [128, 6, 8])
    vec_d = din("vec512", [128, 5, 4])
    convw_d = din("convw", [128, 4, 4])
    lwa_d = din("lwa", [8, 64, 64]); lwx_d = din("lwx", [8, 64, 64]); poolw_d = din("poolw", [4, 128, 128])

    yp = dout("yp", [2048, 1024]); ys = dout("ys", [128, 1024])
    pconv_o = dout("pconv", [3, 512]); plru_o = dout("plru", [1, 512]); ppool_o = dout("ppool", [15, 512])
    pmk_o = dout("pmk", [256, 1024]); pmv_o = dout("pmv", [256, 1024])
    sconv_o = dout("sconv_o", [48, 512]); slru_o = dout("slru_o", [16, 512]); spool_o = dout("spool_o", [240, 512])

    with contextlib.ExitStack() as es:
        def sb(name, shape, dtype=F32):
            return es.enter_context(nc.sbuf_tensor(name, shape, dtype))

        xT = sb("xT", [128, 8, TM])
        hT = sb("hT", [128, 8, TM], BF16)
        regA = sb("regA", [128, RA_WORDS])
        ring = [sb(f"ring{i}", [128, RING_W], BF16) for i in range(NS)]
        kT = sb("kT", [128, 8, 256], BF16)
        Vb = sb("Vb", [128, 2, 1024], BF16)
        stg = [sb(f"stg{i}", [128, 1024]) for i in range(2)]
        rstd = [sb(f"rstd{i}", [128, 512]) for i in range(2)]
        sgt = [sb(f"sgt{i}", [128, 512]) for i in range(2)]
        ident_f = sb("ident_f", [128, 128])
        ident_b = sb("ident_b", [128, 128], BF16)
        ones_b = sb("ones_b", [128, 128], BF16)
        onesm_b = sb("onesm_b", [128, 128], BF16)
        gains = sb("gains_s", [128, 6, 8])
        vec = sb("vec_s", [128, 5, 4])
        convw = sb("convw_s", [128, 4, 4])
        nlam = sb("nlam", [128, 4])
        tl = [sb(f"tl{i}", [128, 4]) for i in range(4)]
        wabd = sb("wabd", [128, 4, 128], BF16)
        wxbd = sb("wxbd", [128, 4, 128], BF16)
        poolw = sb("poolw_s", [128, 4, 128], BF16)
        invc = sb("invc", [128, 4, 16])
        hstate = sb("hstate", [128, 4])
        cL = sb("cL", [128, 4, 3])
        cP = sb("cP", [128, 4, 15])
        sc_hist = sb("sc_hist", [128, 4, 48])
        sp_hist = sb("sp_hist", [128, 4, 240])
        sl_hist = sb("sl_hist", [128, 4, 16])
        bar = sb("bar", [128, 2])
        epsc = sb("epsc", [128, 1])
        onec = sb("onec", [128, 1])
        hvec = sb("hvec", [128, 3, 4])
        tmpT = sb("tmpT", [128, 4, 128])
        psb = [es.enter_context(nc.psum_tensor(f"psb{i}", [128, 512], F32)) for i in range(8)]

        P = Prog(nc)
        cnt = {"ring": 0, "mm": 0, "out": 0, "stg": 0, "rstd": 0, "sgt": 0, "odma": 0}
        out_keys = []

        def bank(cls):
            if cls == "mm":
                b = cnt["mm"] % 4
                cnt["mm"] += 1
                return b
            b = 4 + cnt["out"] % 2
            cnt["out"] += 1
            return b

        def PS(b):
            return ("ps", b)

        def ra_f32(off, n):
            return regA[:, off:off + n]

        def ra_bf16(off, n_bf):
            return regA[:, off:off + n_bf // 2].bitcast(BF16)

        last_phase = [None]

        def ra_barrier(rb=True, phase=None, write_rb=True):
            P.rb_with_ra = rb
            if phase is not None and phase == last_phase[0]:
                return
            last_phase[0] = phase
            if rb and not write_rb:
                P.add("dve", lambda e: e.memset(bar[:, 1:2], 0.0), writes=["RB", "bar1"], cost=0.1)
            P.add("dve", lambda e: e.memset(bar[:, 0:1], 0.0), writes=["RA", "bar"] + (["RB"] if (rb and write_rb) else []), cost=0.1)

        def rb_barrier():
            P.add("dve", lambda e: e.memset(bar[:, 1:2], 0.0), writes=["RB", "bar1"])

        def rb_slot(k):
            return ra_f32(12672 + 1024 * k, 1024)

        def ringload(src, k, c):
            s = cnt["ring"] % NS
            cnt["ring"] += 1
            dst = ring[s][:, 0:k * c].rearrange("p (k c) -> p k c", k=k)
            P.add("pool", lambda e, dst=dst, src=src: e.dma_start(out=dst, in_=src),
                  writes=[("ring", s)], dma=f"ring{s}", nbytes=128 * k * c * 4)
            return dst, ("ring", s)

        def wslab(name, j, c=256):
            v = W[name].rearrange("(kc p) c -> p kc c", p=128)
            kc = W[name].shape[0] // 128
            return ringload(v[:, :, j * c:(j + 1) * c], kc, c)

        def mmgroup(out_ap, pairs, reads, bnk):
            def emit(e, out_ap=out_ap, pairs=pairs):
                last = None
                n = len(pairs)
                for i, (l, r) in enumerate(pairs):
                    last = e.matmul(out_ap, l, r, start=(i == 0), stop=(i == n - 1))
                return last
            cst = sum(max(r.shape[-1], 64) / 2400.0 + 0.012 for (_, r) in pairs)
            P.add("pe", emit, reads=reads, writes=[PS(bnk)], cost=cst)

        P.add("pool", lambda e: e.memset(ident_f[:], 0.0), writes=["ident_f"])
        P.add("pool", lambda e: e.affine_select(ident_f[:], ident_f[:], [[-1, 128]], ALU.not_equal, 1.0,
                                                 base=0, channel_multiplier=1),
              reads=["ident_f"], writes=["ident_f"])
        P.add("dve", lambda e: e.tensor_copy(ident_b[:], ident_f[:]), reads=["ident_f"], writes=["ident_b"])
        P.add("dve", lambda e: e.memset(ones_b[:], 1.0), writes=["ones_b"])
        P.add("dve", lambda e: e.memset(epsc[:], EPS), writes=["epsc"])
        P.add("dve", lambda e: e.memset(onec[:], 1.0), writes=["onec"])
        P.add("dve", lambda e: e.memset(onesm_b[:], 1.0 / 1024.0), writes=["onesm_b"])
        P.add("dve", lambda e: e.memset(hstate[:], 0.0), writes=[("hstate", c) for c in range(4)])
        P.add("dve", lambda e: e.memset(cL[:], 0.0), writes=[("cL", c) for c in range(4)])
        P.add("dve", lambda e: e.memset(cP[:], 0.0), writes=[("cP", c) for c in range(4)])
        P.add("sp", lambda e: e.dma_start(out=gains[:], in_=gains_d), writes=["gains"], dma="c0")
        P.add("sp", lambda e: e.dma_start(out=vec[:], in_=vec_d), writes=["vec"], dma="c1")
        P.add("sp", lambda e: e.dma_start(out=convw[:], in_=convw_d), writes=["convw"], dma="c2")
        P.add("pool", lambda e: e.memset(wabd[:], 0.0), writes=["wabd"])
        P.add("pool", lambda e: e.memset(wxbd[:], 0.0), writes=["wxbd"])

        def bd_load(dst, src, key):
            def emit(e):
                r = []
                for hh in range(8):
                    c, j = hh // 2, hh % 2
                    r.append(e.dma_start(out=dst[64 * j:64 * j + 64, c, 64 * j:64 * j + 64], in_=src[hh]))
                return r
            P.add("pool", emit, reads=[key], writes=[key], dma="bd_" + key, n_dma=8)
        bd_load(wabd, lwa_d, "wabd")
        bd_load(wxbd, lwx_d, "wxbd")
        P.add("pool", lambda e: e.dma_start(out=poolw[:], in_=poolw_d.rearrange("g i j -> i g j")),
              writes=["poolw"], dma="c3")
        for g, win in enumerate((2, 4, 8, 16)):
            P.add("pool", lambda e, g=g, win=win: e.memset(invc[:, g, :], 1.0 / win), reads=["invc"], writes=["invc"])
            for t in range(win - 1):
                P.add("pool", lambda e, g=g, t=t: e.memset(invc[:, g, t:t + 1], 1.0 / (t + 1)),
                      reads=["invc"], writes=["invc"])
        lam = vec[:, 3, :]
        P.add("act", lambda e: e.activation(tl[0][:], lam, AF.Exp, scale=-1.0), reads=["vec"], writes=["tl0"])
        P.add("dve", lambda e: e.tensor_scalar(tl[1][:], tl[0][:], 2.0, None, ALU.add), reads=["tl0"], writes=["tl1"])
        P.add("dve", lambda e: e.reciprocal(tl[1][:], tl[1][:]), reads=["tl1"], writes=["tl1"])
        P.add("dve", lambda e: e.tensor_tensor(tl[1][:], tl[0][:], tl[1][:], ALU.mult), reads=["tl0", "tl1"], writes=["tl1"])
        P.add("dve", lambda e: e.tensor_tensor(tl[2][:], tl[1][:], tl[1][:], ALU.mult), reads=["tl1"], writes=["tl2"])
        P.add("dve", lambda e: e.tensor_scalar(tl[3][:], tl[2][:], 1.0 / 11.0, 1.0 / 9.0, ALU.mult, ALU.add), reads=["tl2"], writes=["tl3"])
        for coef in (1.0 / 7.0, 1.0 / 5.0, 1.0 / 3.0, 1.0):
            P.add("dve", lambda e: e.tensor_tensor(tl[3][:], tl[3][:], tl[2][:], ALU.mult), reads=["tl3", "tl2"], writes=["tl3"])
            P.add("dve", lambda e, coef=coef: e.tensor_scalar(tl[3][:], tl[3][:], coef, None, ALU.add), reads=["tl3"], writes=["tl3"])
        P.add("dve", lambda e: e.tensor_tensor(tl[3][:], tl[3][:], tl[1][:], ALU.mult), reads=["tl3", "tl1"], writes=["tl3"])
        P.add("dve", lambda e: e.tensor_scalar(nlam[:], tl[3][:], -16.0, None, ALU.mult), reads=["tl3"], writes=["nlam"])
        P.add("dve", lambda e: e.tensor_scalar(hvec[:, 0:2, :], vec[:, 1:3, :], 0.5, None, ALU.mult), reads=["vec"], writes=["hvec"])
        P.add("dve", lambda e: e.tensor_scalar(hvec[:, 2, :], nlam[:], 0.5, None, ALU.mult), reads=["nlam", "hvec"], writes=["hvec"])

        def load_rows_T(dram_rows, R, dst_fn, dst_keys):
            k = cnt["stg"] % 2
            cnt["stg"] += 1
            P.add("sp", lambda e: e.dma_start(out=stg[k][0:R, 0:512], in_=dram_rows), writes=[("stg", k)], dma=f"stg{k}")
            b = bank("mm")

            def emit(e):
                last = None
                for c in range(4):
                    last = e.transpose(psb[b][:, c * 128:c * 128 + R], stg[k][0:R, c * 128:(c + 1) * 128], ident_f[0:R, 0:R])
                return last
            P.add("pe", emit, reads=[("stg", k), "ident_f"], writes=[PS(b)])
            for c in range(4):
                P.add("act", lambda e, c=c: e.copy(dst_fn(c), psb[b][:, c * 128:c * 128 + R]),
                      reads=[PS(b)], writes=[dst_keys[c]])

        def store_rows_T(src_fn, src_keys, R, dram_writer, ndma=1, s3=None):
            k = cnt["stg"] % 2
            cnt["stg"] += 1
            b = bank("mm")
            if s3 is not None:
                srcs = src_fn
                for c in range(4):
                    P.add("act", lambda e, c=c: e.copy(tmpT[:, c, 0:R].rearrange("p (s r) -> p s r", s=s3), srcs(c)),
                          reads=list(src_keys), writes=[("tmpT", c)])
                src_fn = lambda c: tmpT[:, c, 0:R]
                src_keys = [("tmpT", c) for c in range(4)]

            def emit(e):
                last = None
                for c in range(4):
                    last = e.transpose(psb[b][0:R, c * 128:(c + 1) * 128], src_fn(c), ident_f[:])
                return last
            P.add("pe", emit, reads=list(src_keys) + ["ident_f"], writes=[PS(b)])
            P.add("act", lambda e: e.copy(stg[k][0:R, 0:512], psb[b][0:R, :]), reads=[PS(b)], writes=[("stg", k)])
            ok = ("o", cnt["odma"])
            cnt["odma"] += 1
            out_keys.append(ok)
            P.add("sp", lambda e: dram_writer(e, stg[k]), reads=[("stg", k)], writes=[ok], dma=f"stg{k}", n_dma=ndma)

        P.stage(1)
        load_rows_T(sconv, 48, lambda c: sc_hist[:, c, :], [("sc_hist", c) for c in range(4)])
        load_rows_T(slru, 16, lambda c: sl_hist[:, c, :], [("sl_hist", c) for c in range(4)])
        load_rows_T(spool[0:128, :], 128, lambda c: sp_hist[:, c, 0:128], [("sp_hist", c) for c in range(4)])
        load_rows_T(spool[128:240, :], 112, lambda c: sp_hist[:, c, 128:240], [("sp_hist", c) for c in range(4)])
        ok = ("o", cnt["odma"]); cnt["odma"] += 1; out_keys.append(ok)
        P.add("sp", lambda e: e.dma_start(out=spool_o.rearrange("(s r) c -> s r c", r=15)[:, 0:7, :],
                                          in_=spool.rearrange("(s r) c -> s r c", r=15)[:, 8:15, :]),
              writes=[ok], dma="hbm2hbm")

        def norm(gi, cols, Wd, xkeys, hkeys, inplace=False, src=None, dst=None):
            src = xT if src is None else src
            dst = hT if dst is None else dst
            P.defw = Wd
            c0 = cols
            for c in range(8):
                P.add("act", lambda e, c=c: e.activation(dst[:, c, c0:c0 + Wd] if not inplace else hT[:, c, c0:c0 + Wd],
                                                         src[:, c, c0:c0 + Wd], AF.Square),
                      reads=[xkeys[c]], writes=[hkeys[c]])
            sqv = hT if inplace else dst
            b = 6
            mmgroup(psb[b][:, 0:Wd], [(onesm_b[:], sqv[:, c, c0:c0 + Wd]) for c in range(8)],
                    reads=list(hkeys) + ["onesm_b"], bnk=b)
            r = cnt["rstd"] % 2
            cnt["rstd"] += 1
            P.add("act", lambda e: e.activation(rstd[r][:, 0:Wd], psb[b][:, 0:Wd], AF.Ln, bias=epsc[:, 0:1]),
                  reads=[PS(b), "epsc"], writes=[("rstd", r)], cost=1.5 + Wd / 1400.0)
            P.add("act", lambda e: e.activation(rstd[r][:, 0:Wd], rstd[r][:, 0:Wd], AF.Exp, scale=-0.5),
                  reads=[("rstd", r)], writes=[("rstd", r)], cost=1.5 + Wd / 1400.0)
            for c in range(8):
                if inplace:
                    P.add("dve", lambda e, c=c: e.scalar_tensor_tensor(src[:, c, c0:c0 + Wd], src[:, c, c0:c0 + Wd],
                                                                       gains[:, gi, c:c + 1], rstd[r][:, 0:Wd], ALU.mult, ALU.mult),
                          reads=[xkeys[c], ("rstd", r), "gains"], writes=[xkeys[c]])
                else:
                    P.add("dve", lambda e, c=c: e.scalar_tensor_tensor(dst[:, c, c0:c0 + Wd], src[:, c, c0:c0 + Wd],
                                                                       gains[:, gi, c:c + 1], rstd[r][:, 0:Wd], ALU.mult, ALU.mult),
                          reads=[xkeys[c], ("rstd", r), "gains", hkeys[c]], writes=[hkeys[c]])

        def xk(sbi):
            return [("x", c, sbi) for c in range(8)]

        def hk(sbi):
            return [("h", c, sbi) for c in range(8)]

        P.stage(0)
        def mem_phase():
            memT = ra_f32(0, 2048).rearrange("p (c m) -> p c m", c=8)
            mT = ra_bf16(2048, 2048).rearrange("p (c m) -> p c m", c=8)
            kst = ra_f32(3072, 2048).rearrange("p (a c) -> p a c", a=2)
            vst = ra_f32(5120, 2048).rearrange("p (a c) -> p a c", a=2)
            mstage = ra_f32(7168, 2048).rearrange("p (a c) -> p a c", a=2)
            P.add("sp", lambda e: e.dma_start(out=mstage, in_=mem.rearrange("(a p) c -> p a c", p=128)),
                  reads=["RA"], writes=["mstage"], dma="mstage")
            for mc in range(2):
                for half in range(2):
                    b = bank("mm")

                    def emit(e, mc=mc, half=half, b=b):
                        last = None
                        for cc in range(4):
                            c = half * 4 + cc
                            last = e.transpose(psb[b][:, cc * 128:(cc + 1) * 128], mstage[:, mc, c * 128:(c + 1) * 128], ident_f[:])
                        return last
                    P.add("pe", emit, reads=["mstage", "ident_f", "RA"], writes=[PS(b)])
                    P.add("act", lambda e, mc=mc, half=half, b=b: e.copy(
                        memT[:, half * 4:half * 4 + 4, mc * 128:(mc + 1) * 128],
                        psb[b][:, :].rearrange("p (c m) -> p c m", c=4)),
                        reads=[PS(b), "RA"], writes=[("memTc", half * 4 + cc) for cc in range(4)])
            mkeys_in = [("memTc", c) for c in range(8)]
            mTk = [("mT", c) for c in range(8)]
            def mem_norm():
                for c in range(8):
                    P.add("act", lambda e, c=c: e.activation(mT[:, c, :], memT[:, c, :], AF.Square),
                          reads=[mkeys_in[c], "RA"], writes=[mTk[c]])
                b = 6
                mmgroup(psb[b][:, 0:256], [(onesm_b[:], mT[:, c, :]) for c in range(8)], reads=mTk + ["onesm_b", "RA"], bnk=b)
                P.add("act", lambda e: e.activation(rstd[0][:, 0:256], psb[b][:, 0:256], AF.Ln, bias=epsc[:, 0:1]),
                      reads=[PS(b), "epsc"], writes=[("rstd", 0)], cost=1.7)
                P.add("act", lambda e: e.activation(rstd[0][:, 0:256], rstd[0][:, 0:256], AF.Exp, scale=-0.5),
                      reads=[("rstd", 0)], writes=[("rstd", 0)], cost=1.7)
                for c in range(8):
                    P.add("dve", lambda e, c=c: e.scalar_tensor_tensor(mT[:, c, :], memT[:, c, :], gains[:, 3, c:c + 1],
                                                                       rstd[0][:, 0:256], ALU.mult, ALU.mult),
                          reads=[mkeys_in[c], ("rstd", 0), "gains", mTk[c], "RA"], writes=[mTk[c]])
            mem_norm()
            for j in range(4):
                Ks, kkey = wslab("wk", j)
                for mc in range(2):
                    b = bank("mm")
                    mmgroup(psb[b][:, 0:256], [(mT[:, kc, mc * 128:(mc + 1) * 128], Ks[:, kc, :]) for kc in range(8)],
                            reads=mTk + [kkey, "RA"], bnk=b)
                    P.add("act", lambda e, b=b, mc=mc, j=j: e.copy(kst[:, mc, j * 256:(j + 1) * 256], psb[b][:, 0:256]),
                          reads=[PS(b), "RA"], writes=[("kst", mc, j)])
                for dc in range(2):
                    b = bank("mm")
                    mmgroup(psb[b][:, 0:256], [(Ks[:, kc, dc * 128:(dc + 1) * 128], mT[:, kc, :]) for kc in range(8)],
                            reads=mTk + [kkey, "RA"], bnk=b)
                    P.add("dve", lambda e, b=b, dc=dc, j=j: e.tensor_copy(kT[:, 2 * j + dc, :], psb[b][:, 0:256]),
                          reads=[PS(b)], writes=[("kT", 2 * j + dc)])
            for j in range(4):
                Vs, vkey = wslab("wv", j)
                for mc in range(2):
                    b = bank("mm")
                    mmgroup(psb[b][:, 0:256], [(mT[:, kc, mc * 128:(mc + 1) * 128], Vs[:, kc, :]) for kc in range(8)],
                            reads=mTk + [vkey, "RA"], bnk=b)
                    P.add("act", lambda e, b=b, mc=mc, j=j: e.copy(vst[:, mc, j * 256:(j + 1) * 256], psb[b][:, 0:256]),
                          reads=[PS(b), "RA"], writes=[("vst", mc, j)])
                    P.add("dve", lambda e, b=b, mc=mc, j=j: e.tensor_copy(Vb[:, mc, j * 256:(j + 1) * 256], psb[b][:, 0:256]),
                          reads=[PS(b)], writes=[("Vb", mc, j)])
            for nm, st_, dst_ in (("kst", kst, pmk_o), ("vst", vst, pmv_o)):
                ok = ("o", cnt["odma"]); cnt["odma"] += 1; out_keys.append(ok)
                P.add("sp", lambda e, st_=st_, dst_=dst_: e.dma_start(out=dst_.rearrange("(a p) c -> p a c", p=128), in_=st_),
                      reads=[(nm, mc, j) for mc in range(2) for j in range(4)] + ["RA"], writes=[ok], dma="o_" + nm)
        kTkeys = [("kT", i) for i in range(8)]
        Vbkeys = [("Vb", mc, j) for mc in range(2) for j in range(4)]

        P.stage(0)
        a_bf = ra_bf16(0, 22 * TM).rearrange("p (f t) -> p f t", f=22)

        def ffn(pfx, sbs, sb_outer=False):
            ra_barrier(rb=False)
            for j in range(11):
                G, gk = wslab(pfx + "g", j)
                U, uk = wslab(pfx + "u", j)
                for (sbi, c0, Wd) in sbs:
                    P.defw = Wd
                    for half in range(2):
                        f = 2 * j + half
                        bg = bank("mm"); bu = bank("mm")
                        mmgroup(psb[bg][:, 0:Wd], [(G[:, k, half * 128:(half + 1) * 128], hT[:, k, c0:c0 + Wd]) for k in range(8)],
                                reads=[gk] + hk(sbi), bnk=bg)
                        mmgroup(psb[bu][:, 0:Wd], [(U[:, k, half * 128:(half + 1) * 128], hT[:, k, c0:c0 + Wd]) for k in range(8)],
                                reads=[uk] + hk(sbi), bnk=bu)
                        t = cnt["sgt"] % 2
                        cnt["sgt"] += 1
                        P.add("act", lambda e, t=t, bg=bg, Wd=Wd: e.activation(sgt[t][:, 0:Wd], psb[bg][:, 0:Wd], AF.Silu),
                              reads=[PS(bg)], writes=[("sgt", t)])
                        P.add("dve", lambda e, t=t, bu=bu, Wd=Wd, f=f, c0=c0: e.tensor_tensor(
                            a_bf[:, f, c0:c0 + Wd], sgt[t][:, 0:Wd], psb[bu][:, 0:Wd], ALU.mult),
                            reads=[("sgt", t), PS(bu), "RA"], writes=[("a", f, sbi)])
            dloop = [(d, [sb_]) for sb_ in sbs for d in range(8)] if sb_outer else [(d, sbs) for d in range(8)]
            for (d, sbl) in dloop:
                v = W[pfx + "d"].rearrange("(fc p) c -> p fc c", p=128)
                Dd, dk = ringload(v[:, :, d * 128:(d + 1) * 128], 22, 128)
                for (sbi, c0, Wd) in sbl:
                    P.defw = Wd
                    bo = bank("out")
                    mmgroup(psb[bo][:, 0:Wd], [(Dd[:, f, :], a_bf[:, f, c0:c0 + Wd]) for f in range(22)],
                            reads=[dk, "RA"] + [("a", f, sbi) for f in range(22)], bnk=bo)
                    P.add("dve", lambda e, bo=bo, d=d, c0=c0, Wd=Wd: e.scalar_tensor_tensor(
                        xT[:, d, c0:c0 + Wd], psb[bo][:, 0:Wd], 0.5, xT[:, d, c0:c0 + Wd], ALU.mult, ALU.add),
                        reads=[PS(bo), ("x", d, sbi)], writes=[("x", d, sbi)])

        def mixer_sb(sbi, c0, Wd, nseq, L, first_prompt, pass_idx, fill=None):
            if fill is None:
                fill = lambda n: None
            P.defw = Wd
            ra_barrier(phase="mixer", write_rb=False)
            Hc, Hp = 3, 15
            def uLv(c):
                return ra_f32(12672 + c * 528, nseq * (Hc + L)).rearrange("p (s t) -> p s t", s=nseq)
            def uPv(g):
                return ra_f32(14784 + g * 528, nseq * (Hp + L)).rearrange("p (s t) -> p s t", s=nseq)
            def f2(base, c):
                return ra_f32(base + c * 512, Wd)
            def f3(base, c):
                return f2(base, c).rearrange("p (s t) -> p s t", s=nseq)
            GA, CV, TA, TB, MM = (16896 if sbi % 2 == 0 else 18944), 0, 2048, 4096, 6144
            def cvbv(c):
                return ra_bf16(8192 + c * 256, 512)[:, 0:Wd]
            def psv(i):
                return ra_f32(9216 + i * 528, nseq * (Hp + L)).rearrange("p (s t) -> p s t", s=nseq)
            def dlv(g):
                return ra_bf16(10272 + g * 256, 512)[:, 0:Wd]
            R4 = range(4)

            norm(1, c0, Wd, xk(sbi), hk(sbi))
            fill(1)
            for c in R4:
                if nseq == 1:
                    P.add("dve", lambda e, c=c: e.tensor_copy(uLv(c)[:, 0, 0:Hc], cL[:, c, :]),
                          reads=["RB", ("cL", c), ("uL", c)], writes=[("uLh", c)])
                    P.add("dve", lambda e, c=c: e.tensor_copy(uPv(c)[:, 0, 0:Hp], cP[:, c, :]),
                          reads=["RB", ("cP", c), ("uP", c)], writes=[("uPh", c)])
                else:
                    P.add("dve", lambda e, c=c: e.tensor_copy(uLv(c)[:, :, 0:Hc], sc_hist[:, c, :].rearrange("p (s r) -> p s r", s=16)),
                          reads=["RB", ("sc_hist", c), ("uL", c)], writes=[("uLh", c)])
                    P.add("dve", lambda e, c=c: e.tensor_copy(uPv(c)[:, :, 0:Hp], sp_hist[:, c, :].rearrange("p (s r) -> p s r", s=16)),
                          reads=["RB", ("sp_hist", c), ("uP", c)], writes=[("uPh", c)])
            for j in (0, 1, 4, 5, 2, 3):
                S, skey = wslab("win", j)
                for half in range(2):
                    cc = 2 * j + half
                    b = bank("mm")
                    mmgroup(psb[b][:, 0:Wd], [(S[:, k, half * 128:(half + 1) * 128], hT[:, k, c0:c0 + Wd]) for k in range(8)],
                            reads=[skey] + hk(sbi), bnk=b)
                    src3 = psb[b][:, 0:Wd].rearrange("p (s t) -> p s t", s=nseq)
                    if cc < 4:
                        P.add("act", lambda e, cc=cc, src3=src3: e.copy(uLv(cc)[:, :, Hc:Hc + L], src3),
                              reads=[PS(b), "RB", ("uLh", cc)], writes=[("uL", cc)])
                    elif cc < 8:
                        P.add("act", lambda e, cc=cc, b=b: e.activation(f2(GA, cc - 4), psb[b][:, 0:Wd], AF.Gelu_apprx_tanh),
                              reads=[PS(b), "RB"], writes=[("ga", sbi % 2, cc - 4)])
                    else:
                        P.add("act", lambda e, cc=cc, src3=src3: e.copy(uPv(cc - 8)[:, :, Hp:Hp + L], src3),
                              reads=[PS(b), "RB", ("uPh", cc - 8)], writes=[("uP", cc - 8)])
                if j == 1:
                    for c in R4:
                        P.add("act", lambda e, c=c: e.activation(f3(CV, c), uLv(c)[:, :, 0:L], AF.Identity,
                                                                 scale=convw[:, c, 0:1], bias=vec[:, 0, c:c + 1]),
                              reads=["RA", ("uL", c), ("uLh", c), "convw", "vec"], writes=[("cv", c)])
                    for k in range(1, 4):
                        for c in R4:
                            P.add("dve", lambda e, c=c, k=k: e.scalar_tensor_tensor(f3(CV, c), uLv(c)[:, :, k:k + L], convw[:, c, k:k + 1],
                                                                                   f3(CV, c), ALU.mult, ALU.add),
                                  reads=["RA", ("uL", c), ("uLh", c), "convw", ("cv", c)], writes=[("cv", c)])
                    for c in R4:
                        P.add("act", lambda e, c=c: e.copy(cvbv(c), f2(CV, c)), reads=["RA", ("cv", c)], writes=[("cvb", c)])
                    if nseq == 1:
                        for c in R4:
                            P.add("dve", lambda e, c=c: e.tensor_copy(cL[:, c, :], uLv(c)[:, 0, L:L + Hc]),
                                  reads=["RB", ("uL", c)], writes=[("cL", c)])
                    fill(1)
                if j == 5:
                    if nseq == 1:
                        for c in R4:
                            P.add("dve", lambda e, c=c: e.tensor_copy(cP[:, c, :], uPv(c)[:, 0, L:L + Hp]),
                                  reads=["RB", ("uP", c)], writes=[("cP", c)])
                    for g, win in enumerate((2, 4, 8, 16)):
                        uP = uPv(g)
                        TT = Hp + L
                        cur = uP
                        curkey = [("uP", g), ("uPh", g)]
                        for lev in range(g + 1):
                            sh = 1 << lev
                            lo = (1 << (lev + 1)) - 1
                            dstb = psv(lev % 2)
                            P.add("dve", lambda e, dstb=dstb, cur=cur, lo=lo, sh=sh, TT=TT: e.tensor_tensor(
                                dstb[:, :, lo:TT], cur[:, :, lo:TT], cur[:, :, lo - sh:TT - sh], ALU.add),
                                reads=["RA"] + curkey, writes=[("psv", lev % 2)])
                            cur = dstb
                            curkey = [("psv", lev % 2)]
                        dl = dlv(g)
                        dl3 = dl.rearrange("p (s t) -> p s t", s=nseq)
                        P.add("dve", lambda e, dl3=dl3, cur=cur, uP=uP, win=win: e.scalar_tensor_tensor(
                            dl3, cur[:, :, Hp:Hp + L], 1.0 / win, uP[:, :, Hp:Hp + L], ALU.mult, ALU.subtract),
                            reads=["RA", ("uP", g)] + curkey, writes=[("dl", g)])
                        if first_prompt:
                            t0 = f2(MM, 0)
                            P.add("dve", lambda e, t0=t0, cur=cur, g=g: e.tensor_tensor(t0[:, 0:15], cur[:, 0, Hp:Hp + 15], invc[:, g, 0:15], ALU.mult),
                                  reads=["RA", "invc"] + curkey, writes=[("mm", 0)])
                            P.add("dve", lambda e, t0=t0, dl=dl, uP=uP, g=g: e.tensor_tensor(dl[:, 0:15], t0[:, 0:15], uP[:, 0, Hp:Hp + 15], ALU.subtract),
                                  reads=["RA", ("mm", 0), ("uP", g), ("dl", g)], writes=[("dl", g)])
                    fill(1)
            for g in R4:
                bq = bank("out")
                mmgroup(psb[bq][:, 0:Wd], [(poolw[:, g, :], dlv(g))], reads=["RA", ("dl", g), "poolw"], bnk=bq)
                P.add("act", lambda e, bq=bq, g=g: e.activation(hT[:, 4 + g, c0:c0 + Wd], psb[bq][:, 0:Wd], AF.Copy, scale=vec[:, 4, g:g + 1]),
                      reads=[PS(bq), "vec"], writes=[("h", 4 + g, sbi)])
            for c in R4:
                br = bank("mm"); bi = bank("mm")
                mmgroup(psb[br][:, 0:Wd], [(wabd[:, c, :], cvbv(c))], reads=["RA", ("cvb", c), "wabd"], bnk=br)
                mmgroup(psb[bi][:, 0:Wd], [(wxbd[:, c, :], cvbv(c))], reads=["RA", ("cvb", c), "wxbd"], bnk=bi)
                P.add("act", lambda e, br=br, c=c: e.activation(f2(TA, c), psb[br][:, 0:Wd], AF.Tanh, scale=0.5, bias=hvec[:, 0, c:c + 1]),
                      reads=["RA", PS(br), "hvec"], writes=[("ta", c)])
                P.add("act", lambda e, bi=bi, c=c: e.activation(f2(TB, c), psb[bi][:, 0:Wd], AF.Tanh, scale=0.5, bias=hvec[:, 1, c:c + 1]),
                      reads=["RA", PS(bi), "hvec"], writes=[("tb", c)])
            fill(1)
            for c in R4:
                P.add("act", lambda e, c=c: e.activation(f2(TA, c), f2(TA, c), AF.Exp, scale=hvec[:, 2, c:c + 1], bias=hvec[:, 2, c:c + 1]),
                      reads=["RA", ("ta", c), "hvec"], writes=[("ta", c)])
            for c in R4:
                P.add("act", lambda e, c=c: e.activation(f2(MM, c), f2(TA, c), AF.Square), reads=["RA", ("ta", c)], writes=[("mm", c)])
            for c in R4:
                P.add("dve", lambda e, c=c: e.scalar_tensor_tensor(f2(TB, c), f2(TB, c), 1.0, f2(CV, c), ALU.add, ALU.mult),
                      reads=["RA", ("tb", c), ("cv", c)], writes=[("tb", c)])
            for c in R4:
                P.add("dve", lambda e, c=c: e.tensor_scalar(f2(MM, c), f2(MM, c), 1.0, None, ALU.min), reads=["RA", ("mm", c)], writes=[("mm", c)])
            for c in R4:
                P.add("act", lambda e, c=c: e.activation(f2(MM, c), f2(MM, c), AF.Sqrt, scale=-1.0, bias=onec[:, 0:1]),
                      reads=["RA", ("mm", c), "onec"], writes=[("mm", c)])
            fill(1)
            for c in R4:
                P.add("dve", lambda e, c=c: e.scalar_tensor_tensor(f2(TB, c), f2(TB, c), 0.5, f2(MM, c), ALU.mult, ALU.mult),
                      reads=["RA", ("tb", c), ("mm", c)], writes=[("tb", c)])
            for c in R4:
                if nseq == 1:
                    P.add("dve", lambda e, c=c: e.tensor_tensor_scan(f2(CV, c), f2(TA, c), f2(TB, c), hstate[:, c:c + 1], ALU.mult, ALU.add),
                          reads=["RA", ("ta", c), ("tb", c), ("cv", c), ("hstate", c)], writes=[("cv", c)], cost=0.2 + Wd / 400.0)
                    P.add("dve", lambda e, c=c: e.tensor_copy(hstate[:, c:c + 1], f2(CV, c)[:, Wd - 1:Wd]),
                          reads=["RA", ("cv", c)], writes=[("hstate", c)])
                else:
                    a3, b3, m3, cv3 = f3(TA, c), f3(TB, c), f3(MM, c), f3(CV, c)
                    P.add("dve", lambda e, m3=m3, a3=a3, c=c: e.tensor_tensor(m3[:, :, 0:1], a3[:, :, 0:1], sl_hist[:, c, :].rearrange("p (s o) -> p s o", o=1), ALU.mult),
                          reads=["RA", ("ta", c), ("mm", c), ("tb", c), ("sl_hist", c)], writes=[("mm", c)])
                    P.add("dve", lambda e, m3=m3, b3=b3: e.tensor_tensor(b3[:, :, 0:1], b3[:, :, 0:1], m3[:, :, 0:1], ALU.add),
                          reads=["RA", ("tb", c), ("mm", c)], writes=[("tb", c)])
                    P.add("dve", lambda e, a3=a3: e.memset(a3[:, :, 0:1], 0.0), reads=["RA", ("ta", c), ("mm", c)], writes=[("ta", c)])
                    P.add("dve", lambda e, c=c: e.tensor_tensor_scan(f2(CV, c), f2(TA, c), f2(TB, c), 0.0, ALU.mult, ALU.add),
                          reads=["RA", ("ta", c), ("tb", c), ("cv", c)], writes=[("cv", c)], cost=0.2 + Wd / 400.0)
                    P.add("dve", lambda e, cv3=cv3, c=c: e.tensor_copy(sl_hist[:, c, :].rearrange("p (s o) -> p s o", o=1), cv3[:, :, L - 1:L]),
                          reads=["RA", ("cv", c), ("sl_hist", c)], writes=[("sl_out", c)])
                P.add("dve", lambda e, c=c: e.tensor_tensor(hT[:, c, c0:c0 + Wd], f2(GA, c), f2(CV, c), ALU.mult),
                      reads=["RA", ("ga", sbi % 2, c), ("cv", c)], writes=[("h", c, sbi)])
            fill(1)
            if nseq > 1:
                store_rows_T(lambda c: uLv(c)[:, :, L:L + Hc], [("uL", c) for c in range(4)] + ["RA"], 48,
                             lambda e, s: e.dma_start(out=sconv_o, in_=s[0:48, 0:512]), s3=16)
                store_rows_T(lambda c: sl_hist[:, c, :], [("sl_out", c) for c in range(4)], 16,
                             lambda e, s: e.dma_start(out=slru_o, in_=s[0:16, 0:512]))
                sp3 = spool_o.rearrange("(s r) c -> s r c", r=15)
                store_rows_T(lambda c: uPv(c)[:, :, Hp:Hp + L], [("uP", c) for c in range(4)] + ["RA"], 128,
                             lambda e, s: [e.dma_start(out=sp3[q, 7:15, :], in_=s[8 * q:8 * q + 8, 0:512]) for q in range(16)], ndma=16, s3=16)
            elif pass_idx == 1 and c0 + Wd == 1024:
                store_rows_T(lambda c: cL[:, c, :], [("cL", c) for c in range(4)], 3,
                             lambda e, s: e.dma_start(out=pconv_o, in_=s[0:3, 0:512]))
                store_rows_T(lambda c: cP[:, c, :], [("cP", c) for c in range(4)], 15,
                             lambda e, s: e.dma_start(out=ppool_o, in_=s[0:15, 0:512]))
                store_rows_T(lambda c: hstate[:, c:c + 1], [("hstate", c) for c in range(4)], 1,
                             lambda e, s: e.dma_start(out=plru_o, in_=s[0:1, 0:512]))
            for j in range(4):
                S, skey = wslab("wout", j)
                for half in range(2):
                    d = 2 * j + half
                    bo = bank("out")
                    mmgroup(psb[bo][:, 0:Wd], [(S[:, k, half * 128:(half + 1) * 128], hT[:, k, c0:c0 + Wd]) for k in range(8)],
                            reads=[skey] + hk(sbi), bnk=bo)
                    P.add("dve", lambda e, bo=bo, d=d: e.tensor_tensor(xT[:, d, c0:c0 + Wd], xT[:, d, c0:c0 + Wd], psb[bo][:, 0:Wd], ALU.add),
                          reads=[PS(bo), ("x", d, sbi)], writes=[("x", d, sbi)])
                fill(1)

        def xattn(sbs, with_sample, with_mem=False):
            ra_barrier()
            if with_mem:
                mem_phase()
            Kst = [ra_bf16(7168 + i * 1024, 2048).rearrange("p (a c) -> p a c", a=2) for i in range(4)]
            Vst = [ra_bf16(11264 + i * 1024, 2048).rearrange("p (a c) -> p a c", a=2) for i in range(4)]
            kTs = [ra_bf16(15360 + i * 1024, 2048).rearrange("p (a m) -> p a m", a=8) for i in range(4)]
            q = ra_bf16(0, 8 * TM).rearrange("p (c t) -> p c t", c=8)
            def pTv(i):
                return ra_bf16(4608 + i * 512, 1024).rearrange("p (a t) -> p a t", a=2)
            def rdv(i):
                return ra_f32(5632 + i * 512, 512)
            for (sbi, c0, Wd) in sbs:
                P.defw = Wd
                norm(2, c0, Wd, xk(sbi), hk(sbi))
            for j in range(4):
                S, skey = wslab("wq", j)
                for (sbi, c0, Wd) in sbs:
                    P.defw = Wd
                    for half in range(2):
                        d = 2 * j + half
                        b = bank("mm")
                        mmgroup(psb[b][:, 0:Wd], [(S[:, k, half * 128:(half + 1) * 128], hT[:, k, c0:c0 + Wd]) for k in range(8)],
                                reads=[skey] + hk(sbi), bnk=b)
                        P.add("act", lambda e, b=b, d=d, c0=c0, Wd=Wd: e.copy(q[:, d, c0:c0 + Wd], psb[b][:, 0:Wd]),
                              reads=[PS(b), "RA"], writes=[("q", d, sbi)])
            okeys = hk
            pcount = 0
            for (sbi, c0, Wd) in sbs:
                P.defw = Wd
                if Wd != 512:
                    continue
                for h in range(4):
                    pi = pcount % 2
                    pcount += 1
                    pT = pTv(pi)
                    for mc in range(2):
                        b = bank("mm")
                        mmgroup(psb[b][:, 0:Wd], [(kT[:, 2 * h + dc, mc * 128:(mc + 1) * 128], q[:, 2 * h + dc, c0:c0 + Wd]) for dc in range(2)],
                                reads=["RA", ("q", 2 * h, sbi), ("q", 2 * h + 1, sbi)] + kTkeys, bnk=b)
                        P.add("act", lambda e, b=b, pT=pT, mc=mc: e.activation(pT[:, mc, :], psb[b][:, 0:512], AF.Exp, scale=0.0625),
                              reads=[PS(b), "RA"], writes=[("pT", pi, mc)])
                    bd = 6
                    mmgroup(psb[bd][:, 0:Wd], [(ones_b[:], pT[:, mc, :]) for mc in range(2)],
                            reads=["RA", ("pT", pi, 0), ("pT", pi, 1), "ones_b"], bnk=bd)
                    rd = rdv(pi)
                    P.add("dve", lambda e, rd=rd, bd=bd: e.reciprocal(rd, psb[bd][:, 0:512]), reads=[PS(bd), "RA"], writes=[("rd", pi)], cost=3.4)
                    for dc in range(2):
                        bo = bank("out")
                        mmgroup(psb[bo][:, 0:Wd], [(Vb[:, mc, h * 256 + dc * 128:h * 256 + (dc + 1) * 128], pT[:, mc, :]) for mc in range(2)],
                                reads=["RA", ("pT", pi, 0), ("pT", pi, 1)] + Vbkeys, bnk=bo)
                        P.add("dve", lambda e, bo=bo, rd=rd, h=h, dc=dc, c0=c0, Wd=Wd: e.tensor_tensor(
                            hT[:, 2 * h + dc, c0:c0 + Wd], psb[bo][:, 0:Wd], rd, ALU.mult),
                            reads=[PS(bo), ("rd", pi), "RA"], writes=[("h", 2 * h + dc, sbi)])
            if with_sample:
                sbi, c0, Wd = sbs[-1]
                for s in range(16):
                    sl = s % 4
                    sp_ = s % 2
                    P.add("pool", lambda e, s=s, sl=sl: e.dma_start(out=Kst[sl], in_=ck[s].rearrange("(a p) c -> p a c", p=128)),
                          reads=["RA"], writes=[("Kst", sl)], dma=f"Kst{sl}", nbytes=1048576)
                    P.add("pool", lambda e, s=s, sl=sl: e.dma_start(out=Vst[sl], in_=cvv[s].rearrange("(a p) c -> p a c", p=128)),
                          reads=["RA"], writes=[("Vst", sl)], dma=f"Vst{sl}", nbytes=1048576)
                    for half in range(2):
                        b = bank("mm")
                        pb = psb[b][:, :].bitcast(BF16)

                        def emit(e, half=half, pb=pb, sl=sl):
                            last = None
                            for hh in range(2):
                                for dc in range(2):
                                    for mc in range(2):
                                        h = 2 * half + hh
                                        col = ((hh * 2 + dc) * 256 + mc * 128)
                                        last = e.transpose(pb[:, col:col + 128],
                                                           Kst[sl][:, mc, h * 256 + dc * 128:h * 256 + (dc + 1) * 128], ident_b[:])
                            return last
                        P.add("pe", emit, reads=[("Kst", sl), "ident_b", "RA"], writes=[PS(b)], cost=1.0)
                        P.add("act", lambda e, half=half, pb=pb, sl=sl: e.copy(
                            kTs[sl][:, 4 * half:4 * half + 4, :], pb.rearrange("p (a m) -> p a m", a=4)),
                            reads=[PS(b), "RA"], writes=[("kTs", sl, half)])
                    pTs = ra_bf16(6656 + sp_ * 32, 64).rearrange("p (a t) -> p a t", a=8)
                    rds = ra_f32(6720, 32).rearrange("p (h o t) -> p h o t", h=4, o=1)
                    bs = bank("out")

                    def emit_sc(e, s=s, sl=sl, bs=bs):
                        last = None
                        for h in range(4):
                            for mc in range(2):
                                for dc in range(2):
                                    last = e.matmul(psb[bs][:, (h * 2 + mc) * 8:(h * 2 + mc) * 8 + 8],
                                                    kTs[sl][:, 2 * h + dc, mc * 128:(mc + 1) * 128],
                                                    q[:, 2 * h + dc, c0 + 8 * s:c0 + 8 * s + 8], start=(dc == 0), stop=(dc == 1))
                        return last
                    P.add("pe", emit_sc, cost=1.2, reads=["RA", ("kTs", sl, 0), ("kTs", sl, 1)] + [("q", d, sbi) for d in range(8)], writes=[PS(bs)])
                    P.add("act", lambda e, bs=bs, pTs=pTs: e.activation(pTs, psb[bs][:, 0:64].rearrange("p (a t) -> p a t", a=8), AF.Exp, scale=0.0625),
                          reads=[PS(bs), "RA"], writes=[("pTs", sp_)])
                    bd = 6

                    def emit_den(e, pTs=pTs, bd=bd):
                        last = None
                        for h in range(4):
                            for mc in range(2):
                                last = e.matmul(psb[bd][:, h * 8:h * 8 + 8], ones_b[:], pTs[:, h * 2 + mc, :], start=(mc == 0), stop=(mc == 1))
                        return last
                    P.add("pe", emit_den, cost=0.6, reads=["RA", ("pTs", sp_), "ones_b"], writes=[PS(bd)])
                    P.add("dve", lambda e, bd=bd, rds=rds: e.reciprocal(rds, psb[bd][:, 0:32].rearrange("p (h o t) -> p h o t", h=4, o=1)),
                          reads=[PS(bd), "RA"], writes=["rds"])
                    bp = 7

                    def emit_pv(e, pTs=pTs, bp=bp, sl=sl):
                        last = None
                        for h in range(4):
                            for dc in range(2):
                                for mc in range(2):
                                    last = e.matmul(psb[bp][:, (h * 2 + dc) * 8:(h * 2 + dc) * 8 + 8],
                                                    Vst[sl][:, mc, h * 256 + dc * 128:h * 256 + (dc + 1) * 128],
                                                    pTs[:, h * 2 + mc, :], start=(mc == 0), stop=(mc == 1))
                        return last
                    P.add("pe", emit_pv, cost=1.2, reads=["RA", ("pTs", sp_), ("Vst", sl)], writes=[PS(bp)])
                    P.add("dve", lambda e, bp=bp, rds=rds, s=s: e.tensor_tensor(
                        hT[:, :, c0 + 8 * s:c0 + 8 * s + 8].rearrange("p (h d) t -> p h d t", h=4),
                        psb[bp][:, 0:64].rearrange("p (h d t) -> p h d t", h=4, d=2),
                        rds.to_broadcast([128, 4, 2, 8]), ALU.mult),
                        reads=[PS(bp), "rds", "RA"], writes=hk(sbi))
            for j in range(4):
                S, skey = wslab("wo", j)
                for (sbi, c0, Wd) in sbs:
                    P.defw = Wd
                    for half in range(2):
                        d = 2 * j + half
                        bo = bank("out")
                        mmgroup(psb[bo][:, 0:Wd], [(S[:, k, half * 128:(half + 1) * 128], hT[:, k, c0:c0 + Wd]) for k in range(8)],
                                reads=[skey] + okeys(sbi), bnk=bo)
                        P.add("dve", lambda e, bo=bo, d=d, c0=c0, Wd=Wd: e.tensor_tensor(xT[:, d, c0:c0 + Wd], xT[:, d, c0:c0 + Wd], psb[bo][:, 0:Wd], ALU.add),
                              reads=[PS(bo), ("x", d, sbi)], writes=[("x", d, sbi)])

        def load_dma(src):
            k = cnt["in"] % 4
            cnt["in"] += 1
            P.add("sp", lambda e, k=k, src=src: e.dma_start(out=rb_slot(k), in_=src), reads=["RB"], writes=[("rbs", k)], dma=f"rbs{k}", nbytes=524288)
            return k

        def consume_tile(k, col, sbi):
            for half in range(2):
                b = bank("mm")

                def emit(e, k=k, half=half, b=b):
                    last = None
                    for cc in range(4):
                        c = half * 4 + cc
                        last = e.transpose(psb[b][:, cc * 128:(cc + 1) * 128], rb_slot(k)[:, c * 128:(c + 1) * 128], ident_f[:])
                    return last
                P.add("pe", emit, reads=[("rbs", k), "ident_f", "RB"], writes=[PS(b)], cost=0.9)
                eng = "act" if half == 0 else "dve"
                fn = (lambda e, half=half, b=b, col=col: e.copy(
                    xT[:, half * 4:half * 4 + 4, col:col + 128], psb[b][:, :].rearrange("p (c m) -> p c m", c=4))) if half == 0 else \
                     (lambda e, half=half, b=b, col=col: e.tensor_copy(
                    xT[:, half * 4:half * 4 + 4, col:col + 128], psb[b][:, :].rearrange("p (c m) -> p c m", c=4)))
                P.add(eng, fn, reads=[PS(b)], writes=[("x", half * 4 + cc, sbi) for cc in range(4)])

        def out_tile(dst, col, sbi):
            k = 4 + cnt["outs"] % 2
            cnt["outs"] += 1
            for half in range(2):
                b = bank("mm")

                def emit(e, half=half, b=b, col=col):
                    last = None
                    for cc in range(4):
                        c = half * 4 + cc
                        last = e.transpose(psb[b][:, cc * 128:(cc + 1) * 128], xT[:, c, col:col + 128], ident_f[:])
                    return last
                P.add("pe", emit, reads=[("x", half * 4 + cc, sbi) for cc in range(4)] + ["ident_f"], writes=[PS(b)], cost=0.9)
                if half == 0:
                    P.add("act", lambda e, b=b, k=k: e.copy(rb_slot(k)[:, 0:512], psb[b][:, :]),
                          reads=[PS(b), "RB"], writes=[("rbs", k)])
                else:
                    P.add("dve", lambda e, b=b, k=k: e.tensor_copy(rb_slot(k)[:, 512:1024], psb[b][:, :]),
                          reads=[PS(b), "RB"], writes=[("rbs", k)])
            ok = ("o", cnt["odma"]); cnt["odma"] += 1; out_keys.append(ok)
            P.add("sp", lambda e, k=k, dst=dst: e.dma_start(out=dst, in_=rb_slot(k)),
                  reads=[("rbs", k), "RB"], writes=[ok], dma=f"rbs{k}", nbytes=524288)

        cnt["in"] = 0
        cnt["outs"] = 0

        def pass_tiles(pi):
            t = [(xp[(pi * 8 + i) * 128:(pi * 8 + i + 1) * 128, :], i * 128, i // 4) for i in range(8)]
            o = [(yp[(pi * 8 + i) * 128:(pi * 8 + i + 1) * 128, :], i * 128, i // 4) for i in range(8)]
            if pi == 1:
                t.append((xsm, 1024, 2))
                o.append((ys, 1024, 2))
            return t, o

        SBS = {0: [(0, 0, 512), (1, 512, 512)], 1: [(0, 0, 512), (1, 512, 512), (2, 1024, 128)]}

        P.stage(3)
        tiles, _ = pass_tiles(0)
        pend = [load_dma(t[0]) for t in tiles[:4]]
        nxt = 4
        for i, (src, col, sbi) in enumerate(tiles):
            consume_tile(pend.pop(0), col, sbi)
            if nxt < len(tiles):
                pend.append(load_dma(tiles[nxt][0]))
                nxt += 1
            if i % 4 == 3:
                P.stage(4)
                sb_ = SBS[0][sbi]
                norm(0, sb_[1], sb_[2], xk(sbi), hk(sbi))
                P.stage(3)

        P.stage(2)
        P.rb_with_ra = False
        mem_phase()
        P.rb_with_ra = True

        for pass_idx in range(2):
            sbs = SBS[pass_idx]
            tiles, otiles = pass_tiles(pass_idx)
            P.stage(4 + 6 * pass_idx)
            ffn("f1", sbs, sb_outer=True)
            P.stage(5 + 6 * pass_idx)
            mixer_sb(0, 0, 512, 1, 512, pass_idx == 0, pass_idx)
            mixer_sb(1, 512, 512, 1, 512, False, pass_idx)
            if pass_idx == 1:
                mixer_sb(2, 1024, 128, 16, 8, False, pass_idx)
            P.stage(6 + 6 * pass_idx)
            xattn(sbs, pass_idx == 1, with_mem=False)
            ntiles = []
            pend = []
            nxt = 0
            if pass_idx == 0:
                P.stage(9)
                rb_barrier()
                ntiles, _ = pass_tiles(1)
                pend = [load_dma(t[0]) for t in ntiles[:4]]
                nxt = 4
            else:
                rb_barrier()
            P.stage(7 + 6 * pass_idx)
            for (sbi, c0, Wd) in sbs:
                P.defw = Wd
                norm(4, c0, Wd, xk(sbi), hk(sbi))
            ffn("f2", sbs, sb_outer=True)
            for (sbi, c0, Wd) in sbs:
                P.defw = Wd
                P.stage(8 + 6 * pass_idx)
                norm(5, c0, Wd, xk(sbi), hk(sbi), inplace=True)
                for (dst, col, s2) in otiles:
                    if s2 == sbi:
                        out_tile(dst, col, sbi)
                if pass_idx == 0:
                    P.stage(9)
                    for (src, col, s2) in ntiles:
                        if s2 == sbi:
                            consume_tile(pend.pop(0), col, sbi)
                            if nxt < len(ntiles):
                                pend.append(load_dma(ntiles[nxt][0]))
                                nxt += 1
                    P.stage(10)
                    norm(0, c0, Wd, xk(sbi), hk(sbi))
            if pass_idx == 0:
                P.stage(9)
                for (src, col, s2) in ntiles:
                    if s2 == 2:
                        consume_tile(pend.pop(0), col, 2)
                P.stage(10)
                norm(0, 1024, 128, xk(2), hk(2))
        P.stage(0)
        P.add("sp", lambda e: e.nop(), reads=out_keys)
        P.build()
    return nc


_NC_CACHE = {}


def kernel(**inp):
    f = lambda a: np.ascontiguousarray(np.asarray(a, dtype=np.float32))
    if "nc" not in _NC_CACHE:
        _NC_CACHE["nc"] = build_nc()
    nc = _NC_CACHE["nc"]
    gains = np.stack([f(inp["ffn1_norm"])[0], f(inp["mix_norm"])[0], f(inp["xattn_norm"])[0],
                      f(inp["mem_norm"])[0], f(inp["ffn2_norm"])[0], f(inp["final_norm"])], axis=0)
    gains = np.ascontiguousarray(gains.reshape(6, 8, 128).transpose(2, 0, 1))
    vec = np.stack([f(inp["conv_b"])[0], f(inp["lru_ba"])[0], f(inp["lru_bx"])[0],
                    f(inp["lru_lambda"])[0], f(inp["pool_scale"])[0]], axis=0)
    vec = np.ascontiguousarray(vec.reshape(5, 4, 128).transpose(2, 0, 1))
    convw = np.ascontiguousarray(f(inp["conv_w"])[0].reshape(4, 4, 128).transpose(2, 1, 0))
    shared = {
        "f1g": f(inp["ffn1_w_gate"])[0], "f1u": f(inp["ffn1_w_up"])[0], "f1d": f(inp["ffn1_w_down"])[0],
        "win": f(inp["w_in"])[0], "wout": f(inp["w_out"])[0], "wq": f(inp["xattn_wq"])[0],
        "wk": f(inp["xattn_wk"])[0], "wv": f(inp["xattn_wv"])[0], "wo": f(inp["xattn_wo"])[0],
        "f2g": f(inp["ffn2_w_gate"])[0], "f2u": f(inp["ffn2_w_up"])[0], "f2d": f(inp["ffn2_w_down"])[0],
        "gains": gains, "vec512": vec, "convw": convw,
        "lwa": f(inp["lru_wa"])[0], "lwx": f(inp["lru_wx"])[0], "poolw": f(inp["pool_w"])[0],
    }
    xpr = f(inp["x_prompt"]); xsa = f(inp["x_sample"]); memp = f(inp["mem_prompt"])
    sc = f(inp["state_conv"])[0]; slr = f(inp["state_lru"])[0]; spl = f(inp["state_pool"])[0]
    ckk = f(inp["cache_mem_k"])[0]; cvv = f(inp["cache_mem_v"])[0]
    in_maps = []
    for c in range(NCORES):
        s0, s1 = 16 * c, 16 * c + 16
        m = dict(shared)
        m["xp"] = xpr[c]
        m["xsm"] = xsa[s0:s1].reshape(128, 1024)
        m["mem"] = memp[c]
        m["sconv"] = sc[s0:s1].reshape(48, 512)
        m["slru"] = slr[s0:s1]
        m["spool"] = spl[s0:s1].reshape(240, 512)
        m["ck"] = ckk[s0:s1].reshape(16, 256, 1024)
        m["cv"] = cvv[s0:s1].reshape(16, 256, 1024)
        in_maps.append(m)
    res = run_bass_kernel_spmd(nc, in_maps, core_ids=list(range(NCORES)))
    R = res.results
    y_prompt = np.stack([R[c]["yp"] for c in range(NCORES)], 0)
    y_sample = np.concatenate([R[c]["ys"].reshape(16, 8, 1024) for c in range(NCORES)], 0)
    p_conv = np.stack([R[c]["pconv"] for c in range(NCORES)], 0)[None]
    p_lru = np.stack([R[c]["plru"].reshape(512) for c in range(NCORES)], 0)[None]
    p_pool = np.stack([R[c]["ppool"] for c in range(NCORES)], 0)[None]
    p_mk = np.stack([R[c]["pmk"].reshape(256, 4, 256) for c in range(NCORES)], 0)[None]
    p_mv = np.stack([R[c]["pmv"].reshape(256, 4, 256) for c in range(NCORES)], 0)[None]
    s_conv = np.concatenate([R[c]["sconv_o"].reshape(16, 3, 512) for c in range(NCORES)], 0)[None]
    s_lru = np.concatenate([R[c]["slru_o"] for c in range(NCORES)], 0)[None]
    s_pool = np.concatenate([R[c]["spool_o"].reshape(16, 15, 512) for c in range(NCORES)], 0)[None]
    outs = (y_prompt, y_sample, p_conv, p_lru, p_pool, p_mk, p_mv, s_conv, s_lru, s_pool)
    return tuple(np.ascontiguousarray(o, dtype=np.float32) for o in outs)
```

```python
import contextlib
import os
import numpy as np
import concourse.bass as bass
import concourse.mybir as mybir
from concourse.bass_utils import run_bass_kernel_spmd

F32 = mybir.dt.float32
BF16 = mybir.dt.bfloat16
AF = mybir.ActivationFunctionType
ALU = mybir.AluOpType

ENGS = ("pe", "act", "dve", "pool", "sp")
NCORES = 8
TM = 1152
NS = 6
RING_W = 2816
RA_WORDS = 21504
EPS = 1e-6


class _Op:
    __slots__ = ("eng", "emit", "reads", "writes", "dma", "deps", "idx", "sig", "dma_cnt", "n_dma", "cost", "nbytes", "alldeps")


class Prog:
    def __init__(self, nc):
        self.nc = nc
        self.ops = []
        self.last_w = {}
        self.readers = {}
        self.dma_tot = {}
        self.muted = False
        self.rb_with_ra = True
        self.defw = 512
        st = os.environ.get("KSTAGES")
        self.stages = None if not st else set(int(x) for x in st.split(","))

    def stage(self, n):
        self.muted = self.stages is not None and n != 0 and n not in self.stages

    def add(self, eng, emit, reads=(), writes=(), dma=None, n_dma=1, cost=None, nbytes=0):
        if self.muted:
            return -1
        if cost is None:
            w = self.defw
            cost = {"pe": 0.5, "act": 0.22 + w / 1400.0, "dve": 0.16 + w / 960.0, "pool": 0.3, "sp": 0.06}[eng]
            if dma is not None:
                cost = 0.06 if eng == "sp" else 0.9
        if self.rb_with_ra and "RA" in reads and "RB" not in reads:
            reads = list(reads) + ["RB"]
        pr = [k for k in reads if isinstance(k, tuple) and k[0] == "ps"]
        if pr:
            reads = [k for k in reads if k not in pr]
            writes = list(writes) + [k for k in pr if k not in writes]
        op = _Op()
        op.eng, op.emit, op.dma, op.n_dma = eng, emit, dma, n_dma
        op.reads, op.writes = tuple(reads), tuple(writes)
        op.cost, op.nbytes = cost, nbytes
        deps = set()
        for k in op.reads:
            w = self.last_w.get(k)
            if w is not None:
                deps.add(w)
        for k in op.writes:
            w = self.last_w.get(k)
            if w is not None:
                deps.add(w)
            deps.update(self.readers.get(k, ()))
        i = len(self.ops)
        op.deps = deps
        op.sig = False
        op.idx = 0
        if dma is not None:
            self.dma_tot[dma] = self.dma_tot.get(dma, 0) + 16 * n_dma
            op.dma_cnt = self.dma_tot[dma]
        else:
            op.dma_cnt = 0
        self.ops.append(op)
        for k in op.writes:
            self.last_w[k] = i
            self.readers[k] = []
        for k in op.reads:
            if k not in op.writes:
                self.readers.setdefault(k, []).append(i)
        return i

    def schedule(self):
        ops = self.ops
        n = len(ops)
        succ = [[] for _ in range(n)]
        indeg = [0] * n
        for i, op in enumerate(ops):
            op.alldeps = set(op.deps)
            indeg[i] = len(op.deps)
            for d in op.deps:
                succ[d].append(i)
        done = [0.0] * n
        ready = [0.0] * n
        avail = {e: [] for e in ENGS}
        for i, op in enumerate(ops):
            if indeg[i] == 0:
                avail[op.eng].append(i)
        free = {e: 0.0 for e in ENGS}
        order = {e: [] for e in ENGS}
        dma_pipe = 0.0
        left = n
        DELTA = 0.4
        if os.environ.get("KNOSCHED"):
            for i, op in enumerate(ops):
                order[op.eng].append(i)
            return order
        while left:
            best = None
            for e in ENGS:
                av = avail[e]
                if not av:
                    continue
                f = free[e]
                st_min = min(max(ready[i], f) for i in av)
                cand = min((i for i in av if max(ready[i], f) <= st_min + DELTA), key=lambda i: (ops[i].cost > 2.5, i))
                st = max(ready[cand], f)
                if best is None or (st, cand) < (best[0], best[1]):
                    best = (st, cand, e)
            st, i, e = best
            op = ops[i]
            avail[e].remove(i)
            order[e].append(i)
            if op.dma is not None:
                free[e] = st + op.cost
                x0 = max(st + op.cost + 1.2, dma_pipe)
                dur = op.nbytes / 330e3
                dma_pipe = x0 + dur
                done[i] = x0 + dur + 0.8
            else:
                free[e] = st + op.cost
                done[i] = st + op.cost + 0.05
            left -= 1
            for j in succ[i]:
                indeg[j] -= 1
                if done[i] > ready[j]:
                    ready[j] = done[i]
                if indeg[j] == 0:
                    avail[ops[j].eng].append(j)
        self.est_total = max(done) if done else 0.0
        return order

    def build(self):
        nc = self.nc
        ops = self.ops
        order = self.schedule()
        for op in ops:
            pruned = set()
            for d in op.deps:
                p = ops[d]
                if p.dma is None and p.eng == "pe" and op.eng == "pe" and op.dma is None:
                    continue
                pruned.add(d)
                if p.dma is None:
                    p.sig = True
            op.deps = pruned
        cnt = {e: 0 for e in ENGS}
        dtot = {}
        for e in ENGS:
            for i in order[e]:
                op = ops[i]
                if op.dma is None:
                    if op.sig:
                        cnt[e] += 1
                        op.idx = cnt[e]
                else:
                    dtot[op.dma] = dtot.get(op.dma, 0) + 16 * op.n_dma
                    op.dma_cnt = dtot[op.dma]
        with contextlib.ExitStack() as es:
            esem = {e: es.enter_context(nc.semaphore("s_" + e)) for e in ENGS}
            dsem = {k: es.enter_context(nc.semaphore("d_" + str(k))) for k in self.dma_tot}
            block = es.enter_context(nc.Block())

            def run_engine(ename):
                def body(eng):
                    waited = {}
                    for oi in order[ename]:
                        op = ops[oi]
                        need = {}
                        for d in op.deps:
                            p = ops[d]
                            if p.dma is not None:
                                key, val = ("d", p.dma), p.dma_cnt
                            else:
                                key, val = ("e", p.eng), p.idx
                            if val > need.get(key, 0):
                                need[key] = val
                        for key, val in need.items():
                            if waited.get(key, 0) >= val:
                                continue
                            waited[key] = val
                            sem = dsem[key[1]] if key[0] == "d" else esem[key[1]]
                            eng.wait_ge(sem, val)
                        ins = op.emit(eng)
                        if op.dma is not None:
                            if isinstance(ins, (list, tuple)):
                                assert len(ins) == op.n_dma
                                for x in ins:
                                    x.then_inc(dsem[op.dma], 16)
                            else:
                                assert op.n_dma == 1
                                ins.then_inc(dsem[op.dma], 16)
                        elif op.sig:
                            ins.then_inc(esem[ename], 1)
                return body

            block.tensor(run_engine("pe"))
            block.scalar(run_engine("act"))
            block.vector(run_engine("dve"))
            block.gpsimd(run_engine("pool"))
            block.sync(run_engine("sp"))


def build_nc():
    nc = bass.Bass("TRN2", target_bir_lowering=False)

    def din(name, shape):
        return nc.dram_tensor(name, shape, F32, kind="ExternalInput").ap()

    def dout(name, shape):
        return nc.dram_tensor(name, shape, F32, kind="ExternalOutput").ap()

    xp = din("xp", [2048, 1024]); xsm = din("xsm", [128, 1024]); mem = din("mem", [256, 1024])
    sconv = din("sconv", [48, 512]); slru = din("slru", [16, 512]); spool = din("spool", [240, 512])
    ck = din("ck", [16, 256, 1024]); cvv = din("cv", [16, 256, 1024])
    W = {}
    for nm, shp in [("f1g", [1024, 2816]), ("f1u", [1024, 2816]), ("f1d", [2816, 1024]),
                    ("win", [1024, 1536]), ("wout", [1024, 1024]), ("wq", [1024, 1024]),
                    ("wk", [1024, 1024]), ("wv", [1024, 1024]), ("wo", [1024, 1024]),
                    ("f2g", [1024, 2816]), ("f2u", [1024, 2816]), ("f2d", [2816, 1024])]:
        W[nm] = din(nm, shp)
    gains_d = din("gains", [128, 6, 8])
    vec_d = din("vec512", [128, 5, 4])
    convw_d = din("convw", [128, 4, 4])
    lwa_d = din("lwa", [8, 64, 64]); lwx_d = din("lwx", [8, 64, 64]); poolw_d = din("poolw", [4, 128, 128])

    yp = dout("yp", [2048, 1024]); ys = dout("ys", [128, 1024])
    pconv_o = dout("pconv", [3, 512]); plru_o = dout("plru", [1, 512]); ppool_o = dout("ppool", [15, 512])
    pmk_o = dout("pmk", [256, 1024]); pmv_o = dout("pmv", [256, 1024])
    sconv_o = dout("sconv_o", [48, 512]); slru_o = dout("slru_o", [16, 512]); spool_o = dout("spool_o", [240, 512])

    with contextlib.ExitStack() as es:
        def sb(name, shape, dtype=F32):
            return es.enter_context(nc.sbuf_tensor(name, shape, dtype))

        xT = sb("xT", [128, 8, TM])
        hT = sb("hT", [128, 8, TM], BF16)
        regA = sb("regA", [128, RA_WORDS])
        ring = [sb(f"ring{i}", [128, RING_W], BF16) for i in range(NS)]
        kT = sb("kT", [128, 8, 256], BF16)
        Vb = sb("Vb", [128, 2, 1024], BF16)
        stg = [sb(f"stg{i}", [128, 1024]) for i in range(2)]
        rstd = [sb(f"rstd{i}", [128, 512]) for i in range(2)]
        sgt = [sb(f"sgt{i}", [128, 512]) for i in range(2)]
        ident_f = sb("ident_f", [128, 128])
        ident_b = sb("ident_b", [128, 128], BF16)
        ones_b = sb("ones_b", [128, 128], BF16)
        onesm_b = sb("onesm_b", [128, 128], BF16)
        gains = sb("gains_s", [128, 6, 8])
        vec = sb("vec_s", [128, 5, 4])
        convw = sb("convw_s", [128, 4, 4])
        nlam = sb("nlam", [128, 4])
        tl = [sb(f"tl{i}", [128, 4]) for i in range(4)]
        wabd = sb("wabd", [128, 4, 128], BF16)
        wxbd = sb("wxbd", [128, 4, 128], BF16)
        poolw = sb("poolw_s", [128, 4, 128], BF16)
        invc = sb("invc", [128, 4, 16])
        hstate = sb("hstate", [128, 4])
        cL = sb("cL", [128, 4, 3])
        cP = sb("cP", [128, 4, 15])
        sc_hist = sb("sc_hist", [128, 4, 48])
        sp_hist = sb("sp_hist", [128, 4, 240])
        sl_hist = sb("sl_hist", [128, 4, 16])
        bar = sb("bar", [128, 2])
        epsc = sb("epsc", [128, 1])
        onec = sb("onec", [128, 1])
        hvec = sb("hvec", [128, 3, 4])
        tmpT = sb("tmpT", [128, 4, 128])
        psb = [es.enter_context(nc.psum_tensor(f"psb{i}", [128, 512], F32)) for i in range(8)]

        P = Prog(nc)
        cnt = {"ring": 0, "mm": 0, "out": 0, "stg": 0, "rstd": 0, "sgt": 0, "odma": 0}
        out_keys = []

        def bank(cls):
            if cls == "mm":
                b = cnt["mm"] % 4
                cnt["mm"] += 1
                return b
            b = (4, 5, 7)[cnt["out"] % 3]
            cnt["out"] += 1
            return b

        def PS(b):
            return ("ps", b)

        def ra_f32(off, n):
            return regA[:, off:off + n]

        def ra_bf16(off, n_bf):
            return regA[:, off:off + n_bf // 2].bitcast(BF16)

        last_phase = [None]

        def ra_barrier(rb=True, phase=None, write_rb=True):
            P.rb_with_ra = rb
            if phase is not None and phase == last_phase[0]:
                return
            last_phase[0] = phase
            if rb and not write_rb:
                P.add("dve", lambda e: e.memset(bar[:, 1:2], 0.0), writes=["RB", "bar1"], cost=0.1)
            P.add("dve", lambda e: e.memset(bar[:, 0:1], 0.0), writes=["RA", "bar"] + (["RB"] if (rb and write_rb) else []), cost=0.1)

        def rb_barrier():
            P.add("dve", lambda e: e.memset(bar[:, 1:2], 0.0), writes=["RB", "bar1"])

        def rb_slot(k):
            return ra_f32(12672 + 1024 * k, 1024)

        def ringload(src, k, c):
            s = cnt["ring"] % NS
            cnt["ring"] += 1
            dst = ring[s][:, 0:k * c].rearrange("p (k c) -> p k c", k=k)
            P.add("pool", lambda e, dst=dst, src=src: e.dma_start(out=dst, in_=src),
                  writes=[("ring", s)], dma=f"ring{s}", nbytes=128 * k * c * 4)
            return dst, ("ring", s)

        def wslab(name, j, c=256):
            v = W[name].rearrange("(kc p) c -> p kc c", p=128)
            kc = W[name].shape[0] // 128
            return ringload(v[:, :, j * c:(j + 1) * c], kc, c)

        def mmgroup(out_ap, pairs, reads, bnk):
            def emit(e, out_ap=out_ap, pairs=pairs):
                last = None
                n = len(pairs)
                for i, (l, r) in enumerate(pairs):
                    last = e.matmul(out_ap, l, r, start=(i == 0), stop=(i == n - 1))
                return last
            cst = sum(max(r.shape[-1], 64) / 2400.0 + 0.012 for (_, r) in pairs)
            P.add("pe", emit, reads=reads, writes=[PS(bnk)], cost=cst)

        P.add("pool", lambda e: e.memset(ident_f[:], 0.0), writes=["ident_f"])
        P.add("pool", lambda e: e.affine_select(ident_f[:], ident_f[:], [[-1, 128]], ALU.not_equal, 1.0,
                                                 base=0, channel_multiplier=1),
              reads=["ident_f"], writes=["ident_f"])
        P.add("dve", lambda e: e.tensor_copy(ident_b[:], ident_f[:]), reads=["ident_f"], writes=["ident_b"])
        P.add("dve", lambda e: e.memset(ones_b[:], 1.0), writes=["ones_b"])
        P.add("dve", lambda e: e.memset(epsc[:], EPS), writes=["epsc"])
        P.add("dve", lambda e: e.memset(onec[:], 1.0), writes=["onec"])
        P.add("dve", lambda e: e.memset(onesm_b[:], 1.0 / 1024.0), writes=["onesm_b"])
        P.add("dve", lambda e: e.memset(hstate[:], 0.0), writes=[("hstate", c) for c in range(4)])
        P.add("dve", lambda e: e.memset(cL[:], 0.0), writes=[("cL", c) for c in range(4)])
        P.add("dve", lambda e: e.memset(cP[:], 0.0), writes=[("cP", c) for c in range(4)])
        P.add("sp", lambda e: e.dma_start(out=gains[:], in_=gains_d), writes=["gains"], dma="c0")
        P.add("sp", lambda e: e.dma_start(out=vec[:], in_=vec_d), writes=["vec"], dma="c1")
        P.add("sp", lambda e: e.dma_start(out=convw[:], in_=convw_d), writes=["convw"], dma="c2")
        P.add("pool", lambda e: e.memset(wabd[:], 0.0), writes=["wabd"])
        P.add("pool", lambda e: e.memset(wxbd[:], 0.0), writes=["wxbd"])

        def bd_load(dst, src, key):
            def emit(e):
                r = []
                for hh in range(8):
                    c, j = hh // 2, hh % 2
                    r.append(e.dma_start(out=dst[64 * j:64 * j + 64, c, 64 * j:64 * j + 64], in_=src[hh]))
                return r
            P.add("pool", emit, reads=[key], writes=[key], dma="bd_" + key, n_dma=8)
        bd_load(wabd, lwa_d, "wabd")
        bd_load(wxbd, lwx_d, "wxbd")
        P.add("pool", lambda e: e.dma_start(out=poolw[:], in_=poolw_d.rearrange("g i j -> i g j")),
              writes=["poolw"], dma="c3")
        for g, win in enumerate((2, 4, 8, 16)):
            P.add("pool", lambda e, g=g, win=win: e.memset(invc[:, g, :], 1.0 / win), reads=["invc"], writes=["invc"])
            for t in range(win - 1):
                P.add("pool", lambda e, g=g, t=t: e.memset(invc[:, g, t:t + 1], 1.0 / (t + 1)),
                      reads=["invc"], writes=["invc"])
        lam = vec[:, 3, :]
        P.add("act", lambda e: e.activation(tl[0][:], lam, AF.Exp, scale=-1.0), reads=["vec"], writes=["tl0"])
        P.add("dve", lambda e: e.tensor_scalar(tl[1][:], tl[0][:], 2.0, None, ALU.add), reads=["tl0"], writes=["tl1"])
        P.add("dve", lambda e: e.reciprocal(tl[1][:], tl[1][:]), reads=["tl1"], writes=["tl1"])
        P.add("dve", lambda e: e.tensor_tensor(tl[1][:], tl[0][:], tl[1][:], ALU.mult), reads=["tl0", "tl1"], writes=["tl1"])
        P.add("dve", lambda e: e.tensor_tensor(tl[2][:], tl[1][:], tl[1][:], ALU.mult), reads=["tl1"], writes=["tl2"])
        P.add("dve", lambda e: e.tensor_scalar(tl[3][:], tl[2][:], 1.0 / 11.0, 1.0 / 9.0, ALU.mult, ALU.add), reads=["tl2"], writes=["tl3"])
        for coef in (1.0 / 7.0, 1.0 / 5.0, 1.0 / 3.0, 1.0):
            P.add("dve", lambda e: e.tensor_tensor(tl[3][:], tl[3][:], tl[2][:], ALU.mult), reads=["tl3", "tl2"], writes=["tl3"])
            P.add("dve", lambda e, coef=coef: e.tensor_scalar(tl[3][:], tl[3][:], coef, None, ALU.add), reads=["tl3"], writes=["tl3"])
        P.add("dve", lambda e: e.tensor_tensor(tl[3][:], tl[3][:], tl[1][:], ALU.mult), reads=["tl3", "tl1"], writes=["tl3"])
        P.add("dve", lambda e: e.tensor_scalar(nlam[:], tl[3][:], -16.0, None, ALU.mult), reads=["tl3"], writes=["nlam"])
        P.add("dve", lambda e: e.tensor_scalar(hvec[:, 0:2, :], vec[:, 1:3, :], 0.5, None, ALU.mult), reads=["vec"], writes=["hvec"])
        P.add("dve", lambda e: e.tensor_scalar(hvec[:, 2, :], nlam[:], 0.5, None, ALU.mult), reads=["nlam", "hvec"], writes=["hvec"])

        def load_rows_T(dram_rows, R, dst_fn, dst_keys):
            k = cnt["stg"] % 2
            cnt["stg"] += 1
            P.add("sp", lambda e: e.dma_start(out=stg[k][0:R, 0:512], in_=dram_rows), writes=[("stg", k)], dma=f"stg{k}")
            b = bank("mm")

            def emit(e):
                last = None
                for c in range(4):
                    last = e.transpose(psb[b][:, c * 128:c * 128 + R], stg[k][0:R, c * 128:(c + 1) * 128], ident_f[0:R, 0:R])
                return last
            P.add("pe", emit, reads=[("stg", k), "ident_f"], writes=[PS(b)])
            for c in range(4):
                P.add("act", lambda e, c=c: e.copy(dst_fn(c), psb[b][:, c * 128:c * 128 + R]),
                      reads=[PS(b)], writes=[dst_keys[c]])

        def store_rows_T(src_fn, src_keys, R, dram_writer, ndma=1, s3=None):
            k = cnt["stg"] % 2
            cnt["stg"] += 1
            b = bank("mm")
            if s3 is not None:
                srcs = src_fn
                for c in range(4):
                    P.add("act", lambda e, c=c: e.copy(tmpT[:, c, 0:R].rearrange("p (s r) -> p s r", s=s3), srcs(c)),
                          reads=list(src_keys), writes=[("tmpT", c)])
                src_fn = lambda c: tmpT[:, c, 0:R]
                src_keys = [("tmpT", c) for c in range(4)]

            def emit(e):
                last = None
                for c in range(4):
                    last = e.transpose(psb[b][0:R, c * 128:(c + 1) * 128], src_fn(c), ident_f[:])
                return last
            P.add("pe", emit, reads=list(src_keys) + ["ident_f"], writes=[PS(b)])
            P.add("act", lambda e: e.copy(stg[k][0:R, 0:512], psb[b][0:R, :]), reads=[PS(b)], writes=[("stg", k)])
            ok = ("o", cnt["odma"])
            cnt["odma"] += 1
            out_keys.append(ok)
            P.add("sp", lambda e: dram_writer(e, stg[k]), reads=[("stg", k)], writes=[ok], dma=f"stg{k}", n_dma=ndma)

        P.stage(1)
        load_rows_T(sconv, 48, lambda c: sc_hist[:, c, :], [("sc_hist", c) for c in range(4)])
        load_rows_T(slru, 16, lambda c: sl_hist[:, c, :], [("sl_hist", c) for c in range(4)])
        load_rows_T(spool[0:128, :], 128, lambda c: sp_hist[:, c, 0:128], [("sp_hist", c) for c in range(4)])
        load_rows_T(spool[128:240, :], 112, lambda c: sp_hist[:, c, 128:240], [("sp_hist", c) for c in range(4)])
        ok = ("o", cnt["odma"]); cnt["odma"] += 1; out_keys.append(ok)
        P.add("sp", lambda e: e.dma_start(out=spool_o.rearrange("(s r) c -> s r c", r=15)[:, 0:7, :],
                                          in_=spool.rearrange("(s r) c -> s r c", r=15)[:, 8:15, :]),
              writes=[ok], dma="hbm2hbm")

        def norm(gi, cols, Wd, xkeys, hkeys, inplace=False, src=None, dst=None):
            src = xT if src is None else src
            dst = hT if dst is None else dst
            P.defw = Wd
            c0 = cols
            for c in range(8):
                P.add("act", lambda e, c=c: e.activation(dst[:, c, c0:c0 + Wd] if not inplace else hT[:, c, c0:c0 + Wd],
                                                         src[:, c, c0:c0 + Wd], AF.Square),
                      reads=[xkeys[c]], writes=[hkeys[c]])
            sqv = hT if inplace else dst
            b = 6
            mmgroup(psb[b][:, 0:Wd], [(onesm_b[:], sqv[:, c, c0:c0 + Wd]) for c in range(8)],
                    reads=list(hkeys) + ["onesm_b"], bnk=b)
            r = cnt["rstd"] % 2
            cnt["rstd"] += 1
            P.add("act", lambda e: e.activation(rstd[r][:, 0:Wd], psb[b][:, 0:Wd], AF.Ln, bias=epsc[:, 0:1]),
                  reads=[PS(b), "epsc"], writes=[("rstd", r)], cost=1.5 + Wd / 1400.0)
            P.add("act", lambda e: e.activation(rstd[r][:, 0:Wd], rstd[r][:, 0:Wd], AF.Exp, scale=-0.5),
                  reads=[("rstd", r)], writes=[("rstd", r)], cost=1.5 + Wd / 1400.0)
            for c in range(8):
                if inplace:
                    P.add("dve", lambda e, c=c: e.scalar_tensor_tensor(src[:, c, c0:c0 + Wd], src[:, c, c0:c0 + Wd],
                                                                       gains[:, gi, c:c + 1], rstd[r][:, 0:Wd], ALU.mult, ALU.mult),
                          reads=[xkeys[c], ("rstd", r), "gains"], writes=[xkeys[c]])
                else:
                    P.add("dve", lambda e, c=c: e.scalar_tensor_tensor(dst[:, c, c0:c0 + Wd], src[:, c, c0:c0 + Wd],
                                                                       gains[:, gi, c:c + 1], rstd[r][:, 0:Wd], ALU.mult, ALU.mult),
                          reads=[xkeys[c], ("rstd", r), "gains", hkeys[c]], writes=[hkeys[c]])

        def xk(sbi):
            return [("x", c, sbi) for c in range(8)]

        def hk(sbi):
            return [("h", c, sbi) for c in range(8)]

        P.stage(0)
        def mem_phase():
            memT = ra_f32(0, 2048).rearrange("p (c m) -> p c m", c=8)
            mT = ra_bf16(2048, 2048).rearrange("p (c m) -> p c m", c=8)
            kst = ra_f32(3072, 2048).rearrange("p (a c) -> p a c", a=2)
            vst = ra_f32(5120, 2048).rearrange("p (a c) -> p a c", a=2)
            mstage = ra_f32(7168, 2048).rearrange("p (a c) -> p a c", a=2)
            P.add("sp", lambda e: e.dma_start(out=mstage, in_=mem.rearrange("(a p) c -> p a c", p=128)),
                  reads=["RA"], writes=["mstage"], dma="mstage")
            for mc in range(2):
                for half in range(2):
                    b = bank("mm")

                    def emit(e, mc=mc, half=half, b=b):
                        last = None
                        for cc in range(4):
                            c = half * 4 + cc
                            last = e.transpose(psb[b][:, cc * 128:(cc + 1) * 128], mstage[:, mc, c * 128:(c + 1) * 128], ident_f[:])
                        return last
                    P.add("pe", emit, reads=["mstage", "ident_f", "RA"], writes=[PS(b)])
                    P.add("act", lambda e, mc=mc, half=half, b=b: e.copy(
                        memT[:, half * 4:half * 4 + 4, mc * 128:(mc + 1) * 128],
                        psb[b][:, :].rearrange("p (c m) -> p c m", c=4)),
                        reads=[PS(b), "RA"], writes=[("memTc", half * 4 + cc) for cc in range(4)])
            mkeys_in = [("memTc", c) for c in range(8)]
            mTk = [("mT", c) for c in range(8)]
            def mem_norm():
                for c in range(8):
                    P.add("act", lambda e, c=c: e.activation(mT[:, c, :], memT[:, c, :], AF.Square),
                          reads=[mkeys_in[c], "RA"], writes=[mTk[c]])
                b = 6
                mmgroup(psb[b][:, 0:256], [(onesm_b[:], mT[:, c, :]) for c in range(8)], reads=mTk + ["onesm_b", "RA"], bnk=b)
                P.add("act", lambda e: e.activation(rstd[0][:, 0:256], psb[b][:, 0:256], AF.Sqrt, bias=epsc[:, 0:1]),
                      reads=[PS(b), "epsc"], writes=[("rstd", 0)])
                P.add("dve", lambda e: e.reciprocal(rstd[0][:, 0:256], rstd[0][:, 0:256]), reads=[("rstd", 0)], writes=[("rstd", 0)])
                for c in range(8):
                    P.add("dve", lambda e, c=c: e.scalar_tensor_tensor(mT[:, c, :], memT[:, c, :], gains[:, 3, c:c + 1],
                                                                       rstd[0][:, 0:256], ALU.mult, ALU.mult),
                          reads=[mkeys_in[c], ("rstd", 0), "gains", mTk[c], "RA"], writes=[mTk[c]])
            mem_norm()
            for j in range(4):
                Ks, kkey = wslab("wk", j)
                for mc in range(2):
                    b = bank("mm")
                    mmgroup(psb[b][:, 0:256], [(mT[:, kc, mc * 128:(mc + 1) * 128], Ks[:, kc, :]) for kc in range(8)],
                            reads=mTk + [kkey, "RA"], bnk=b)
                    P.add("act", lambda e, b=b, mc=mc, j=j: e.copy(kst[:, mc, j * 256:(j + 1) * 256], psb[b][:, 0:256]),
                          reads=[PS(b), "RA"], writes=[("kst", mc, j)])
                for dc in range(2):
                    b = bank("mm")
                    mmgroup(psb[b][:, 0:256], [(Ks[:, kc, dc * 128:(dc + 1) * 128], mT[:, kc, :]) for kc in range(8)],
                            reads=mTk + [kkey, "RA"], bnk=b)
                    P.add("dve", lambda e, b=b, dc=dc, j=j: e.tensor_copy(kT[:, 2 * j + dc, :], psb[b][:, 0:256]),
                          reads=[PS(b)], writes=[("kT", 2 * j + dc)])
            for j in range(4):
                Vs, vkey = wslab("wv", j)
                for mc in range(2):
                    b = bank("mm")
                    mmgroup(psb[b][:, 0:256], [(mT[:, kc, mc * 128:(mc + 1) * 128], Vs[:, kc, :]) for kc in range(8)],
                            reads=mTk + [vkey, "RA"], bnk=b)
                    P.add("act", lambda e, b=b, mc=mc, j=j: e.copy(vst[:, mc, j * 256:(j + 1) * 256], psb[b][:, 0:256]),
                          reads=[PS(b), "RA"], writes=[("vst", mc, j)])
                    P.add("dve", lambda e, b=b, mc=mc, j=j: e.tensor_copy(Vb[:, mc, j * 256:(j + 1) * 256], psb[b][:, 0:256]),
                          reads=[PS(b)], writes=[("Vb", mc, j)])
            for nm, st_, dst_ in (("kst", kst, pmk_o), ("vst", vst, pmv_o)):
                ok = ("o", cnt["odma"]); cnt["odma"] += 1; out_keys.append(ok)
                P.add("sp", lambda e, st_=st_, dst_=dst_: e.dma_start(out=dst_.rearrange("(a p) c -> p a c", p=128), in_=st_),
                      reads=[(nm, mc, j) for mc in range(2) for j in range(4)] + ["RA"], writes=[ok], dma="o_" + nm)
        kTkeys = [("kT", i) for i in range(8)]
        Vbkeys = [("Vb", mc, j) for mc in range(2) for j in range(4)]

        P.stage(0)
        a_bf = ra_bf16(0, 22 * TM).rearrange("p (f t) -> p f t", f=22)

        def ffn(pfx, sbs, sb_outer=False):
            ra_barrier(rb=False)
            for j in range(11):
                G, gk = wslab(pfx + "g", j)
                U, uk = wslab(pfx + "u", j)
                for (sbi, c0, Wd) in sbs:
                    P.defw = Wd
                    for half in range(2):
                        f = 2 * j + half
                        bg = bank("mm"); bu = bank("mm")
                        mmgroup(psb[bg][:, 0:Wd], [(G[:, k, half * 128:(half + 1) * 128], hT[:, k, c0:c0 + Wd]) for k in range(8)],
                                reads=[gk] + hk(sbi), bnk=bg)
                        mmgroup(psb[bu][:, 0:Wd], [(U[:, k, half * 128:(half + 1) * 128], hT[:, k, c0:c0 + Wd]) for k in range(8)],
                                reads=[uk] + hk(sbi), bnk=bu)
                        t = cnt["sgt"] % 2
                        cnt["sgt"] += 1
                        P.add("act", lambda e, t=t, bg=bg, Wd=Wd: e.activation(sgt[t][:, 0:Wd], psb[bg][:, 0:Wd], AF.Silu),
                              reads=[PS(bg)], writes=[("sgt", t)])
                        P.add("dve", lambda e, t=t, bu=bu, Wd=Wd, f=f, c0=c0: e.tensor_tensor(
                            a_bf[:, f, c0:c0 + Wd], sgt[t][:, 0:Wd], psb[bu][:, 0:Wd], ALU.mult),
                            reads=[("sgt", t), PS(bu), "RA"], writes=[("a", f, sbi)])
            dloop = [(d, [sb_]) for sb_ in sbs for d in range(8)] if sb_outer else [(d, sbs) for d in range(8)]
            for (d, sbl) in dloop:
                v = W[pfx + "d"].rearrange("(fc p) c -> p fc c", p=128)
                Dd, dk = ringload(v[:, :, d * 128:(d + 1) * 128], 22, 128)
                for (sbi, c0, Wd) in sbl:
                    P.defw = Wd
                    bo = bank("out")
                    mmgroup(psb[bo][:, 0:Wd], [(Dd[:, f, :], a_bf[:, f, c0:c0 + Wd]) for f in range(22)],
                            reads=[dk, "RA"] + [("a", f, sbi) for f in range(22)], bnk=bo)
                    P.add("dve", lambda e, bo=bo, d=d, c0=c0, Wd=Wd: e.scalar_tensor_tensor(
                        xT[:, d, c0:c0 + Wd], psb[bo][:, 0:Wd], 0.5, xT[:, d, c0:c0 + Wd], ALU.mult, ALU.add),
                        reads=[PS(bo), ("x", d, sbi)], writes=[("x", d, sbi)])

        def mixer_sb(sbi, c0, Wd, nseq, L, first_prompt, pass_idx, fill=None):
            if fill is None:
                fill = lambda n: None
            P.defw = Wd
            ra_barrier(phase="mixer", write_rb=False)
            Hc, Hp = 3, 15
            def uLv(c):
                return ra_f32(12672 + c * 528, nseq * (Hc + L)).rearrange("p (s t) -> p s t", s=nseq)
            def uPv(g):
                return ra_f32(14784 + g * 528, nseq * (Hp + L)).rearrange("p (s t) -> p s t", s=nseq)
            def f2(base, c):
                return ra_f32(base + c * 512, Wd)
            def f3(base, c):
                return f2(base, c).rearrange("p (s t) -> p s t", s=nseq)
            GA, CV, TA, TB, MM = (16896 if sbi % 2 == 0 else 18944), 0, 2048, 4096, 6144
            def cvbv(c):
                return ra_bf16(8192 + c * 256, 512)[:, 0:Wd]
            def psv(i):
                return ra_f32(9216 + i * 528, nseq * (Hp + L)).rearrange("p (s t) -> p s t", s=nseq)
            def dlv(g):
                return ra_bf16(10272 + g * 256, 512)[:, 0:Wd]
            R4 = range(4)

            norm(1, c0, Wd, xk(sbi), hk(sbi))
            fill(1)
            for c in R4:
                if nseq == 1:
                    P.add("dve", lambda e, c=c: e.tensor_copy(uLv(c)[:, 0, 0:Hc], cL[:, c, :]),
                          reads=["RB", ("cL", c), ("uL", c)], writes=[("uLh", c)])
                    P.add("dve", lambda e, c=c: e.tensor_copy(uPv(c)[:, 0, 0:Hp], cP[:, c, :]),
                          reads=["RB", ("cP", c), ("uP", c)], writes=[("uPh", c)])
                else:
                    P.add("dve", lambda e, c=c: e.tensor_copy(uLv(c)[:, :, 0:Hc], sc_hist[:, c, :].rearrange("p (s r) -> p s r", s=16)),
                          reads=["RB", ("sc_hist", c), ("uL", c)], writes=[("uLh", c)])
                    P.add("dve", lambda e, c=c: e.tensor_copy(uPv(c)[:, :, 0:Hp], sp_hist[:, c, :].rearrange("p (s r) -> p s r", s=16)),
                          reads=["RB", ("sp_hist", c), ("uP", c)], writes=[("uPh", c)])
            for j in (0, 1, 4, 5, 2, 3):
                S, skey = wslab("win", j)
                for half in range(2):
                    cc = 2 * j + half
                    b = bank("mm")
                    mmgroup(psb[b][:, 0:Wd], [(S[:, k, half * 128:(half + 1) * 128], hT[:, k, c0:c0 + Wd]) for k in range(8)],
                            reads=[skey] + hk(sbi), bnk=b)
                    src3 = psb[b][:, 0:Wd].rearrange("p (s t) -> p s t", s=nseq)
                    if cc < 4:
                        P.add("act", lambda e, cc=cc, src3=src3: e.copy(uLv(cc)[:, :, Hc:Hc + L], src3),
                              reads=[PS(b), "RB", ("uLh", cc)], writes=[("uL", cc)])
                    elif cc < 8:
                        P.add("act", lambda e, cc=cc, b=b: e.activation(f2(GA, cc - 4), psb[b][:, 0:Wd], AF.Gelu_apprx_tanh),
                              reads=[PS(b), "RB"], writes=[("ga", sbi % 2, cc - 4)])
                    else:
                        P.add("act", lambda e, cc=cc, src3=src3: e.copy(uPv(cc - 8)[:, :, Hp:Hp + L], src3),
                              reads=[PS(b), "RB", ("uPh", cc - 8)], writes=[("uP", cc - 8)])
                if j == 1:
                    for c in R4:
                        P.add("act", lambda e, c=c: e.activation(f3(CV, c), uLv(c)[:, :, 0:L], AF.Identity,
                                                                 scale=convw[:, c, 0:1], bias=vec[:, 0, c:c + 1]),
                              reads=["RA", ("uL", c), ("uLh", c), "convw", "vec"], writes=[("cv", c)])
                    for k in range(1, 4):
                        for c in R4:
                            P.add("dve", lambda e, c=c, k=k: e.scalar_tensor_tensor(f3(CV, c), uLv(c)[:, :, k:k + L], convw[:, c, k:k + 1],
                                                                                   f3(CV, c), ALU.mult, ALU.add),
                                  reads=["RA", ("uL", c), ("uLh", c), "convw", ("cv", c)], writes=[("cv", c)])
                    for c in R4:
                        P.add("act", lambda e, c=c: e.copy(cvbv(c), f2(CV, c)), reads=["RA", ("cv", c)], writes=[("cvb", c)])
                    if nseq == 1:
                        for c in R4:
                            P.add("dve", lambda e, c=c: e.tensor_copy(cL[:, c, :], uLv(c)[:, 0, L:L + Hc]),
                                  reads=["RB", ("uL", c)], writes=[("cL", c)])
                    fill(1)
                if j == 5:
                    if nseq == 1:
                        for c in R4:
                            P.add("dve", lambda e, c=c: e.tensor_copy(cP[:, c, :], uPv(c)[:, 0, L:L + Hp]),
                                  reads=["RB", ("uP", c)], writes=[("cP", c)])
                    for g, win in enumerate((2, 4, 8, 16)):
                        uP = uPv(g)
                        TT = Hp + L
                        cur = uP
                        curkey = [("uP", g), ("uPh", g)]
                        for lev in range(g + 1):
                            sh = 1 << lev
                            lo = (1 << (lev + 1)) - 1
                            dstb = psv(lev % 2)
                            P.add("dve", lambda e, dstb=dstb, cur=cur, lo=lo, sh=sh, TT=TT: e.tensor_tensor(
                                dstb[:, :, lo:TT], cur[:, :, lo:TT], cur[:, :, lo - sh:TT - sh], ALU.add),
                                reads=["RA"] + curkey, writes=[("psv", lev % 2)])
                            cur = dstb
                            curkey = [("psv", lev % 2)]
                        dl = dlv(g)
                        dl3 = dl.rearrange("p (s t) -> p s t", s=nseq)
                        P.add("dve", lambda e, dl3=dl3, cur=cur, uP=uP, win=win: e.scalar_tensor_tensor(
                            dl3, cur[:, :, Hp:Hp + L], 1.0 / win, uP[:, :, Hp:Hp + L], ALU.mult, ALU.subtract),
                            reads=["RA", ("uP", g)] + curkey, writes=[("dl", g)])
                        if first_prompt:
                            t0 = f2(MM, 0)
                            P.add("dve", lambda e, t0=t0, cur=cur, g=g: e.tensor_tensor(t0[:, 0:15], cur[:, 0, Hp:Hp + 15], invc[:, g, 0:15], ALU.mult),
                                  reads=["RA", "invc"] + curkey, writes=[("mm", 0)])
                            P.add("dve", lambda e, t0=t0, dl=dl, uP=uP, g=g: e.tensor_tensor(dl[:, 0:15], t0[:, 0:15], uP[:, 0, Hp:Hp + 15], ALU.subtract),
                                  reads=["RA", ("mm", 0), ("uP", g), ("dl", g)], writes=[("dl", g)])
                    fill(1)
            for g in R4:
                bq = bank("out")
                mmgroup(psb[bq][:, 0:Wd], [(poolw[:, g, :], dlv(g))], reads=["RA", ("dl", g), "poolw"], bnk=bq)
                P.add("act", lambda e, bq=bq, g=g: e.activation(hT[:, 4 + g, c0:c0 + Wd], psb[bq][:, 0:Wd], AF.Copy, scale=vec[:, 4, g:g + 1]),
                      reads=[PS(bq), "vec"], writes=[("h", 4 + g, sbi)])
            for c in R4:
                br = bank("mm"); bi = bank("mm")
                mmgroup(psb[br][:, 0:Wd], [(wabd[:, c, :], cvbv(c))], reads=["RA", ("cvb", c), "wabd"], bnk=br)
                mmgroup(psb[bi][:, 0:Wd], [(wxbd[:, c, :], cvbv(c))], reads=["RA", ("cvb", c), "wxbd"], bnk=bi)
                P.add("act", lambda e, br=br, c=c: e.activation(f2(TA, c), psb[br][:, 0:Wd], AF.Tanh, scale=0.5, bias=hvec[:, 0, c:c + 1]),
                      reads=["RA", PS(br), "hvec"], writes=[("ta", c)])
                P.add("act", lambda e, bi=bi, c=c: e.activation(f2(TB, c), psb[bi][:, 0:Wd], AF.Tanh, scale=0.5, bias=hvec[:, 1, c:c + 1]),
                      reads=["RA", PS(bi), "hvec"], writes=[("tb", c)])
            fill(1)
            for c in R4:
                P.add("act", lambda e, c=c: e.activation(f2(TA, c), f2(TA, c), AF.Exp, scale=hvec[:, 2, c:c + 1], bias=hvec[:, 2, c:c + 1]),
                      reads=["RA", ("ta", c), "hvec"], writes=[("ta", c)])
            for c in R4:
                P.add("act", lambda e, c=c: e.activation(f2(MM, c), f2(TA, c), AF.Square), reads=["RA", ("ta", c)], writes=[("mm", c)])
            for c in R4:
                P.add("dve", lambda e, c=c: e.scalar_tensor_tensor(f2(TB, c), f2(TB, c), 1.0, f2(CV, c), ALU.add, ALU.mult),
                      reads=["RA", ("tb", c), ("cv", c)], writes=[("tb", c)])
            for c in R4:
                P.add("dve", lambda e, c=c: e.tensor_scalar(f2(MM, c), f2(MM, c), 1.0, None, ALU.min), reads=["RA", ("mm", c)], writes=[("mm", c)])
            for c in R4:
                P.add("act", lambda e, c=c: e.activation(f2(MM, c), f2(MM, c), AF.Sqrt, scale=-1.0, bias=onec[:, 0:1]),
                      reads=["RA", ("mm", c), "onec"], writes=[("mm", c)])
            fill(1)
            for c in R4:
                P.add("dve", lambda e, c=c: e.scalar_tensor_tensor(f2(TB, c), f2(TB, c), 0.5, f2(MM, c), ALU.mult, ALU.mult),
                      reads=["RA", ("tb", c), ("mm", c)], writes=[("tb", c)])
            for c in R4:
                if nseq == 1:
                    P.add("dve", lambda e, c=c: e.tensor_tensor_scan(f2(CV, c), f2(TA, c), f2(TB, c), hstate[:, c:c + 1], ALU.mult, ALU.add),
                          reads=["RA", ("ta", c), ("tb", c), ("cv", c), ("hstate", c)], writes=[("cv", c)], cost=0.2 + Wd / 400.0)
                    P.add("dve", lambda e, c=c: e.tensor_copy(hstate[:, c:c + 1], f2(CV, c)[:, Wd - 1:Wd]),
                          reads=["RA", ("cv", c)], writes=[("hstate", c)])
                else:
                    a3, b3, m3, cv3 = f3(TA, c), f3(TB, c), f3(MM, c), f3(CV, c)
                    P.add("dve", lambda e, m3=m3, a3=a3, c=c: e.tensor_tensor(m3[:, :, 0:1], a3[:, :, 0:1], sl_hist[:, c, :].rearrange("p (s o) -> p s o", o=1), ALU.mult),
                          reads=["RA", ("ta", c), ("mm", c), ("tb", c), ("sl_hist", c)], writes=[("mm", c)])
                    P.add("dve", lambda e, m3=m3, b3=b3: e.tensor_tensor(b3[:, :, 0:1], b3[:, :, 0:1], m3[:, :, 0:1], ALU.add),
                          reads=["RA", ("tb", c), ("mm", c)], writes=[("tb", c)])
                    P.add("dve", lambda e, a3=a3: e.memset(a3[:, :, 0:1], 0.0), reads=["RA", ("ta", c), ("mm", c)], writes=[("ta", c)])
                    P.add("dve", lambda e, c=c: e.tensor_tensor_scan(f2(CV, c), f2(TA, c), f2(TB, c), 0.0, ALU.mult, ALU.add),
                          reads=["RA", ("ta", c), ("tb", c), ("cv", c)], writes=[("cv", c)], cost=0.2 + Wd / 400.0)
                    P.add("dve", lambda e, cv3=cv3, c=c: e.tensor_copy(sl_hist[:, c, :].rearrange("p (s o) -> p s o", o=1), cv3[:, :, L - 1:L]),
                          reads=["RA", ("cv", c), ("sl_hist", c)], writes=[("sl_out", c)])
                P.add("dve", lambda e, c=c: e.tensor_tensor(hT[:, c, c0:c0 + Wd], f2(GA, c), f2(CV, c), ALU.mult),
                      reads=["RA", ("ga", sbi % 2, c), ("cv", c)], writes=[("h", c, sbi)])
            fill(1)
            if nseq > 1:
                store_rows_T(lambda c: uLv(c)[:, :, L:L + Hc], [("uL", c) for c in range(4)] + ["RA"], 48,
                             lambda e, s: e.dma_start(out=sconv_o, in_=s[0:48, 0:512]), s3=16)
                store_rows_T(lambda c: sl_hist[:, c, :], [("sl_out", c) for c in range(4)], 16,
                             lambda e, s: e.dma_start(out=slru_o, in_=s[0:16, 0:512]))
                sp3 = spool_o.rearrange("(s r) c -> s r c", r=15)
                store_rows_T(lambda c: uPv(c)[:, :, Hp:Hp + L], [("uP", c) for c in range(4)] + ["RA"], 128,
                             lambda e, s: [e.dma_start(out=sp3[q, 7:15, :], in_=s[8 * q:8 * q + 8, 0:512]) for q in range(16)], ndma=16, s3=16)
            elif pass_idx == 1 and c0 + Wd == 1024:
                store_rows_T(lambda c: cL[:, c, :], [("cL", c) for c in range(4)], 3,
                             lambda e, s: e.dma_start(out=pconv_o, in_=s[0:3, 0:512]))
                store_rows_T(lambda c: cP[:, c, :], [("cP", c) for c in range(4)], 15,
                             lambda e, s: e.dma_start(out=ppool_o, in_=s[0:15, 0:512]))
                store_rows_T(lambda c: hstate[:, c:c + 1], [("hstate", c) for c in range(4)], 1,
                             lambda e, s: e.dma_start(out=plru_o, in_=s[0:1, 0:512]))
            for j in range(4):
                S, skey = wslab("wout", j)
                for half in range(2):
                    d = 2 * j + half
                    bo = bank("out")
                    mmgroup(psb[bo][:, 0:Wd], [(S[:, k, half * 128:(half + 1) * 128], hT[:, k, c0:c0 + Wd]) for k in range(8)],
                            reads=[skey] + hk(sbi), bnk=bo)
                    P.add("dve", lambda e, bo=bo, d=d: e.tensor_tensor(xT[:, d, c0:c0 + Wd], xT[:, d, c0:c0 + Wd], psb[bo][:, 0:Wd], ALU.add),
                          reads=[PS(bo), ("x", d, sbi)], writes=[("x", d, sbi)])
                fill(1)

        def xattn(sbs, with_sample, with_mem=False):
            ra_barrier()
            if with_mem:
                mem_phase()
            Kst = [ra_bf16(7168 + i * 1024, 2048).rearrange("p (a c) -> p a c", a=2) for i in range(4)]
            Vst = [ra_bf16(11264 + i * 1024, 2048).rearrange("p (a c) -> p a c", a=2) for i in range(4)]
            kTs = [ra_bf16(15360 + i * 1024, 2048).rearrange("p (a m) -> p a m", a=8) for i in range(4)]
            q = ra_bf16(0, 8 * TM).rearrange("p (c t) -> p c t", c=8)
            def pTv(i):
                return ra_bf16(4608 + i * 512, 1024).rearrange("p (a t) -> p a t", a=2)
            def rdv(i):
                return ra_f32(5632 + i * 512, 512)
            for (sbi, c0, Wd) in sbs:
                P.defw = Wd
                norm(2, c0, Wd, xk(sbi), hk(sbi))
            for j in range(4):
                S, skey = wslab("wq", j)
                for (sbi, c0, Wd) in sbs:
                    P.defw = Wd
                    for half in range(2):
                        d = 2 * j + half
                        b = bank("mm")
                        mmgroup(psb[b][:, 0:Wd], [(S[:, k, half * 128:(half + 1) * 128], hT[:, k, c0:c0 + Wd]) for k in range(8)],
                                reads=[skey] + hk(sbi), bnk=b)
                        P.add("act", lambda e, b=b, d=d, c0=c0, Wd=Wd: e.copy(q[:, d, c0:c0 + Wd], psb[b][:, 0:Wd]),
                              reads=[PS(b), "RA"], writes=[("q", d, sbi)])
            okeys = hk
            pcount = 0
            for (sbi, c0, Wd) in sbs:
                P.defw = Wd
                if Wd != 512:
                    continue
                for h in range(4):
                    pi = pcount % 2
                    pcount += 1
                    pT = pTv(pi)
                    for mc in range(2):
                        b = bank("mm")
                        mmgroup(psb[b][:, 0:Wd], [(kT[:, 2 * h + dc, mc * 128:(mc + 1) * 128], q[:, 2 * h + dc, c0:c0 + Wd]) for dc in range(2)],
                                reads=["RA", ("q", 2 * h, sbi), ("q", 2 * h + 1, sbi)] + kTkeys, bnk=b)
                        P.add("act", lambda e, b=b, pT=pT, mc=mc: e.activation(pT[:, mc, :], psb[b][:, 0:512], AF.Exp, scale=0.0625),
                              reads=[PS(b), "RA"], writes=[("pT", pi, mc)])
                    bd = 6
                    mmgroup(psb[bd][:, 0:Wd], [(ones_b[:], pT[:, mc, :]) for mc in range(2)],
                            reads=["RA", ("pT", pi, 0), ("pT", pi, 1), "ones_b"], bnk=bd)
                    rd = rdv(pi)
                    P.add("dve", lambda e, rd=rd, bd=bd: e.reciprocal(rd, psb[bd][:, 0:512]), reads=[PS(bd), "RA"], writes=[("rd", pi)], cost=3.4)
                    for dc in range(2):
                        bo = bank("out")
                        mmgroup(psb[bo][:, 0:Wd], [(Vb[:, mc, h * 256 + dc * 128:h * 256 + (dc + 1) * 128], pT[:, mc, :]) for mc in range(2)],
                                reads=["RA", ("pT", pi, 0), ("pT", pi, 1)] + Vbkeys, bnk=bo)
                        P.add("dve", lambda e, bo=bo, rd=rd, h=h, dc=dc, c0=c0, Wd=Wd: e.tensor_tensor(
                            hT[:, 2 * h + dc, c0:c0 + Wd], psb[bo][:, 0:Wd], rd, ALU.mult),
                            reads=[PS(bo), ("rd", pi), "RA"], writes=[("h", 2 * h + dc, sbi)])
            if with_sample:
                sbi, c0, Wd = sbs[-1]
                for s in range(16):
                    sl = s % 4
                    sp_ = s % 2
                    P.add("pool", lambda e, s=s, sl=sl: e.dma_start(out=Kst[sl], in_=ck[s].rearrange("(a p) c -> p a c", p=128)),
                          reads=["RA"], writes=[("Kst", sl)], dma=f"Kst{sl}", nbytes=1048576)
                    P.add("pool", lambda e, s=s, sl=sl: e.dma_start(out=Vst[sl], in_=cvv[s].rearrange("(a p) c -> p a c", p=128)),
                          reads=["RA"], writes=[("Vst", sl)], dma=f"Vst{sl}", nbytes=1048576)
                    for half in range(2):
                        b = bank("mm")
                        pb = psb[b][:, :].bitcast(BF16)

                        def emit(e, half=half, pb=pb, sl=sl):
                            last = None
                            for hh in range(2):
                                for dc in range(2):
                                    for mc in range(2):
                                        h = 2 * half + hh
                                        col = ((hh * 2 + dc) * 256 + mc * 128)
                                        last = e.transpose(pb[:, col:col + 128],
                                                           Kst[sl][:, mc, h * 256 + dc * 128:h * 256 + (dc + 1) * 128], ident_b[:])
                            return last
                        P.add("pe", emit, reads=[("Kst", sl), "ident_b", "RA"], writes=[PS(b)], cost=1.0)
                        P.add("act", lambda e, half=half, pb=pb, sl=sl: e.copy(
                            kTs[sl][:, 4 * half:4 * half + 4, :], pb.rearrange("p (a m) -> p a m", a=4)),
                            reads=[PS(b), "RA"], writes=[("kTs", sl, half)])
                    pTs = ra_bf16(6656 + sp_ * 32, 64).rearrange("p (a t) -> p a t", a=8)
                    rds = ra_f32(6720, 32).rearrange("p (h o t) -> p h o t", h=4, o=1)
                    bs = bank("out")

                    def emit_sc(e, s=s, sl=sl, bs=bs):
                        last = None
                        for h in range(4):
                            for mc in range(2):
                                for dc in range(2):
                                    last = e.matmul(psb[bs][:, (h * 2 + mc) * 8:(h * 2 + mc) * 8 + 8],
                                                    kTs[sl][:, 2 * h + dc, mc * 128:(mc + 1) * 128],
                                                    q[:, 2 * h + dc, c0 + 8 * s:c0 + 8 * s + 8], start=(dc == 0), stop=(dc == 1))
                        return last
                    P.add("pe", emit_sc, cost=1.2, reads=["RA", ("kTs", sl, 0), ("kTs", sl, 1)] + [("q", d, sbi) for d in range(8)], writes=[PS(bs)])
                    P.add("act", lambda e, bs=bs, pTs=pTs: e.activation(pTs, psb[bs][:, 0:64].rearrange("p (a t) -> p a t", a=8), AF.Exp, scale=0.0625),
                          reads=[PS(bs), "RA"], writes=[("pTs", sp_)])
                    bd = 6

                    def emit_den(e, pTs=pTs, bd=bd):
                        last = None
                        for h in range(4):
                            for mc in range(2):
                                last = e.matmul(psb[bd][:, h * 8:h * 8 + 8], ones_b[:], pTs[:, h * 2 + mc, :], start=(mc == 0), stop=(mc == 1))
                        return last
                    P.add("pe", emit_den, cost=0.6, reads=["RA", ("pTs", sp_), "ones_b"], writes=[PS(bd)])
                    P.add("dve", lambda e, bd=bd, rds=rds: e.reciprocal(rds, psb[bd][:, 0:32].rearrange("p (h o t) -> p h o t", h=4, o=1)),
                          reads=[PS(bd), "RA"], writes=["rds"])
                    bp = bank("out")

                    def emit_pv(e, pTs=pTs, bp=bp, sl=sl):
                        last = None
                        for h in range(4):
                            for dc in range(2):
                                for mc in range(2):
                                    last = e.matmul(psb[bp][:, (h * 2 + dc) * 8:(h * 2 + dc) * 8 + 8],
                                                    Vst[sl][:, mc, h * 256 + dc * 128:h * 256 + (dc + 1) * 128],
                                                    pTs[:, h * 2 + mc, :], start=(mc == 0), stop=(mc == 1))
                        return last
                    P.add("pe", emit_pv, cost=1.2, reads=["RA", ("pTs", sp_), ("Vst", sl)], writes=[PS(bp)])
                    P.add("dve", lambda e, bp=bp, rds=rds, s=s: e.tensor_tensor(
                        hT[:, :, c0 + 8 * s:c0 + 8 * s + 8].rearrange("p (h d) t -> p h d t", h=4),
                        psb[bp][:, 0:64].rearrange("p (h d t) -> p h d t", h=4, d=2),
                        rds.to_broadcast([128, 4, 2, 8]), ALU.mult),
                        reads=[PS(bp), "rds", "RA"], writes=hk(sbi))
            for j in range(4):
                S, skey = wslab("wo", j)
                for (sbi, c0, Wd) in sbs:
                    P.defw = Wd
                    for half in range(2):
                        d = 2 * j + half
                        bo = bank("out")
                        mmgroup(psb[bo][:, 0:Wd], [(S[:, k, half * 128:(half + 1) * 128], hT[:, k, c0:c0 + Wd]) for k in range(8)],
                                reads=[skey] + okeys(sbi), bnk=bo)
                        P.add("dve", lambda e, bo=bo, d=d, c0=c0, Wd=Wd: e.tensor_tensor(xT[:, d, c0:c0 + Wd], xT[:, d, c0:c0 + Wd], psb[bo][:, 0:Wd], ALU.add),
                              reads=[PS(bo), ("x", d, sbi)], writes=[("x", d, sbi)])

        def load_dma(src):
            k = cnt["in"] % 4
            cnt["in"] += 1
            P.add("sp", lambda e, k=k, src=src: e.dma_start(out=rb_slot(k), in_=src), reads=["RB"], writes=[("rbs", k)], dma=f"rbs{k}", nbytes=524288)
            return k

        def consume_tile(k, col, sbi):
            for half in range(2):
                b = bank("mm")

                def emit(e, k=k, half=half, b=b):
                    last = None
                    for cc in range(4):
                        c = half * 4 + cc
                        last = e.transpose(psb[b][:, cc * 128:(cc + 1) * 128], rb_slot(k)[:, c * 128:(c + 1) * 128], ident_f[:])
                    return last
                P.add("pe", emit, reads=[("rbs", k), "ident_f", "RB"], writes=[PS(b)], cost=0.9)
                eng = "act" if half == 0 else "dve"
                fn = (lambda e, half=half, b=b, col=col: e.copy(
                    xT[:, half * 4:half * 4 + 4, col:col + 128], psb[b][:, :].rearrange("p (c m) -> p c m", c=4))) if half == 0 else \
                     (lambda e, half=half, b=b, col=col: e.tensor_copy(
                    xT[:, half * 4:half * 4 + 4, col:col + 128], psb[b][:, :].rearrange("p (c m) -> p c m", c=4)))
                P.add(eng, fn, reads=[PS(b)], writes=[("x", half * 4 + cc, sbi) for cc in range(4)])

        def out_tile(dst, col, sbi):
            k = 4 + cnt["outs"] % 2
            cnt["outs"] += 1
            for half in range(2):
                b = bank("mm")

                def emit(e, half=half, b=b, col=col):
                    last = None
                    for cc in range(4):
                        c = half * 4 + cc
                        last = e.transpose(psb[b][:, cc * 128:(cc + 1) * 128], xT[:, c, col:col + 128], ident_f[:])
                    return last
                P.add("pe", emit, reads=[("x", half * 4 + cc, sbi) for cc in range(4)] + ["ident_f"], writes=[PS(b)], cost=0.9)
                if half == 0:
                    P.add("act", lambda e, b=b, k=k: e.copy(rb_slot(k)[:, 0:512], psb[b][:, :]),
                          reads=[PS(b), "RB"], writes=[("rbs", k)])
                else:
                    P.add("dve", lambda e, b=b, k=k: e.tensor_copy(rb_slot(k)[:, 512:1024], psb[b][:, :]),
                          reads=[PS(b), "RB"], writes=[("rbs", k)])
            ok = ("o", cnt["odma"]); cnt["odma"] += 1; out_keys.append(ok)
            P.add("sp", lambda e, k=k, dst=dst: e.dma_start(out=dst, in_=rb_slot(k)),
                  reads=[("rbs", k), "RB"], writes=[ok], dma=f"rbs{k}", nbytes=524288)

        cnt["in"] = 0
        cnt["outs"] = 0

        def pass_tiles(pi):
            t = [(xp[(pi * 8 + i) * 128:(pi * 8 + i + 1) * 128, :], i * 128, i // 4) for i in range(8)]
            o = [(yp[(pi * 8 + i) * 128:(pi * 8 + i + 1) * 128, :], i * 128, i // 4) for i in range(8)]
            if pi == 1:
                t.append((xsm, 1024, 2))
                o.append((ys, 1024, 2))
            return t, o

        SBS = {0: [(0, 0, 512), (1, 512, 512)], 1: [(0, 0, 512), (1, 512, 512), (2, 1024, 128)]}

        P.stage(3)
        tiles, _ = pass_tiles(0)
        pend = [load_dma(t[0]) for t in tiles[:4]]
        nxt = 4
        for i, (src, col, sbi) in enumerate(tiles):
            consume_tile(pend.pop(0), col, sbi)
            if nxt < len(tiles):
                pend.append(load_dma(tiles[nxt][0]))
                nxt += 1
            if i % 4 == 3:
                P.stage(4)
                sb_ = SBS[0][sbi]
                norm(0, sb_[1], sb_[2], xk(sbi), hk(sbi))
                P.stage(3)

        P.stage(2)
        P.rb_with_ra = False
        mem_phase()
        P.rb_with_ra = True

        for pass_idx in range(2):
            sbs = SBS[pass_idx]
            tiles, otiles = pass_tiles(pass_idx)
            P.stage(4 + 6 * pass_idx)
            ffn("f1", sbs, sb_outer=True)
            P.stage(5 + 6 * pass_idx)
            mixer_sb(0, 0, 512, 1, 512, pass_idx == 0, pass_idx)
            mixer_sb(1, 512, 512, 1, 512, False, pass_idx)
            if pass_idx == 1:
                mixer_sb(2, 1024, 128, 16, 8, False, pass_idx)
            P.stage(6 + 6 * pass_idx)
            xattn(sbs, pass_idx == 1, with_mem=False)
            ntiles = []
            pend = []
            nxt = 0
            if pass_idx == 0:
                P.stage(9)
                rb_barrier()
                ntiles, _ = pass_tiles(1)
                pend = [load_dma(t[0]) for t in ntiles[:4]]
                nxt = 4
            else:
                rb_barrier()
            P.stage(7 + 6 * pass_idx)
            for (sbi, c0, Wd) in sbs:
                P.defw = Wd
                norm(4, c0, Wd, xk(sbi), hk(sbi))
            ffn("f2", sbs, sb_outer=True)
            for (sbi, c0, Wd) in sbs:
                P.defw = Wd
                P.stage(8 + 6 * pass_idx)
                norm(5, c0, Wd, xk(sbi), hk(sbi), inplace=True)
                for (dst, col, s2) in otiles:
                    if s2 == sbi:
                        out_tile(dst, col, sbi)
                if pass_idx == 0:
                    P.stage(9)
                    for (src, col, s2) in ntiles:
                        if s2 == sbi:
                            consume_tile(pend.pop(0), col, sbi)
                            if nxt < len(ntiles):
                                pend.append(load_dma(ntiles[nxt][0]))
                                nxt += 1
                    P.stage(10)
                    norm(0, c0, Wd, xk(sbi), hk(sbi))
            if pass_idx == 0:
                P.stage(9)
                for (src, col, s2) in ntiles:
                    if s2 == 2:
                        consume_tile(pend.pop(0), col, 2)
                P.stage(10)
                norm(0, 1024, 128, xk(2), hk(2))
        P.stage(0)
        P.add("sp", lambda e: e.nop(), reads=out_keys)
        P.build()
    return nc


_NC_CACHE = {}


def kernel(**inp):
    f = lambda a: np.ascontiguousarray(np.asarray(a, dtype=np.float32))
    if "nc" not in _NC_CACHE:
        _NC_CACHE["nc"] = build_nc()
    nc = _NC_CACHE["nc"]
    gains = np.stack([f(inp["ffn1_norm"])[0], f(inp["mix_norm"])[0], f(inp["xattn_norm"])[0],
                      f(inp["mem_norm"])[0], f(inp["ffn2_norm"])[0], f(inp["final_norm"])], axis=0)
    gains = np.ascontiguousarray(gains.reshape(6, 8, 128).transpose(2, 0, 1))
    vec = np.stack([f(inp["conv_b"])[0], f(inp["lru_ba"])[0], f(inp["lru_bx"])[0],
                    f(inp["lru_lambda"])[0], f(inp["pool_scale"])[0]], axis=0)
    vec = np.ascontiguousarray(vec.reshape(5, 4, 128).transpose(2, 0, 1))
    convw = np.ascontiguousarray(f(inp["conv_w"])[0].reshape(4, 4, 128).transpose(2, 1, 0))
    shared = {
        "f1g": f(inp["ffn1_w_gate"])[0], "f1u": f(inp["ffn1_w_up"])[0], "f1d": f(inp["ffn1_w_down"])[0],
        "win": f(inp["w_in"])[0], "wout": f(inp["w_out"])[0], "wq": f(inp["xattn_wq"])[0],
        "wk": f(inp["xattn_wk"])[0], "wv": f(inp["xattn_wv"])[0], "wo": f(inp["xattn_wo"])[0],
        "f2g": f(inp["ffn2_w_gate"])[0], "f2u": f(inp["ffn2_w_up"])[0], "f2d": f(inp["ffn2_w_down"])[0],
        "gains": gains, "vec512": vec, "convw": convw,
        "lwa": f(inp["lru_wa"])[0], "lwx": f(inp["lru_wx"])[0], "poolw": f(inp["pool_w"])[0],
    }
    xpr = f(inp["x_prompt"]); xsa = f(inp["x_sample"]); memp = f(inp["mem_prompt"])
    sc = f(inp["state_conv"])[0]; slr = f(inp["state_lru"])[0]; spl = f(inp["state_pool"])[0]
    ckk = f(inp["cache_mem_k"])[0]; cvv = f(inp["cache_mem_v"])[0]
    in_maps = []
    for c in range(NCORES):
        s0, s1 = 16 * c, 16 * c + 16
        m = dict(shared)
        m["xp"] = xpr[c]
        m["xsm"] = xsa[s0:s1].reshape(128, 1024)
        m["mem"] = memp[c]
        m["sconv"] = sc[s0:s1].reshape(48, 512)
        m["slru"] = slr[s0:s1]
        m["spool"] = spl[s0:s1].reshape(240, 512)
        m["ck"] = ckk[s0:s1].reshape(16, 256, 1024)
        m["cv"] = cvv[s0:s1].reshape(16, 256, 1024)
        in_maps.append(m)
    res = run_bass_kernel_spmd(nc, in_maps, core_ids=list(range(NCORES)))
    R = res.results
    y_prompt = np.stack([R[c]["yp"] for c in range(NCORES)], 0)
    y_sample = np.concatenate([R[c]["ys"].reshape(16, 8, 1024) for c in range(NCORES)], 0)
    p_conv = np.stack([R[c]["pconv"] for c in range(NCORES)], 0)[None]
    p_lru = np.stack([R[c]["plru"].reshape(512) for c in range(NCORES)], 0)[None]
    p_pool = np.stack([R[c]["ppool"] for c in range(NCORES)], 0)[None]
    p_mk = np.stack([R[c]["pmk"].reshape(256, 4, 256) for c in range(NCORES)], 0)[None]
    p_mv = np.stack([R[c]["pmv"].reshape(256, 4, 256) for c in range(NCORES)], 0)[None]
    s_conv = np.concatenate([R[c]["sconv_o"].reshape(16, 3, 512) for c in range(NCORES)], 0)[None]
    s_lru = np.concatenate([R[c]["slru_o"] for c in range(NCORES)], 0)[None]
    s_pool = np.concatenate([R[c]["spool_o"].reshape(16, 15, 512) for c in range(NCORES)], 0)[None]
    outs = (y_prompt, y_sample, p_conv, p_lru, p_pool, p_mk, p_mv, s_conv, s_lru, s_pool)
    return tuple(np.ascontiguousarray(o, dtype=np.float32) for o in outs)
```

```python
import contextlib
import os
import numpy as np
import concourse.bass as bass
import concourse.mybir as mybir
from concourse.bass_utils import run_bass_kernel_spmd

F32 = mybir.dt.float32
BF16 = mybir.dt.bfloat16
AF = mybir.ActivationFunctionType
ALU = mybir.AluOpType

ENGS = ("pe", "act", "dve", "pool", "sp")
NCORES = 8
TM = 1152
NS = 6
RING_W = 2816
RA_WORDS = 21504
EPS = 1e-6


class _Op:
    __slots__ = ("eng", "emit", "reads", "writes", "dma", "deps", "idx", "sig", "dma_cnt", "n_dma", "cost", "nbytes", "alldeps")


class Prog:
    def __init__(self, nc):
        self.nc = nc
        self.ops = []
        self.last_w = {}
        self.readers = {}
        self.dma_tot = {}
        self.muted = False
        self.rb_with_ra = True
        self.defw = 512
        st = os.environ.get("KSTAGES")
        self.stages = None if not st else set(int(x) for x in st.split(","))

    def stage(self, n):
        self.muted = self.stages is not None and n != 0 and n not in self.stages

    def add(self, eng, emit, reads=(), writes=(), dma=None, n_dma=1, cost=None, nbytes=0):
        if self.muted:
            return -1
        if cost is None:
            w = self.defw
            cost = {"pe": 0.5, "act": 0.22 + w / 1400.0, "dve": 0.16 + w / 960.0, "pool": 0.3, "sp": 0.06}[eng]
            if dma is not None:
                cost = 0.06 if eng == "sp" else 0.9
        if self.rb_with_ra and "RA" in reads and "RB" not in reads:
            reads = list(reads) + ["RB"]
        pr = [k for k in reads if isinstance(k, tuple) and k[0] == "ps"]
        if pr:
            reads = [k for k in reads if k not in pr]
            writes = list(writes) + [k for k in pr if k not in writes]
        op = _Op()
        op.eng, op.emit, op.dma, op.n_dma = eng, emit, dma, n_dma
        op.reads, op.writes = tuple(reads), tuple(writes)
        op.cost, op.nbytes = cost, nbytes
        deps = set()
        for k in op.reads:
            w = self.last_w.get(k)
            if w is not None:
                deps.add(w)
        for k in op.writes:
            w = self.last_w.get(k)
            if w is not None:
                deps.add(w)
            deps.update(self.readers.get(k, ()))
        i = len(self.ops)
        op.deps = deps
        op.sig = False
        op.idx = 0
        if dma is not None:
            self.dma_tot[dma] = self.dma_tot.get(dma, 0) + 16 * n_dma
            op.dma_cnt = self.dma_tot[dma]
        else:
            op.dma_cnt = 0
        self.ops.append(op)
        for k in op.writes:
            self.last_w[k] = i
            self.readers[k] = []
        for k in op.reads:
            if k not in op.writes:
                self.readers.setdefault(k, []).append(i)
        return i

    def schedule(self):
        ops = self.ops
        n = len(ops)
        succ = [[] for _ in range(n)]
        indeg = [0] * n
        for i, op in enumerate(ops):
            op.alldeps = set(op.deps)
            indeg[i] = len(op.deps)
            for d in op.deps:
                succ[d].append(i)
        done = [0.0] * n
        ready = [0.0] * n
        avail = {e: [] for e in ENGS}
        for i, op in enumerate(ops):
            if indeg[i] == 0:
                avail[op.eng].append(i)
        free = {e: 0.0 for e in ENGS}
        order = {e: [] for e in ENGS}
        dma_pipe = 0.0
        left = n
        DELTA = 0.4
        if os.environ.get("KNOSCHED"):
            for i, op in enumerate(ops):
                order[op.eng].append(i)
            return order
        while left:
            best = None
            for e in ENGS:
                av = avail[e]
                if not av:
                    continue
                f = free[e]
                st_min = min(max(ready[i], f) for i in av)
                cand = min((i for i in av if max(ready[i], f) <= st_min + DELTA), key=lambda i: (ops[i].cost > 2.5, i))
                st = max(ready[cand], f)
                if best is None or (st, cand) < (best[0], best[1]):
                    best = (st, cand, e)
            st, i, e = best
            op = ops[i]
            avail[e].remove(i)
            order[e].append(i)
            if op.dma is not None:
                free[e] = st + op.cost
                x0 = max(st + op.cost + 1.2, dma_pipe)
                dur = op.nbytes / 300e3
                dma_pipe = x0 + dur
                done[i] = x0 + dur + 0.8
            else:
                free[e] = st + op.cost
                done[i] = st + op.cost + 0.05
            left -= 1
            for j in succ[i]:
                indeg[j] -= 1
                if done[i] > ready[j]:
                    ready[j] = done[i]
                if indeg[j] == 0:
                    avail[ops[j].eng].append(j)
        self.est_total = max(done) if done else 0.0
        return order

    def build(self):
        nc = self.nc
        ops = self.ops
        order = self.schedule()
        for op in ops:
            pruned = set()
            for d in op.deps:
                p = ops[d]
                if p.dma is None and p.eng == "pe" and op.eng == "pe" and op.dma is None:
                    continue
                pruned.add(d)
                if p.dma is None:
                    p.sig = True
            op.deps = pruned
        cnt = {e: 0 for e in ENGS}
        dtot = {}
        for e in ENGS:
            for i in order[e]:
                op = ops[i]
                if op.dma is None:
                    if op.sig:
                        cnt[e] += 1
                        op.idx = cnt[e]
                else:
                    dtot[op.dma] = dtot.get(op.dma, 0) + 16 * op.n_dma
                    op.dma_cnt = dtot[op.dma]
        with contextlib.ExitStack() as es:
            esem = {e: es.enter_context(nc.semaphore("s_" + e)) for e in ENGS}
            dsem = {k: es.enter_context(nc.semaphore("d_" + str(k))) for k in self.dma_tot}
            block = es.enter_context(nc.Block())

            def run_engine(ename):
                def body(eng):
                    waited = {}
                    for oi in order[ename]:
                        op = ops[oi]
                        need = {}
                        for d in op.deps:
                            p = ops[d]
                            if p.dma is not None:
                                key, val = ("d", p.dma), p.dma_cnt
                            else:
                                key, val = ("e", p.eng), p.idx
                            if val > need.get(key, 0):
                                need[key] = val
                        for key, val in need.items():
                            if waited.get(key, 0) >= val:
                                continue
                            waited[key] = val
                            sem = dsem[key[1]] if key[0] == "d" else esem[key[1]]
                            eng.wait_ge(sem, val)
                        ins = op.emit(eng)
                        if op.dma is not None:
                            if isinstance(ins, (list, tuple)):
                                assert len(ins) == op.n_dma
                                for x in ins:
                                    x.then_inc(dsem[op.dma], 16)
                            else:
                                assert op.n_dma == 1
                                ins.then_inc(dsem[op.dma], 16)
                        elif op.sig:
                            ins.then_inc(esem[ename], 1)
                return body

            block.tensor(run_engine("pe"))
            block.scalar(run_engine("act"))
            block.vector(run_engine("dve"))
            block.gpsimd(run_engine("pool"))
            block.sync(run_engine("sp"))


def build_nc():
    nc = bass.Bass("TRN2", target_bir_lowering=False)

    def din(name, shape):
        return nc.dram_tensor(name, shape, F32, kind="ExternalInput").ap()

    def dout(name, shape):
        return nc.dram_tensor(name, shape, F32, kind="ExternalOutput").ap()

    xp = din("xp", [2048, 1024]); xsm = din("xsm", [128, 1024]); mem = din("mem", [256, 1024])
    sconv = din("sconv", [48, 512]); slru = din("slru", [16, 512]); spool = din("spool", [240, 512])
    ck = din("ck", [16, 256, 1024]); cvv = din("cv", [16, 256, 1024])
    W = {}
    for nm, shp in [("f1g", [1024, 2816]), ("f1u", [1024, 2816]), ("f1d", [2816, 1024]),
                    ("win", [1024, 1536]), ("wout", [1024, 1024]), ("wq", [1024, 1024]),
                    ("wk", [1024, 1024]), ("wv", [1024, 1024]), ("wo", [1024, 1024]),
                    ("f2g", [1024, 2816]), ("f2u", [1024, 2816]), ("f2d", [2816, 1024])]:
        W[nm] = din(nm, shp)
    gains_d = din("gains", [128, 6, 8])
    vec_d = din("vec512", [128, 5, 4])
    convw_d = din("convw", [128, 4, 4])
    lwa_d = din("lwa", [8, 64, 64]); lwx_d = din("lwx", [8, 64, 64]); poolw_d = din("poolw", [4, 128, 128])

    yp = dout("yp", [2048, 1024]); ys = dout("ys", [128, 1024])
    pconv_o = dout("pconv", [3, 512]); plru_o = dout("plru", [1, 512]); ppool_o = dout("ppool", [15, 512])
    pmk_o = dout("pmk", [256, 1024]); pmv_o = dout("pmv", [256, 1024])
    sconv_o = dout("sconv_o", [48, 512]); slru_o = dout("slru_o", [16, 512]); spool_o = dout("spool_o", [240, 512])

    with contextlib.ExitStack() as es:
        def sb(name, shape, dtype=F32):
            return es.enter_context(nc.sbuf_tensor(name, shape, dtype))

        xT = sb("xT", [128, 8, TM])
        hT = sb("hT", [128, 8, TM], BF16)
        regA = sb("regA", [128, RA_WORDS])
        ring = [sb(f"ring{i}", [128, RING_W], BF16) for i in range(NS)]
        kT = sb("kT", [128, 8, 256], BF16)
        Vb = sb("Vb", [128, 2, 1024], BF16)
        stg = [sb(f"stg{i}", [128, 1024]) for i in range(2)]
        rstd = [sb(f"rstd{i}", [128, 512]) for i in range(2)]
        sgt = [sb(f"sgt{i}", [128, 512]) for i in range(2)]
        ident_f = sb("ident_f", [128, 128])
        ident_b = sb("ident_b", [128, 128], BF16)
        ones_b = sb("ones_b", [128, 128], BF16)
        onesm_b = sb("onesm_b", [128, 128], BF16)
        gains = sb("gains_s", [128, 6, 8])
        vec = sb("vec_s", [128, 5, 4])
        convw = sb("convw_s", [128, 4, 4])
        nlam = sb("nlam", [128, 4])
        tl = [sb(f"tl{i}", [128, 4]) for i in range(4)]
        wabd = sb("wabd", [128, 4, 128], BF16)
        wxbd = sb("wxbd", [128, 4, 128], BF16)
        poolw = sb("poolw_s", [128, 4, 128], BF16)
        invc = sb("invc", [128, 4, 16])
        hstate = sb("hstate", [128, 4])
        cL = sb("cL", [128, 4, 3])
        cP = sb("cP", [128, 4, 15])
        sc_hist = sb("sc_hist", [128, 4, 48])
        sp_hist = sb("sp_hist", [128, 4, 240])
        sl_hist = sb("sl_hist", [128, 4, 16])
        bar = sb("bar", [128, 2])
        epsc = sb("epsc", [128, 1])
        onec = sb("onec", [128, 1])
        hvec = sb("hvec", [128, 3, 4])
        tmpT = sb("tmpT", [128, 4, 128])
        psb = [es.enter_context(nc.psum_tensor(f"psb{i}", [128, 512], F32)) for i in range(8)]

        P = Prog(nc)
        cnt = {"ring": 0, "mm": 0, "out": 0, "stg": 0, "rstd": 0, "sgt": 0, "odma": 0}
        out_keys = []

        def bank(cls):
            if cls == "mm":
                b = cnt["mm"] % 4
                cnt["mm"] += 1
                return b
            b = 4 + cnt["out"] % 2
            cnt["out"] += 1
            return b

        def PS(b):
            return ("ps", b)

        def ra_f32(off, n):
            return regA[:, off:off + n]

        def ra_bf16(off, n_bf):
            return regA[:, off:off + n_bf // 2].bitcast(BF16)

        last_phase = [None]

        def ra_barrier(rb=True, phase=None, write_rb=True):
            P.rb_with_ra = rb
            if phase is not None and phase == last_phase[0]:
                return
            last_phase[0] = phase
            if rb and not write_rb:
                P.add("dve", lambda e: e.memset(bar[:, 1:2], 0.0), writes=["RB", "bar1"], cost=0.1)
            P.add("dve", lambda e: e.memset(bar[:, 0:1], 0.0), writes=["RA", "bar"] + (["RB"] if (rb and write_rb) else []), cost=0.1)

        def rb_barrier():
            P.add("dve", lambda e: e.memset(bar[:, 1:2], 0.0), writes=["RB", "bar1"])

        def rb_slot(k):
            return ra_f32(12672 + 1024 * k, 1024)

        def ringload(src, k, c):
            s = cnt["ring"] % NS
            cnt["ring"] += 1
            dst = ring[s][:, 0:k * c].rearrange("p (k c) -> p k c", k=k)
            P.add("pool", lambda e, dst=dst, src=src: e.dma_start(out=dst, in_=src),
                  writes=[("ring", s)], dma=f"ring{s}", nbytes=128 * k * c * 4)
            return dst, ("ring", s)

        def wslab(name, j, c=256):
            v = W[name].rearrange("(kc p) c -> p kc c", p=128)
            kc = W[name].shape[0] // 128
            return ringload(v[:, :, j * c:(j + 1) * c], kc, c)

        def mmgroup(out_ap, pairs, reads, bnk):
            def emit(e, out_ap=out_ap, pairs=pairs):
                last = None
                n = len(pairs)
                for i, (l, r) in enumerate(pairs):
                    last = e.matmul(out_ap, l, r, start=(i == 0), stop=(i == n - 1))
                return last
            cst = sum(max(r.shape[-1], 64) / 2400.0 + 0.012 for (_, r) in pairs)
            P.add("pe", emit, reads=reads, writes=[PS(bnk)], cost=cst)

        P.add("pool", lambda e: e.memset(ident_f[:], 0.0), writes=["ident_f"])
        P.add("pool", lambda e: e.affine_select(ident_f[:], ident_f[:], [[-1, 128]], ALU.not_equal, 1.0,
                                                 base=0, channel_multiplier=1),
              reads=["ident_f"], writes=["ident_f"])
        P.add("dve", lambda e: e.tensor_copy(ident_b[:], ident_f[:]), reads=["ident_f"], writes=["ident_b"])
        P.add("dve", lambda e: e.memset(ones_b[:], 1.0), writes=["ones_b"])
        P.add("dve", lambda e: e.memset(epsc[:], EPS), writes=["epsc"])
        P.add("dve", lambda e: e.memset(onec[:], 1.0), writes=["onec"])
        P.add("dve", lambda e: e.memset(onesm_b[:], 1.0 / 1024.0), writes=["onesm_b"])
        P.add("dve", lambda e: e.memset(hstate[:], 0.0), writes=[("hstate", c) for c in range(4)])
        P.add("dve", lambda e: e.memset(cL[:], 0.0), writes=[("cL", c) for c in range(4)])
        P.add("dve", lambda e: e.memset(cP[:], 0.0), writes=[("cP", c) for c in range(4)])
        P.add("sp", lambda e: e.dma_start(out=gains[:], in_=gains_d), writes=["gains"], dma="c0")
        P.add("sp", lambda e: e.dma_start(out=vec[:], in_=vec_d), writes=["vec"], dma="c1")
        P.add("sp", lambda e: e.dma_start(out=convw[:], in_=convw_d), writes=["convw"], dma="c2")
        P.add("pool", lambda e: e.memset(wabd[:], 0.0), writes=["wabd"])
        P.add("pool", lambda e: e.memset(wxbd[:], 0.0), writes=["wxbd"])

        def bd_load(dst, src, key):
            def emit(e):
                r = []
                for hh in range(8):
                    c, j = hh // 2, hh % 2
                    r.append(e.dma_start(out=dst[64 * j:64 * j + 64, c, 64 * j:64 * j + 64], in_=src[hh]))
                return r
            P.add("pool", emit, reads=[key], writes=[key], dma="bd_" + key, n_dma=8)
        bd_load(wabd, lwa_d, "wabd")
        bd_load(wxbd, lwx_d, "wxbd")
        P.add("pool", lambda e: e.dma_start(out=poolw[:], in_=poolw_d.rearrange("g i j -> i g j")),
              writes=["poolw"], dma="c3")
        for g, win in enumerate((2, 4, 8, 16)):
            P.add("pool", lambda e, g=g, win=win: e.memset(invc[:, g, :], 1.0 / win), reads=["invc"], writes=["invc"])
            for t in range(win - 1):
                P.add("pool", lambda e, g=g, t=t: e.memset(invc[:, g, t:t + 1], 1.0 / (t + 1)),
                      reads=["invc"], writes=["invc"])
        lam = vec[:, 3, :]
        P.add("act", lambda e: e.activation(tl[0][:], lam, AF.Exp, scale=-1.0), reads=["vec"], writes=["tl0"])
        P.add("dve", lambda e: e.tensor_scalar(tl[1][:], tl[0][:], 2.0, None, ALU.add), reads=["tl0"], writes=["tl1"])
        P.add("dve", lambda e: e.reciprocal(tl[1][:], tl[1][:]), reads=["tl1"], writes=["tl1"])
        P.add("dve", lambda e: e.tensor_tensor(tl[1][:], tl[0][:], tl[1][:], ALU.mult), reads=["tl0", "tl1"], writes=["tl1"])
        P.add("dve", lambda e: e.tensor_tensor(tl[2][:], tl[1][:], tl[1][:], ALU.mult), reads=["tl1"], writes=["tl2"])
        P.add("dve", lambda e: e.tensor_scalar(tl[3][:], tl[2][:], 1.0 / 11.0, 1.0 / 9.0, ALU.mult, ALU.add), reads=["tl2"], writes=["tl3"])
        for coef in (1.0 / 7.0, 1.0 / 5.0, 1.0 / 3.0, 1.0):
            P.add("dve", lambda e: e.tensor_tensor(tl[3][:], tl[3][:], tl[2][:], ALU.mult), reads=["tl3", "tl2"], writes=["tl3"])
            P.add("dve", lambda e, coef=coef: e.tensor_scalar(tl[3][:], tl[3][:], coef, None, ALU.add), reads=["tl3"], writes=["tl3"])
        P.add("dve", lambda e: e.tensor_tensor(tl[3][:], tl[3][:], tl[1][:], ALU.mult), reads=["tl3", "tl1"], writes=["tl3"])
        P.add("dve", lambda e: e.tensor_scalar(nlam[:], tl[3][:], -16.0, None, ALU.mult), reads=["tl3"], writes=["nlam"])
        P.add("dve", lambda e: e.tensor_scalar(hvec[:, 0:2, :], vec[:, 1:3, :], 0.5, None, ALU.mult), reads=["vec"], writes=["hvec"])
        P.add("dve", lambda e: e.tensor_scalar(hvec[:, 2, :], nlam[:], 0.5, None, ALU.mult), reads=["nlam", "hvec"], writes=["hvec"])

        def load_rows_T(dram_rows, R, dst_fn, dst_keys):
            k = cnt["stg"] % 2
            cnt["stg"] += 1
            P.add("sp", lambda e: e.dma_start(out=stg[k][0:R, 0:512], in_=dram_rows), writes=[("stg", k)], dma=f"stg{k}")
            b = bank("mm")

            def emit(e):
                last = None
                for c in range(4):
                    last = e.transpose(psb[b][:, c * 128:c * 128 + R], stg[k][0:R, c * 128:(c + 1) * 128], ident_f[0:R, 0:R])
                return last
            P.add("pe", emit, reads=[("stg", k), "ident_f"], writes=[PS(b)])
            for c in range(4):
                P.add("act", lambda e, c=c: e.copy(dst_fn(c), psb[b][:, c * 128:c * 128 + R]),
                      reads=[PS(b)], writes=[dst_keys[c]])

        def store_rows_T(src_fn, src_keys, R, dram_writer, ndma=1, s3=None):
            k = cnt["stg"] % 2
            cnt["stg"] += 1
            b = bank("mm")
            if s3 is not None:
                srcs = src_fn
                for c in range(4):
                    P.add("act", lambda e, c=c: e.copy(tmpT[:, c, 0:R].rearrange("p (s r) -> p s r", s=s3), srcs(c)),
                          reads=list(src_keys), writes=[("tmpT", c)])
                src_fn = lambda c: tmpT[:, c, 0:R]
                src_keys = [("tmpT", c) for c in range(4)]

            def emit(e):
                last = None
                for c in range(4):
                    last = e.transpose(psb[b][0:R, c * 128:(c + 1) * 128], src_fn(c), ident_f[:])
                return last
            P.add("pe", emit, reads=list(src_keys) + ["ident_f"], writes=[PS(b)])
            P.add("act", lambda e: e.copy(stg[k][0:R, 0:512], psb[b][0:R, :]), reads=[PS(b)], writes=[("stg", k)])
            ok = ("o", cnt["odma"])
            cnt["odma"] += 1
            out_keys.append(ok)
            P.add("sp", lambda e: dram_writer(e, stg[k]), reads=[("stg", k)], writes=[ok], dma=f"stg{k}", n_dma=ndma)

        P.stage(1)
        load_rows_T(sconv, 48, lambda c: sc_hist[:, c, :], [("sc_hist", c) for c in range(4)])
        load_rows_T(slru, 16, lambda c: sl_hist[:, c, :], [("sl_hist", c) for c in range(4)])
        load_rows_T(spool[0:128, :], 128, lambda c: sp_hist[:, c, 0:128], [("sp_hist", c) for c in range(4)])
        load_rows_T(spool[128:240, :], 112, lambda c: sp_hist[:, c, 128:240], [("sp_hist", c) for c in range(4)])
        ok = ("o", cnt["odma"]); cnt["odma"] += 1; out_keys.append(ok)
        P.add("sp", lambda e: e.dma_start(out=spool_o.rearrange("(s r) c -> s r c", r=15)[:, 0:7, :],
                                          in_=spool.rearrange("(s r) c -> s r c", r=15)[:, 8:15, :]),
              writes=[ok], dma="hbm2hbm")

        def norm(gi, cols, Wd, xkeys, hkeys, inplace=False, src=None, dst=None):
            src = xT if src is None else src
            dst = hT if dst is None else dst
            P.defw = Wd
            c0 = cols
            for c in range(8):
                P.add("act", lambda e, c=c: e.activation(dst[:, c, c0:c0 + Wd] if not inplace else hT[:, c, c0:c0 + Wd],
                                                         src[:, c, c0:c0 + Wd], AF.Square),
                      reads=[xkeys[c]], writes=[hkeys[c]])
            sqv = hT if inplace else dst
            b = 6
            mmgroup(psb[b][:, 0:Wd], [(onesm_b[:], sqv[:, c, c0:c0 + Wd]) for c in range(8)],
                    reads=list(hkeys) + ["onesm_b"], bnk=b)
            r = cnt["rstd"] % 2
            cnt["rstd"] += 1
            P.add("act", lambda e: e.activation(rstd[r][:, 0:Wd], psb[b][:, 0:Wd], AF.Ln, bias=epsc[:, 0:1]),
                  reads=[PS(b), "epsc"], writes=[("rstd", r)], cost=1.5 + Wd / 1400.0)
            P.add("act", lambda e: e.activation(rstd[r][:, 0:Wd], rstd[r][:, 0:Wd], AF.Exp, scale=-0.5),
                  reads=[("rstd", r)], writes=[("rstd", r)], cost=1.5 + Wd / 1400.0)
            for c in range(8):
                if inplace:
                    P.add("dve", lambda e, c=c: e.scalar_tensor_tensor(src[:, c, c0:c0 + Wd], src[:, c, c0:c0 + Wd],
                                                                       gains[:, gi, c:c + 1], rstd[r][:, 0:Wd], ALU.mult, ALU.mult),
                          reads=[xkeys[c], ("rstd", r), "gains"], writes=[xkeys[c]])
                else:
                    P.add("dve", lambda e, c=c: e.scalar_tensor_tensor(dst[:, c, c0:c0 + Wd], src[:, c, c0:c0 + Wd],
                                                                       gains[:, gi, c:c + 1], rstd[r][:, 0:Wd], ALU.mult, ALU.mult),
                          reads=[xkeys[c], ("rstd", r), "gains", hkeys[c]], writes=[hkeys[c]])

        def xk(sbi):
            return [("x", c, sbi) for c in range(8)]

        def hk(sbi):
            return [("h", c, sbi) for c in range(8)]

        P.stage(0)
        def mem_phase():
            memT = ra_f32(0, 2048).rearrange("p (c m) -> p c m", c=8)
            mT = ra_bf16(2048, 2048).rearrange("p (c m) -> p c m", c=8)
            kst = ra_f32(3072, 2048).rearrange("p (a c) -> p a c", a=2)
            vst = ra_f32(5120, 2048).rearrange("p (a c) -> p a c", a=2)
            mstage = ra_f32(7168, 2048).rearrange("p (a c) -> p a c", a=2)
            P.add("sp", lambda e: e.dma_start(out=mstage, in_=mem.rearrange("(a p) c -> p a c", p=128)),
                  reads=["RA"], writes=["mstage"], dma="mstage")
            for mc in range(2):
                for half in range(2):
                    b = bank("mm")

                    def emit(e, mc=mc, half=half, b=b):
                        last = None
                        for cc in range(4):
                            c = half * 4 + cc
                            last = e.transpose(psb[b][:, cc * 128:(cc + 1) * 128], mstage[:, mc, c * 128:(c + 1) * 128], ident_f[:])
                        return last
                    P.add("pe", emit, reads=["mstage", "ident_f", "RA"], writes=[PS(b)])
                    P.add("act", lambda e, mc=mc, half=half, b=b: e.copy(
                        memT[:, half * 4:half * 4 + 4, mc * 128:(mc + 1) * 128],
                        psb[b][:, :].rearrange("p (c m) -> p c m", c=4)),
                        reads=[PS(b), "RA"], writes=[("memTc", half * 4 + cc) for cc in range(4)])
            mkeys_in = [("memTc", c) for c in range(8)]
            mTk = [("mT", c) for c in range(8)]
            def mem_norm():
                for c in range(8):
                    P.add("act", lambda e, c=c: e.activation(mT[:, c, :], memT[:, c, :], AF.Square),
                          reads=[mkeys_in[c], "RA"], writes=[mTk[c]])
                b = 6
                mmgroup(psb[b][:, 0:256], [(onesm_b[:], mT[:, c, :]) for c in range(8)], reads=mTk + ["onesm_b", "RA"], bnk=b)
                P.add("act", lambda e: e.activation(rstd[0][:, 0:256], psb[b][:, 0:256], AF.Sqrt, bias=epsc[:, 0:1]),
                      reads=[PS(b), "epsc"], writes=[("rstd", 0)])
                P.add("dve", lambda e: e.reciprocal(rstd[0][:, 0:256], rstd[0][:, 0:256]), reads=[("rstd", 0)], writes=[("rstd", 0)])
                for c in range(8):
                    P.add("dve", lambda e, c=c: e.scalar_tensor_tensor(mT[:, c, :], memT[:, c, :], gains[:, 3, c:c + 1],
                                                                       rstd[0][:, 0:256], ALU.mult, ALU.mult),
                          reads=[mkeys_in[c], ("rstd", 0), "gains", mTk[c], "RA"], writes=[mTk[c]])
            mem_norm()
            for j in range(4):
                Ks, kkey = wslab("wk", j)
                for mc in range(2):
                    b = bank("mm")
                    mmgroup(psb[b][:, 0:256], [(mT[:, kc, mc * 128:(mc + 1) * 128], Ks[:, kc, :]) for kc in range(8)],
                            reads=mTk + [kkey, "RA"], bnk=b)
                    P.add("act", lambda e, b=b, mc=mc, j=j: e.copy(kst[:, mc, j * 256:(j + 1) * 256], psb[b][:, 0:256]),
                          reads=[PS(b), "RA"], writes=[("kst", mc, j)])
                for dc in range(2):
                    b = bank("mm")
                    mmgroup(psb[b][:, 0:256], [(Ks[:, kc, dc * 128:(dc + 1) * 128], mT[:, kc, :]) for kc in range(8)],
                            reads=mTk + [kkey, "RA"], bnk=b)
                    P.add("dve", lambda e, b=b, dc=dc, j=j: e.tensor_copy(kT[:, 2 * j + dc, :], psb[b][:, 0:256]),
                          reads=[PS(b)], writes=[("kT", 2 * j + dc)])
            for j in range(4):
                Vs, vkey = wslab("wv", j)
                for mc in range(2):
                    b = bank("mm")
                    mmgroup(psb[b][:, 0:256], [(mT[:, kc, mc * 128:(mc + 1) * 128], Vs[:, kc, :]) for kc in range(8)],
                            reads=mTk + [vkey, "RA"], bnk=b)
                    P.add("act", lambda e, b=b, mc=mc, j=j: e.copy(vst[:, mc, j * 256:(j + 1) * 256], psb[b][:, 0:256]),
                          reads=[PS(b), "RA"], writes=[("vst", mc, j)])
                    P.add("dve", lambda e, b=b, mc=mc, j=j: e.tensor_copy(Vb[:, mc, j * 256:(j + 1) * 256], psb[b][:, 0:256]),
                          reads=[PS(b)], writes=[("Vb", mc, j)])
            for nm, st_, dst_ in (("kst", kst, pmk_o), ("vst", vst, pmv_o)):
                ok = ("o", cnt["odma"]); cnt["odma"] += 1; out_keys.append(ok)
                P.add("sp", lambda e, st_=st_, dst_=dst_: e.dma_start(out=dst_.rearrange("(a p) c -> p a c", p=128), in_=st_),
                      reads=[(nm, mc, j) for mc in range(2) for j in range(4)] + ["RA"], writes=[ok], dma="o_" + nm)
        kTkeys = [("kT", i) for i in range(8)]
        Vbkeys = [("Vb", mc, j) for mc in range(2) for j in range(4)]

        P.stage(0)
        a_bf = ra_bf16(0, 22 * TM).rearrange("p (f t) -> p f t", f=22)

        def ffn(pfx, sbs, sb_outer=False):
            ra_barrier(rb=False)
            for j in range(11):
                G, gk = wslab(pfx + "g", j)
                U, uk = wslab(pfx + "u", j)
                for (sbi, c0, Wd) in sbs:
                    P.defw = Wd
                    for half in range(2):
                        f = 2 * j + half
                        bg = bank("mm"); bu = bank("mm")
                        mmgroup(psb[bg][:, 0:Wd], [(G[:, k, half * 128:(half + 1) * 128], hT[:, k, c0:c0 + Wd]) for k in range(8)],
                                reads=[gk] + hk(sbi), bnk=bg)
                        mmgroup(psb[bu][:, 0:Wd], [(U[:, k, half * 128:(half + 1) * 128], hT[:, k, c0:c0 + Wd]) for k in range(8)],
                                reads=[uk] + hk(sbi), bnk=bu)
                        t = cnt["sgt"] % 2
                        cnt["sgt"] += 1
                        P.add("act", lambda e, t=t, bg=bg, Wd=Wd: e.activation(sgt[t][:, 0:Wd], psb[bg][:, 0:Wd], AF.Silu),
                              reads=[PS(bg)], writes=[("sgt", t)])
                        P.add("dve", lambda e, t=t, bu=bu, Wd=Wd, f=f, c0=c0: e.tensor_tensor(
                            a_bf[:, f, c0:c0 + Wd], sgt[t][:, 0:Wd], psb[bu][:, 0:Wd], ALU.mult),
                            reads=[("sgt", t), PS(bu), "RA"], writes=[("a", f, sbi)])
            dloop = [(d, [sb_]) for sb_ in sbs for d in range(8)] if sb_outer else [(d, sbs) for d in range(8)]
            for (d, sbl) in dloop:
                v = W[pfx + "d"].rearrange("(fc p) c -> p fc c", p=128)
                Dd, dk = ringload(v[:, :, d * 128:(d + 1) * 128], 22, 128)
                for (sbi, c0, Wd) in sbl:
                    P.defw = Wd
                    bo = bank("out")
                    mmgroup(psb[bo][:, 0:Wd], [(Dd[:, f, :], a_bf[:, f, c0:c0 + Wd]) for f in range(22)],
                            reads=[dk, "RA"] + [("a", f, sbi) for f in range(22)], bnk=bo)
                    P.add("dve", lambda e, bo=bo, d=d, c0=c0, Wd=Wd: e.scalar_tensor_tensor(
                        xT[:, d, c0:c0 + Wd], psb[bo][:, 0:Wd], 0.5, xT[:, d, c0:c0 + Wd], ALU.mult, ALU.add),
                        reads=[PS(bo), ("x", d, sbi)], writes=[("x", d, sbi)])

        def mixer_sb(sbi, c0, Wd, nseq, L, first_prompt, pass_idx, fill=None):
            if fill is None:
                fill = lambda n: None
            P.defw = Wd
            ra_barrier(phase="mixer", write_rb=False)
            Hc, Hp = 3, 15
            def uLv(c):
                return ra_f32(12672 + c * 528, nseq * (Hc + L)).rearrange("p (s t) -> p s t", s=nseq)
            def uPv(g):
                return ra_f32(14784 + g * 528, nseq * (Hp + L)).rearrange("p (s t) -> p s t", s=nseq)
            def f2(base, c):
                return ra_f32(base + c * 512, Wd)
            def f3(base, c):
                return f2(base, c).rearrange("p (s t) -> p s t", s=nseq)
            GA, CV, TA, TB, MM = (16896 if sbi % 2 == 0 else 18944), 0, 2048, 4096, 6144
            def cvbv(c):
                return ra_bf16(8192 + c * 256, 512)[:, 0:Wd]
            def psv(i):
                return ra_f32(9216 + i * 528, nseq * (Hp + L)).rearrange("p (s t) -> p s t", s=nseq)
            def dlv(g):
                return ra_bf16(10272 + g * 256, 512)[:, 0:Wd]
            R4 = range(4)

            norm(1, c0, Wd, xk(sbi), hk(sbi))
            fill(1)
            for c in R4:
                if nseq == 1:
                    P.add("dve", lambda e, c=c: e.tensor_copy(uLv(c)[:, 0, 0:Hc], cL[:, c, :]),
                          reads=["RB", ("cL", c), ("uL", c)], writes=[("uLh", c)])
                    P.add("dve", lambda e, c=c: e.tensor_copy(uPv(c)[:, 0, 0:Hp], cP[:, c, :]),
                          reads=["RB", ("cP", c), ("uP", c)], writes=[("uPh", c)])
                else:
                    P.add("dve", lambda e, c=c: e.tensor_copy(uLv(c)[:, :, 0:Hc], sc_hist[:, c, :].rearrange("p (s r) -> p s r", s=16)),
                          reads=["RB", ("sc_hist", c), ("uL", c)], writes=[("uLh", c)])
                    P.add("dve", lambda e, c=c: e.tensor_copy(uPv(c)[:, :, 0:Hp], sp_hist[:, c, :].rearrange("p (s r) -> p s r", s=16)),
                          reads=["RB", ("sp_hist", c), ("uP", c)], writes=[("uPh", c)])
            for j in (0, 1, 4, 5, 2, 3):
                S, skey = wslab("win", j)
                for half in range(2):
                    cc = 2 * j + half
                    b = bank("mm")
                    mmgroup(psb[b][:, 0:Wd], [(S[:, k, half * 128:(half + 1) * 128], hT[:, k, c0:c0 + Wd]) for k in range(8)],
                            reads=[skey] + hk(sbi), bnk=b)
                    src3 = psb[b][:, 0:Wd].rearrange("p (s t) -> p s t", s=nseq)
                    if cc < 4:
                        P.add("act", lambda e, cc=cc, src3=src3: e.copy(uLv(cc)[:, :, Hc:Hc + L], src3),
                              reads=[PS(b), "RB", ("uLh", cc)], writes=[("uL", cc)])
                    elif cc < 8:
                        P.add("act", lambda e, cc=cc, b=b: e.activation(f2(GA, cc - 4), psb[b][:, 0:Wd], AF.Gelu_apprx_tanh),
                              reads=[PS(b), "RB"], writes=[("ga", sbi % 2, cc - 4)])
                    else:
                        P.add("act", lambda e, cc=cc, src3=src3: e.copy(uPv(cc - 8)[:, :, Hp:Hp + L], src3),
                              reads=[PS(b), "RB", ("uPh", cc - 8)], writes=[("uP", cc - 8)])
                if j == 1:
                    for c in R4:
                        P.add("act", lambda e, c=c: e.activation(f3(CV, c), uLv(c)[:, :, 0:L], AF.Identity,
                                                                 scale=convw[:, c, 0:1], bias=vec[:, 0, c:c + 1]),
                              reads=["RA", ("uL", c), ("uLh", c), "convw", "vec"], writes=[("cv", c)])
                    for k in range(1, 4):
                        for c in R4:
                            P.add("dve", lambda e, c=c, k=k: e.scalar_tensor_tensor(f3(CV, c), uLv(c)[:, :, k:k + L], convw[:, c, k:k + 1],
                                                                                   f3(CV, c), ALU.mult, ALU.add),
                                  reads=["RA", ("uL", c), ("uLh", c), "convw", ("cv", c)], writes=[("cv", c)])
                    for c in R4:
                        P.add("act", lambda e, c=c: e.copy(cvbv(c), f2(CV, c)), reads=["RA", ("cv", c)], writes=[("cvb", c)])
                    if nseq == 1:
                        for c in R4:
                            P.add("dve", lambda e, c=c: e.tensor_copy(cL[:, c, :], uLv(c)[:, 0, L:L + Hc]),
                                  reads=["RB", ("uL", c)], writes=[("cL", c)])
                    fill(1)
                if j == 5:
                    if nseq == 1:
                        for c in R4:
                            P.add("dve", lambda e, c=c: e.tensor_copy(cP[:, c, :], uPv(c)[:, 0, L:L + Hp]),
                                  reads=["RB", ("uP", c)], writes=[("cP", c)])
                    for g, win in enumerate((2, 4, 8, 16)):
                        uP = uPv(g)
                        TT = Hp + L
                        cur = uP
                        curkey = [("uP", g), ("uPh", g)]
                        for lev in range(g + 1):
                            sh = 1 << lev
                            lo = (1 << (lev + 1)) - 1
                            dstb = psv(lev % 2)
                            P.add("dve", lambda e, dstb=dstb, cur=cur, lo=lo, sh=sh, TT=TT: e.tensor_tensor(
                                dstb[:, :, lo:TT], cur[:, :, lo:TT], cur[:, :, lo - sh:TT - sh], ALU.add),
                                reads=["RA"] + curkey, writes=[("psv", lev % 2)])
                            cur = dstb
                            curkey = [("psv", lev % 2)]
                        dl = dlv(g)
                        dl3 = dl.rearrange("p (s t) -> p s t", s=nseq)
                        P.add("dve", lambda e, dl3=dl3, cur=cur, uP=uP, win=win: e.scalar_tensor_tensor(
                            dl3, cur[:, :, Hp:Hp + L], 1.0 / win, uP[:, :, Hp:Hp + L], ALU.mult, ALU.subtract),
                            reads=["RA", ("uP", g)] + curkey, writes=[("dl", g)])
                        if first_prompt:
                            t0 = f2(MM, 0)
                            P.add("dve", lambda e, t0=t0, cur=cur, g=g: e.tensor_tensor(t0[:, 0:15], cur[:, 0, Hp:Hp + 15], invc[:, g, 0:15], ALU.mult),
                                  reads=["RA", "invc"] + curkey, writes=[("mm", 0)])
                            P.add("dve", lambda e, t0=t0, dl=dl, uP=uP, g=g: e.tensor_tensor(dl[:, 0:15], t0[:, 0:15], uP[:, 0, Hp:Hp + 15], ALU.subtract),
                                  reads=["RA", ("mm", 0), ("uP", g), ("dl", g)], writes=[("dl", g)])
                    fill(1)
            for g in R4:
                bq = bank("out")
                mmgroup(psb[bq][:, 0:Wd], [(poolw[:, g, :], dlv(g))], reads=["RA", ("dl", g), "poolw"], bnk=bq)
                P.add("act", lambda e, bq=bq, g=g: e.activation(hT[:, 4 + g, c0:c0 + Wd], psb[bq][:, 0:Wd], AF.Copy, scale=vec[:, 4, g:g + 1]),
                      reads=[PS(bq), "vec"], writes=[("h", 4 + g, sbi)])
            for c in R4:
                br = bank("mm"); bi = bank("mm")
                mmgroup(psb[br][:, 0:Wd], [(wabd[:, c, :], cvbv(c))], reads=["RA", ("cvb", c), "wabd"], bnk=br)
                mmgroup(psb[bi][:, 0:Wd], [(wxbd[:, c, :], cvbv(c))], reads=["RA", ("cvb", c), "wxbd"], bnk=bi)
                P.add("act", lambda e, br=br, c=c: e.activation(f2(TA, c), psb[br][:, 0:Wd], AF.Tanh, scale=0.5, bias=hvec[:, 0, c:c + 1]),
                      reads=["RA", PS(br), "hvec"], writes=[("ta", c)])
                P.add("act", lambda e, bi=bi, c=c: e.activation(f2(TB, c), psb[bi][:, 0:Wd], AF.Tanh, scale=0.5, bias=hvec[:, 1, c:c + 1]),
                      reads=["RA", PS(bi), "hvec"], writes=[("tb", c)])
            fill(1)
            for c in R4:
                P.add("act", lambda e, c=c: e.activation(f2(TA, c), f2(TA, c), AF.Exp, scale=hvec[:, 2, c:c + 1], bias=hvec[:, 2, c:c + 1]),
                      reads=["RA", ("ta", c), "hvec"], writes=[("ta", c)])
            for c in R4:
                P.add("act", lambda e, c=c: e.activation(f2(MM, c), f2(TA, c), AF.Square), reads=["RA", ("ta", c)], writes=[("mm", c)])
            for c in R4:
                P.add("dve", lambda e, c=c: e.scalar_tensor_tensor(f2(TB, c), f2(TB, c), 1.0, f2(CV, c), ALU.add, ALU.mult),
                      reads=["RA", ("tb", c), ("cv", c)], writes=[("tb", c)])
            for c in R4:
                P.add("dve", lambda e, c=c: e.tensor_scalar(f2(MM, c), f2(MM, c), 1.0, None, ALU.min), reads=["RA", ("mm", c)], writes=[("mm", c)])
            for c in R4:
                P.add("act", lambda e, c=c: e.activation(f2(MM, c), f2(MM, c), AF.Sqrt, scale=-1.0, bias=onec[:, 0:1]),
                      reads=["RA", ("mm", c), "onec"], writes=[("mm", c)])
            fill(1)
            for c in R4:
                P.add("dve", lambda e, c=c: e.scalar_tensor_tensor(f2(TB, c), f2(TB, c), 0.5, f2(MM, c), ALU.mult, ALU.mult),
                      reads=["RA", ("tb", c), ("mm", c)], writes=[("tb", c)])
            for c in R4:
                if nseq == 1:
                    P.add("dve", lambda e, c=c: e.tensor_tensor_scan(f2(CV, c), f2(TA, c), f2(TB, c), hstate[:, c:c + 1], ALU.mult, ALU.add),
                          reads=["RA", ("ta", c), ("tb", c), ("cv", c), ("hstate", c)], writes=[("cv", c)], cost=0.2 + Wd / 400.0)
                    P.add("dve", lambda e, c=c: e.tensor_copy(hstate[:, c:c + 1], f2(CV, c)[:, Wd - 1:Wd]),
                          reads=["RA", ("cv", c)], writes=[("hstate", c)])
                else:
                    a3, b3, m3, cv3 = f3(TA, c), f3(TB, c), f3(MM, c), f3(CV, c)
                    P.add("dve", lambda e, m3=m3, a3=a3, c=c: e.tensor_tensor(m3[:, :, 0:1], a3[:, :, 0:1], sl_hist[:, c, :].rearrange("p (s o) -> p s o", o=1), ALU.mult),
                          reads=["RA", ("ta", c), ("mm", c), ("tb", c), ("sl_hist", c)], writes=[("mm", c)])
                    P.add("dve", lambda e, m3=m3, b3=b3: e.tensor_tensor(b3[:, :, 0:1], b3[:, :, 0:1], m3[:, :, 0:1], ALU.add),
                          reads=["RA", ("tb", c), ("mm", c)], writes=[("tb", c)])
                    P.add("dve", lambda e, a3=a3: e.memset(a3[:, :, 0:1], 0.0), reads=["RA", ("ta", c), ("mm", c)], writes=[("ta", c)])
                    P.add("dve", lambda e, c=c: e.tensor_tensor_scan(f2(CV, c), f2(TA, c), f2(TB, c), 0.0, ALU.mult, ALU.add),
                          reads=["RA", ("ta", c), ("tb", c), ("cv", c)], writes=[("cv", c)], cost=0.2 + Wd / 400.0)
                    P.add("dve", lambda e, cv3=cv3, c=c: e.tensor_copy(sl_hist[:, c, :].rearrange("p (s o) -> p s o", o=1), cv3[:, :, L - 1:L]),
                          reads=["RA", ("cv", c), ("sl_hist", c)], writes=[("sl_out", c)])
                P.add("dve", lambda e, c=c: e.tensor_tensor(hT[:, c, c0:c0 + Wd], f2(GA, c), f2(CV, c), ALU.mult),
                      reads=["RA", ("ga", sbi % 2, c), ("cv", c)], writes=[("h", c, sbi)])
            fill(1)
            if nseq > 1:
                store_rows_T(lambda c: uLv(c)[:, :, L:L + Hc], [("uL", c) for c in range(4)] + ["RA"], 48,
                             lambda e, s: e.dma_start(out=sconv_o, in_=s[0:48, 0:512]), s3=16)
                store_rows_T(lambda c: sl_hist[:, c, :], [("sl_out", c) for c in range(4)], 16,
                             lambda e, s: e.dma_start(out=slru_o, in_=s[0:16, 0:512]))
                sp3 = spool_o.rearrange("(s r) c -> s r c", r=15)
                store_rows_T(lambda c: uPv(c)[:, :, Hp:Hp + L], [("uP", c) for c in range(4)] + ["RA"], 128,
                             lambda e, s: [e.dma_start(out=sp3[q, 7:15, :], in_=s[8 * q:8 * q + 8, 0:512]) for q in range(16)], ndma=16, s3=16)
            elif pass_idx == 1 and c0 + Wd == 1024:
                store_rows_T(lambda c: cL[:, c, :], [("cL", c) for c in range(4)], 3,
                             lambda e, s: e.dma_start(out=pconv_o, in_=s[0:3, 0:512]))
                store_rows_T(lambda c: cP[:, c, :], [("cP", c) for c in range(4)], 15,
                             lambda e, s: e.dma_start(out=ppool_o, in_=s[0:15, 0:512]))
                store_rows_T(lambda c: hstate[:, c:c + 1], [("hstate", c) for c in range(4)], 1,
                             lambda e, s: e.dma_start(out=plru_o, in_=s[0:1, 0:512]))
            for j in range(4):
                S, skey = wslab("wout", j)
                for half in range(2):
                    d = 2 * j + half
                    bo = bank("out")
                    mmgroup(psb[bo][:, 0:Wd], [(S[:, k, half * 128:(half + 1) * 128], hT[:, k, c0:c0 + Wd]) for k in range(8)],
                            reads=[skey] + hk(sbi), bnk=bo)
                    P.add("dve", lambda e, bo=bo, d=d: e.tensor_tensor(xT[:, d, c0:c0 + Wd], xT[:, d, c0:c0 + Wd], psb[bo][:, 0:Wd], ALU.add),
                          reads=[PS(bo), ("x", d, sbi)], writes=[("x", d, sbi)])
                fill(1)

        def xattn(sbs, with_sample, with_mem=False):
            ra_barrier()
            if with_mem:
                mem_phase()
            Kst = [ra_bf16(7168 + i * 1024, 2048).rearrange("p (a c) -> p a c", a=2) for i in range(4)]
            Vst = [ra_bf16(11264 + i * 1024, 2048).rearrange("p (a c) -> p a c", a=2) for i in range(4)]
            kTs = [ra_bf16(15360 + i * 1024, 2048).rearrange("p (a m) -> p a m", a=8) for i in range(4)]
            q = ra_bf16(0, 8 * TM).rearrange("p (c t) -> p c t", c=8)
            def pTv(i):
                return ra_bf16(4608 + i * 512, 1024).rearrange("p (a t) -> p a t", a=2)
            def rdv(i):
                return ra_f32(5632 + i * 512, 512)
            for (sbi, c0, Wd) in sbs:
                P.defw = Wd
                norm(2, c0, Wd, xk(sbi), hk(sbi))
            for j in range(4):
                S, skey = wslab("wq", j)
                for (sbi, c0, Wd) in sbs:
                    P.defw = Wd
                    for half in range(2):
                        d = 2 * j + half
                        b = bank("mm")
                        mmgroup(psb[b][:, 0:Wd], [(S[:, k, half * 128:(half + 1) * 128], hT[:, k, c0:c0 + Wd]) for k in range(8)],
                                reads=[skey] + hk(sbi), bnk=b)
                        P.add("act", lambda e, b=b, d=d, c0=c0, Wd=Wd: e.copy(q[:, d, c0:c0 + Wd], psb[b][:, 0:Wd]),
                              reads=[PS(b), "RA"], writes=[("q", d, sbi)])
            okeys = hk
            pcount = 0
            for (sbi, c0, Wd) in sbs:
                P.defw = Wd
                if Wd != 512:
                    continue
                for h in range(4):
                    pi = pcount % 2
                    pcount += 1
                    pT = pTv(pi)
                    for mc in range(2):
                        b = bank("mm")
                        mmgroup(psb[b][:, 0:Wd], [(kT[:, 2 * h + dc, mc * 128:(mc + 1) * 128], q[:, 2 * h + dc, c0:c0 + Wd]) for dc in range(2)],
                                reads=["RA", ("q", 2 * h, sbi), ("q", 2 * h + 1, sbi)] + kTkeys, bnk=b)
                        P.add("act", lambda e, b=b, pT=pT, mc=mc: e.activation(pT[:, mc, :], psb[b][:, 0:512], AF.Exp, scale=0.0625),
                              reads=[PS(b), "RA"], writes=[("pT", pi, mc)])
                    bd = 6
                    mmgroup(psb[bd][:, 0:Wd], [(ones_b[:], pT[:, mc, :]) for mc in range(2)],
                            reads=["RA", ("pT", pi, 0), ("pT", pi, 1), "ones_b"], bnk=bd)
                    rd = rdv(pi)
                    P.add("dve", lambda e, rd=rd, bd=bd: e.reciprocal(rd, psb[bd][:, 0:512]), reads=[PS(bd), "RA"], writes=[("rd", pi)], cost=3.4)
                    for dc in range(2):
                        bo = bank("out")
                        mmgroup(psb[bo][:, 0:Wd], [(Vb[:, mc, h * 256 + dc * 128:h * 256 + (dc + 1) * 128], pT[:, mc, :]) for mc in range(2)],
                                reads=["RA", ("pT", pi, 0), ("pT", pi, 1)] + Vbkeys, bnk=bo)
                        P.add("dve", lambda e, bo=bo, rd=rd, h=h, dc=dc, c0=c0, Wd=Wd: e.tensor_tensor(
                            hT[:, 2 * h + dc, c0:c0 + Wd], psb[bo][:, 0:Wd], rd, ALU.mult),
                            reads=[PS(bo), ("rd", pi), "RA"], writes=[("h", 2 * h + dc, sbi)])
            if with_sample:
                sbi, c0, Wd = sbs[-1]
                for s in range(16):
                    sl = s % 4
                    sp_ = s % 2
                    P.add("pool", lambda e, s=s, sl=sl: e.dma_start(out=Kst[sl], in_=ck[s].rearrange("(a p) c -> p a c", p=128)),
                          reads=["RA"], writes=[("Kst", sl)], dma=f"Kst{sl}", nbytes=1048576)
                    P.add("pool", lambda e, s=s, sl=sl: e.dma_start(out=Vst[sl], in_=cvv[s].rearrange("(a p) c -> p a c", p=128)),
                          reads=["RA"], writes=[("Vst", sl)], dma=f"Vst{sl}", nbytes=1048576)
                    for half in range(2):
                        b = bank("mm")
                        pb = psb[b][:, :].bitcast(BF16)

                        def emit(e, half=half, pb=pb, sl=sl):
                            last = None
                            for hh in range(2):
                                for dc in range(2):
                                    for mc in range(2):
                                        h = 2 * half + hh
                                        col = ((hh * 2 + dc) * 256 + mc * 128)
                                        last = e.transpose(pb[:, col:col + 128],
                                                           Kst[sl][:, mc, h * 256 + dc * 128:h * 256 + (dc + 1) * 128], ident_b[:])
                            return last
                        P.add("pe", emit, reads=[("Kst", sl), "ident_b", "RA"], writes=[PS(b)], cost=1.0)
                        P.add("act", lambda e, half=half, pb=pb, sl=sl: e.copy(
                            kTs[sl][:, 4 * half:4 * half + 4, :], pb.rearrange("p (a m) -> p a m", a=4)),
                            reads=[PS(b), "RA"], writes=[("kTs", sl, half)])
                    pTs = ra_bf16(6656 + sp_ * 32, 64).rearrange("p (a t) -> p a t", a=8)
                    rds = ra_f32(6720, 32).rearrange("p (h o t) -> p h o t", h=4, o=1)
                    bs = bank("out")

                    def emit_sc(e, s=s, sl=sl, bs=bs):
                        last = None
                        for h in range(4):
                            for mc in range(2):
                                for dc in range(2):
                                    last = e.matmul(psb[bs][:, (h * 2 + mc) * 8:(h * 2 + mc) * 8 + 8],
                                                    kTs[sl][:, 2 * h + dc, mc * 128:(mc + 1) * 128],
                                                    q[:, 2 * h + dc, c0 + 8 * s:c0 + 8 * s + 8], start=(dc == 0), stop=(dc == 1))
                        return last
                    P.add("pe", emit_sc, cost=1.2, reads=["RA", ("kTs", sl, 0), ("kTs", sl, 1)] + [("q", d, sbi) for d in range(8)], writes=[PS(bs)])
                    P.add("act", lambda e, bs=bs, pTs=pTs: e.activation(pTs, psb[bs][:, 0:64].rearrange("p (a t) -> p a t", a=8), AF.Exp, scale=0.0625),
                          reads=[PS(bs), "RA"], writes=[("pTs", sp_)])
                    bd = 6

                    def emit_den(e, pTs=pTs, bd=bd):
                        last = None
                        for h in range(4):
                            for mc in range(2):
                                last = e.matmul(psb[bd][:, h * 8:h * 8 + 8], ones_b[:], pTs[:, h * 2 + mc, :], start=(mc == 0), stop=(mc == 1))
                        return last
                    P.add("pe", emit_den, cost=0.6, reads=["RA", ("pTs", sp_), "ones_b"], writes=[PS(bd)])
                    P.add("dve", lambda e, bd=bd, rds=rds: e.reciprocal(rds, psb[bd][:, 0:32].rearrange("p (h o t) -> p h o t", h=4, o=1)),
                          reads=[PS(bd), "RA"], writes=["rds"])
                    bp = 7

                    def emit_pv(e, pTs=pTs, bp=bp, sl=sl):
                        last = None
                        for h in range(4):
                            for dc in range(2):
                                for mc in range(2):
                                    last = e.matmul(psb[bp][:, (h * 2 + dc) * 8:(h * 2 + dc) * 8 + 8],
                                                    Vst[sl][:, mc, h * 256 + dc * 128:h * 256 + (dc + 1) * 128],
                                                    pTs[:, h * 2 + mc, :], start=(mc == 0), stop=(mc == 1))
                        return last
                    P.add("pe", emit_pv, cost=1.2, reads=["RA", ("pTs", sp_), ("Vst", sl)], writes=[PS(bp)])
                    P.add("dve", lambda e, bp=bp, rds=rds, s=s: e.tensor_tensor(
                        hT[:, :, c0 + 8 * s:c0 + 8 * s + 8].rearrange("p (h d) t -> p h d t", h=4),
                        psb[bp][:, 0:64].rearrange("p (h d t) -> p h d t", h=4, d=2),
                        rds.to_broadcast([128, 4, 2, 8]), ALU.mult),
                        reads=[PS(bp), "rds", "RA"], writes=hk(sbi))
            for j in range(4):
                S, skey = wslab("wo", j)
                for (sbi, c0, Wd) in sbs:
                    P.defw = Wd
                    for half in range(2):
                        d = 2 * j + half
                        bo = bank("out")
                        mmgroup(psb[bo][:, 0:Wd], [(S[:, k, half * 128:(half + 1) * 128], hT[:, k, c0:c0 + Wd]) for k in range(8)],
                                reads=[skey] + okeys(sbi), bnk=bo)
                        P.add("dve", lambda e, bo=bo, d=d, c0=c0, Wd=Wd: e.tensor_tensor(xT[:, d, c0:c0 + Wd], xT[:, d, c0:c0 + Wd], psb[bo][:, 0:Wd], ALU.add),
                              reads=[PS(bo), ("x", d, sbi)], writes=[("x", d, sbi)])

        def load_dma(src):
            k = cnt["in"] % 4
            cnt["in"] += 1
            P.add("sp", lambda e, k=k, src=src: e.dma_start(out=rb_slot(k), in_=src), reads=["RB"], writes=[("rbs", k)], dma=f"rbs{k}", nbytes=524288)
            return k

        def consume_tile(k, col, sbi):
            for half in range(2):
                b = bank("mm")

                def emit(e, k=k, half=half, b=b):
                    last = None
                    for cc in range(4):
                        c = half * 4 + cc
                        last = e.transpose(psb[b][:, cc * 128:(cc + 1) * 128], rb_slot(k)[:, c * 128:(c + 1) * 128], ident_f[:])
                    return last
                P.add("pe", emit, reads=[("rbs", k), "ident_f", "RB"], writes=[PS(b)], cost=0.9)
                eng = "act" if half == 0 else "dve"
                fn = (lambda e, half=half, b=b, col=col: e.copy(
                    xT[:, half * 4:half * 4 + 4, col:col + 128], psb[b][:, :].rearrange("p (c m) -> p c m", c=4))) if half == 0 else \
                     (lambda e, half=half, b=b, col=col: e.tensor_copy(
                    xT[:, half * 4:half * 4 + 4, col:col + 128], psb[b][:, :].rearrange("p (c m) -> p c m", c=4)))
                P.add(eng, fn, reads=[PS(b)], writes=[("x", half * 4 + cc, sbi) for cc in range(4)])

        def out_tile(dst, col, sbi):
            k = 4 + cnt["outs"] % 2
            cnt["outs"] += 1
            for half in range(2):
                b = bank("mm")

                def emit(e, half=half, b=b, col=col):
                    last = None
                    for cc in range(4):
                        c = half * 4 + cc
                        last = e.transpose(psb[b][:, cc * 128:(cc + 1) * 128], xT[:, c, col:col + 128], ident_f[:])
                    return last
                P.add("pe", emit, reads=[("x", half * 4 + cc, sbi) for cc in range(4)] + ["ident_f"], writes=[PS(b)], cost=0.9)
                if half == 0:
                    P.add("act", lambda e, b=b, k=k: e.copy(rb_slot(k)[:, 0:512], psb[b][:, :]),
                          reads=[PS(b), "RB"], writes=[("rbs", k)])
                else:
                    P.add("dve", lambda e, b=b, k=k: e.tensor_copy(rb_slot(k)[:, 512:1024], psb[b][:, :]),
                          reads=[PS(b), "RB"], writes=[("rbs", k)])
            ok = ("o", cnt["odma"]); cnt["odma"] += 1; out_keys.append(ok)
            P.add("sp", lambda e, k=k, dst=dst: e.dma_start(out=dst, in_=rb_slot(k)),
                  reads=[("rbs", k), "RB"], writes=[ok], dma=f"rbs{k}", nbytes=524288)

        cnt["in"] = 0
        cnt["outs"] = 0

        def pass_tiles(pi):
            t = [(xp[(pi * 8 + i) * 128:(pi * 8 + i + 1) * 128, :], i * 128, i // 4) for i in range(8)]
            o = [(yp[(pi * 8 + i) * 128:(pi * 8 + i + 1) * 128, :], i * 128, i // 4) for i in range(8)]
            if pi == 1:
                t.append((xsm, 1024, 2))
                o.append((ys, 1024, 2))
            return t, o

        SBS = {0: [(0, 0, 512), (1, 512, 512)], 1: [(0, 0, 512), (1, 512, 512), (2, 1024, 128)]}

        P.stage(3)
        tiles, _ = pass_tiles(0)
        pend = [load_dma(t[0]) for t in tiles[:4]]
        nxt = 4
        for i, (src, col, sbi) in enumerate(tiles):
            consume_tile(pend.pop(0), col, sbi)
            if nxt < len(tiles):
                pend.append(load_dma(tiles[nxt][0]))
                nxt += 1
            if i % 4 == 3:
                P.stage(4)
                sb_ = SBS[0][sbi]
                norm(0, sb_[1], sb_[2], xk(sbi), hk(sbi))
                P.stage(3)

        P.stage(2)
        P.rb_with_ra = False
        mem_phase()
        P.rb_with_ra = True

        for pass_idx in range(2):
            sbs = SBS[pass_idx]
            tiles, otiles = pass_tiles(pass_idx)
            P.stage(4 + 6 * pass_idx)
            ffn("f1", sbs, sb_outer=True)
            P.stage(5 + 6 * pass_idx)
            mixer_sb(0, 0, 512, 1, 512, pass_idx == 0, pass_idx)
            mixer_sb(1, 512, 512, 1, 512, False, pass_idx)
            if pass_idx == 1:
                mixer_sb(2, 1024, 128, 16, 8, False, pass_idx)
            P.stage(6 + 6 * pass_idx)
            xattn(sbs, pass_idx == 1, with_mem=False)
            ntiles = []
            pend = []
            nxt = 0
            if pass_idx == 0:
                P.stage(9)
                rb_barrier()
                ntiles, _ = pass_tiles(1)
                pend = [load_dma(t[0]) for t in ntiles[:4]]
                nxt = 4
            else:
                rb_barrier()
            P.stage(7 + 6 * pass_idx)
            for (sbi, c0, Wd) in sbs:
                P.defw = Wd
                norm(4, c0, Wd, xk(sbi), hk(sbi))
            ffn("f2", sbs, sb_outer=True)
            for (sbi, c0, Wd) in sbs:
                P.defw = Wd
                P.stage(8 + 6 * pass_idx)
                norm(5, c0, Wd, xk(sbi), hk(sbi), inplace=True)
                for (dst, col, s2) in otiles:
                    if s2 == sbi:
                        out_tile(dst, col, sbi)
                if pass_idx == 0:
                    P.stage(9)
                    for (src, col, s2) in ntiles:
                        if s2 == sbi:
                            consume_tile(pend.pop(0), col, sbi)
                            if nxt < len(ntiles):
                                pend.append(load_dma(ntiles[nxt][0]))
                                nxt += 1
                    P.stage(10)
                    norm(0, c0, Wd, xk(sbi), hk(sbi))
            if pass_idx == 0:
                P.stage(9)
                for (src, col, s2) in ntiles:
                    if s2 == 2:
                        consume_tile(pend.pop(0), col, 2)
                P.stage(10)
                norm(0, 1024, 128, xk(2), hk(2))
        P.stage(0)
        P.add("sp", lambda e: e.nop(), reads=out_keys)
        P.build()
    return nc


_NC_CACHE = {}


def kernel(**inp):
    f = lambda a: np.ascontiguousarray(np.asarray(a, dtype=np.float32))
    if "nc" not in _NC_CACHE:
        _NC_CACHE["nc"] = build_nc()
    nc = _NC_CACHE["nc"]
    gains = np.stack([f(inp["ffn1_norm"])[0], f(inp["mix_norm"])[0], f(inp["xattn_norm"])[0],
                      f(inp["mem_norm"])[0], f(inp["ffn2_norm"])[0], f(inp["final_norm"])], axis=0)
    gains = np.ascontiguousarray(gains.reshape(6, 8, 128).transpose(2, 0, 1))
    vec = np.stack([f(inp["conv_b"])[0], f(inp["lru_ba"])[0], f(inp["lru_bx"])[0],
                    f(inp["lru_lambda"])[0], f(inp["pool_scale"])[0]], axis=0)
    vec = np.ascontiguousarray(vec.reshape(5, 4, 128).transpose(2, 0, 1))
    convw = np.ascontiguousarray(f(inp["conv_w"])[0].reshape(4, 4, 128).transpose(2, 1, 0))
    shared = {
        "f1g": f(inp["ffn1_w_gate"])[0], "f1u": f(inp["ffn1_w_up"])[0], "f1d": f(inp["ffn1_w_down"])[0],
        "win": f(inp["w_in"])[0], "wout": f(inp["w_out"])[0], "wq": f(inp["xattn_wq"])[0],
        "wk": f(inp["xattn_wk"])[0], "wv": f(inp["xattn_wv"])[0], "wo": f(inp["xattn_wo"])[0],
        "f2g": f(inp["ffn2_w_gate"])[0], "f2u": f(inp["ffn2_w_up"])[0], "f2d": f(inp["ffn2_w_down"])[0],
        "gains": gains, "vec512": vec, "convw": convw,
        "lwa": f(inp["lru_wa"])[0], "lwx": f(inp["lru_wx"])[0], "poolw": f(inp["pool_w"])[0],
    }
    xpr = f(inp["x_prompt"]); xsa = f(inp["x_sample"]); memp = f(inp["mem_prompt"])
    sc = f(inp["state_conv"])[0]; slr = f(inp["state_lru"])[0]; spl = f(inp["state_pool"])[0]
    ckk = f(inp["cache_mem_k"])[0]; cvv = f(inp["cache_mem_v"])[0]
    in_maps = []
    for c in range(NCORES):
        s0, s1 = 16 * c, 16 * c + 16
        m = dict(shared)
        m["xp"] = xpr[c]
        m["xsm"] = xsa[s0:s1].reshape(128, 1024)
        m["mem"] = memp[c]
        m["sconv"] = sc[s0:s1].reshape(48, 512)
        m["slru"] = slr[s0:s1]
        m["spool"] = spl[s0:s1].reshape(240, 512)
        m["ck"] = ckk[s0:s1].reshape(16, 256, 1024)
        m["cv"] = cvv[s0:s1].reshape(16, 256, 1024)
        in_maps.append(m)
    res = run_bass_kernel_spmd(nc, in_maps, core_ids=list(range(NCORES)))
    R = res.results
    y_prompt = np.stack([R[c]["yp"] for c in range(NCORES)], 0)
    y_sample = np.concatenate([R[c]["ys"].reshape(16, 8, 1024) for c in range(NCORES)], 0)
    p_conv = np.stack([R[c]["pconv"] for c in range(NCORES)], 0)[None]
    p_lru = np.stack([R[c]["plru"].reshape(512) for c in range(NCORES)], 0)[None]
    p_pool = np.stack([R[c]["ppool"] for c in range(NCORES)], 0)[None]
    p_mk = np.stack([R[c]["pmk"].reshape(256, 4, 256) for c in range(NCORES)], 0)[None]
    p_mv = np.stack([R[c]["pmv"].reshape(256, 4, 256) for c in range(NCORES)], 0)[None]
    s_conv = np.concatenate([R[c]["sconv_o"].reshape(16, 3, 512) for c in range(NCORES)], 0)[None]
    s_lru = np.concatenate([R[c]["slru_o"] for c in range(NCORES)], 0)[None]
    s_pool = np.concatenate([R[c]["spool_o"].reshape(16, 15, 512) for c in range(NCORES)], 0)[None]
    outs = (y_prompt, y_sample, p_conv, p_lru, p_pool, p_mk, p_mv, s_conv, s_lru, s_pool)
    return tuple(np.ascontiguousarray(o, dtype=np.float32) for o in outs)
```
